# Optimizing a Trainium2 kernel written in Bass

```python
import math
import jax, jax.numpy as jnp
from jax import lax
import numpy as np

D_MODEL = 1024
BATCH = 1
SEQ = 16384
DEPTH = 2

A_PATTERNS = ((128, 1), (512, 4), (2048, 16))
A_GROUPS = 3
A_HEADS = 8
A_HEAD_DIM = 64
A_WIDTH = A_HEADS * A_HEAD_DIM
A_BLOCK = 128
M_HEADS = 16
M_Q_LORA = 256
M_KV_LORA = 128
M_NOPE = 64
M_ROPE = 32
M_V = 64
M_WIDTH = M_HEADS * M_V
ROPE_THETA = 10000.0
Q_BLOCK = 128
REL_BUCKETS = 32
REL_MAX_DIST = 2048
N_BIAS_HEADS = A_GROUPS * A_HEADS
EPS = 1e-6
NEG_INF = -1e30

SPLIT_SIZES = (3 * A_GROUPS * A_WIDTH, A_WIDTH, M_Q_LORA, M_KV_LORA, M_ROPE, M_WIDTH, 2 * D_MODEL)
D_IN = sum(SPLIT_SIZES)
SPLIT_IDX = tuple(sum(SPLIT_SIZES[:i + 1]) for i in range(len(SPLIT_SIZES) - 1))

kernel_name = 'hybrid_dilated_mla_gated_block'


def rms_norm(x, g):
    xf = x.astype(jnp.float32)
    y = xf * lax.rsqrt(jnp.mean(xf * xf, axis=-1, keepdims=True) + EPS)
    return (y * g.astype(jnp.float32)).astype(x.dtype)


def t5_bucket(dist):
    exact = REL_BUCKETS // 2
    d = jnp.maximum(dist, 1).astype(jnp.float32)
    large = exact + (jnp.log(d / exact) / math.log(REL_MAX_DIST / exact) * (REL_BUCKETS - exact)).astype(jnp.int32)
    large = jnp.minimum(large, REL_BUCKETS - 1)
    return jnp.where(dist < exact, dist, large)


def apply_rope(t, cos, sin):
    half = t.shape[-1] // 2
    t1 = t[..., :half].astype(jnp.float32)
    t2 = t[..., half:].astype(jnp.float32)
    return jnp.concatenate([t1 * cos - t2 * sin, t1 * sin + t2 * cos], axis=-1).astype(t.dtype)


def dilated_attention_group(q, k, v, bias_tab, window, dilation):
    B, S, H, Dh = q.shape
    r = dilation
    n_rel = window // r
    L = -(-S // r)
    nb = -(-L // A_BLOCK)
    Lp = nb * A_BLOCK
    pad = Lp * r - S

    def to_blocks(t):
        t = jnp.pad(t, ((0, 0), (0, pad), (0, 0), (0, 0)))
        t = t.reshape(B, Lp, r, H, Dh).transpose(0, 2, 1, 3, 4)
        return t.reshape(B, r, nb, A_BLOCK, H, Dh)

    def with_prev(t):
        prev = jnp.concatenate([jnp.zeros_like(t[:, :, :1]), t[:, :, :-1]], axis=2)
        return jnp.concatenate([prev, t], axis=3)

    qb = to_blocks(q)
    kw = with_prev(to_blocks(k))
    vw = with_prev(to_blocks(v))
    s = jnp.einsum('bpnqhd,bpnkhd->bpnhqk', qb, kw).astype(jnp.float32) * (Dh ** -0.5)
    qi = jnp.arange(A_BLOCK)[:, None]
    ki = jnp.arange(2 * A_BLOCK)[None, :]
    j = qi + A_BLOCK - ki
    band = (j >= 0) & (j <= n_rel)
    bias = bias_tab[t5_bucket(jnp.maximum(j, 0) * r)].astype(jnp.float32).transpose(2, 0, 1)
    key_sub = jnp.arange(nb)[:, None, None] * A_BLOCK + ki[None] - A_BLOCK
    valid = band[None] & (key_sub >= 0)
    s = jnp.where(valid[None, None, :, None], s + bias, NEG_INF)
    m = jnp.max(s, axis=-1, keepdims=True)
    p = jnp.exp(s - m)
    l = jnp.sum(p, axis=-1, keepdims=True)
    o = jnp.einsum('bpnhqk,bpnkhd->bpnqhd', (p / l).astype(v.dtype), vw)
    lse = (m + jnp.log(l))[..., 0]
    o = o.reshape(B, r, Lp, H, Dh).transpose(0, 2, 1, 3, 4).reshape(B, Lp * r, H, Dh)[:, :S]
    lse = lse.transpose(0, 1, 2, 4, 3).reshape(B, r, Lp, H).transpose(0, 2, 1, 3).reshape(B, Lp * r, H)[:, :S]
    return o, lse


def dilated_mixer(a_qkv, rel_bias):
    B, S, _ = a_qkv.shape
    qkv = a_qkv.reshape(B, S, 3, A_GROUPS, A_HEADS, A_HEAD_DIM)
    outs, lses = [], []
    for g, (w, r) in enumerate(A_PATTERNS):
        o, lse = dilated_attention_group(qkv[:, :, 0, g], qkv[:, :, 1, g], qkv[:, :, 2, g],
                                         rel_bias[:, g * A_HEADS:(g + 1) * A_HEADS], w, r)
        outs.append(o)
        lses.append(lse)
    wts = jax.nn.softmax(jnp.stack(lses), axis=0)
    o = jnp.sum(jnp.stack(outs).astype(jnp.float32) * wts[..., None], axis=0)
    return o.reshape(B, S, A_WIDTH).astype(a_qkv.dtype)


def mla_mixer(c_q, c_kv, k_rope_raw, q_norm_g, w_uq, kv_norm_g, w_ukv, cos, sin):
    B, S, _ = c_q.shape
    q = (rms_norm(c_q, q_norm_g) @ w_uq).reshape(B, S, M_HEADS, M_NOPE + M_ROPE)
    q_nope = q[..., :M_NOPE]
    q_rope = apply_rope(q[..., M_NOPE:], cos[:, :, None], sin[:, :, None])
    kv = (rms_norm(c_kv, kv_norm_g) @ w_ukv).reshape(B, S, M_HEADS, M_NOPE + M_V)
    k_nope = kv[..., :M_NOPE]
    v = kv[..., M_NOPE:]
    k_rope = apply_rope(k_rope_raw, cos, sin)
    nb = S // Q_BLOCK
    scale = (M_NOPE + M_ROPE) ** -0.5
    k_pos = jnp.arange(S)

    def blocks(t):
        return t.reshape((B, nb, Q_BLOCK) + t.shape[2:]).swapaxes(0, 1)

    def attend(args):
        qn, qr, start = args
        s = (jnp.einsum('bqhd,bkhd->bhqk', qn, k_nope)
             + jnp.einsum('bqhd,bkd->bhqk', qr, k_rope)).astype(jnp.float32) * scale
        q_pos = start + jnp.arange(Q_BLOCK)
        s = jnp.where(k_pos[None, :] <= q_pos[:, None], s, NEG_INF)
        p = jax.nn.softmax(s, axis=-1)
        return jnp.einsum('bhqk,bkhd->bqhd', p.astype(v.dtype), v)

    o = lax.map(attend, (blocks(q_nope), blocks(q_rope), jnp.arange(nb) * Q_BLOCK))
    return o.swapaxes(0, 1).reshape(B, S, M_WIDTH)


def hybrid_layer(x, mod, norm_g, w_in, q_norm_g, w_uq, kv_norm_g, w_ukv, w_out_a, w_out_b, w_o, rel_bias, cos, sin):
    shift, scale, gate = jnp.split(mod, 3, axis=-1)
    h = rms_norm(x, norm_g) * (1 + scale[:, None]) + shift[:, None]
    a_qkv, a_z, m_cq, m_ckv, m_kr, m_z, merge = jnp.split(h @ w_in, SPLIT_IDX, axis=-1)
    y_a = dilated_mixer(a_qkv, rel_bias) * jax.nn.silu(a_z)
    y_m = mla_mixer(m_cq, m_ckv, m_kr, q_norm_g, w_uq, kv_norm_g, w_ukv, cos, sin) * jax.nn.silu(m_z)
    g_a, g_m = jnp.split(jax.nn.sigmoid(merge), 2, axis=-1)
    merged = g_a * (y_a @ w_out_a) + g_m * (y_m @ w_out_b)
    return x + gate[:, None] * (merged @ w_o)


def setup_inputs(seed: int = 0) -> dict:
    key = jax.random.key(seed)
    ks = jax.random.split(key, 16)
    n = jax.random.normal
    f32 = jnp.float32
    x = n(ks[0], (BATCH, SEQ, D_MODEL), f32)
    c = n(ks[1], (BATCH, D_MODEL), f32)
    positions = jnp.broadcast_to(jnp.arange(SEQ, dtype=jnp.int32)[None], (BATCH, SEQ))
    w_ada = n(ks[2], (DEPTH, D_MODEL, 3 * D_MODEL), f32) * (0.5 * D_MODEL ** -0.5)
    b_ada = n(ks[3], (DEPTH, 3 * D_MODEL), f32) * 0.01
    norm_g = 1.0 + 0.01 * n(ks[4], (DEPTH, D_MODEL), f32)
    w_in = n(ks[5], (DEPTH, D_MODEL, D_IN), f32) * D_MODEL ** -0.5
    q_norm_g = 1.0 + 0.01 * n(ks[6], (DEPTH, M_Q_LORA), f32)
    w_uq = n(ks[7], (DEPTH, M_Q_LORA, M_HEADS * (M_NOPE + M_ROPE)), f32) * M_Q_LORA ** -0.5
    kv_norm_g = 1.0 + 0.01 * n(ks[8], (DEPTH, M_KV_LORA), f32)
    w_ukv = n(ks[9], (DEPTH, M_KV_LORA, M_HEADS * (M_NOPE + M_V)), f32) * M_KV_LORA ** -0.5
    w_out_a = n(ks[10], (DEPTH, A_WIDTH, D_MODEL), f32) * A_WIDTH ** -0.5
    w_out_b = n(ks[11], (DEPTH, M_WIDTH, D_MODEL), f32) * M_WIDTH ** -0.5
    w_o = n(ks[12], (DEPTH, D_MODEL, D_MODEL), f32) * D_MODEL ** -0.5
    rel_bias = n(ks[13], (REL_BUCKETS, N_BIAS_HEADS), f32) * 0.5
    final_norm_g = 1.0 + 0.01 * n(ks[14], (D_MODEL,), f32)
    return {'x': x, 'c': c, 'positions': positions, 'w_ada': w_ada, 'b_ada': b_ada, 'norm_g': norm_g,
            'w_in': w_in, 'q_norm_g': q_norm_g, 'w_uq': w_uq, 'kv_norm_g': kv_norm_g, 'w_ukv': w_ukv,
            'w_out_a': w_out_a, 'w_out_b': w_out_b, 'w_o': w_o, 'rel_bias': rel_bias, 'final_norm_g': final_norm_g}


def reference(x, c, positions, w_ada, b_ada, norm_g, w_in, q_norm_g, w_uq, kv_norm_g, w_ukv,
              w_out_a, w_out_b, w_o, rel_bias, final_norm_g):
    inv_freq = 1.0 / (ROPE_THETA ** (jnp.arange(0, M_ROPE, 2, dtype=jnp.float32) / M_ROPE))
    ang = positions.astype(jnp.float32)[..., None] * inv_freq
    cos, sin = jnp.cos(ang), jnp.sin(ang)
    c_act = jax.nn.silu(c)
    for l in range(DEPTH):
        mod = c_act @ w_ada[l] + b_ada[l]
        x = hybrid_layer(x, mod, norm_g[l], w_in[l], q_norm_g[l], w_uq[l], kv_norm_g[l], w_ukv[l],
                         w_out_a[l], w_out_b[l], w_o[l], rel_bias, cos, sin)
    return rms_norm(x, final_norm_g)
```

```python
import math
import numpy as np
import ml_dtypes
import concourse.bass as bass
import concourse.mybir as mybir
from concourse.bass_utils import run_bass_kernel_spmd

F32 = mybir.dt.float32
BF16 = mybir.dt.bfloat16
I32 = mybir.dt.int32
AF = mybir.ActivationFunctionType
ALU = mybir.AluOpType

D = 1024
NCORE = 8
NSLOT = 8
EPS = 1e-6
NW = 1440
TWO_PI = 2.0 * math.pi
import os as _os
PARTS = set(_os.environ.get('PARTS', 'dil,gates,lat,krope,q,k,vm,vd').split(','))
PSTOP = _os.environ.get('PSTOP')
DSTOP = _os.environ.get('DSTOP')


def I(method, *args, **kw):
    return lambda e: getattr(e, method)(*args, **kw)


class Prog:
    STREAMS = ["pe", "act", "dve", "pool", "sp"]

    def __init__(self):
        self.ops = []
        self.lastw = {}
        self.readers = {}
        self.dma_n = {}
        self.slot_last = {}
        self.bar = {}
        self.last_in_class = {}

    def _cls(self, o):
        if o["dma"]:
            return (o["stream"], o["slot"])
        if o["cc"]:
            return "cc"
        return o["stream"]

    def op(self, stream, fn, r=(), w=(), dma=False, cc=False):
        idx = len(self.ops)
        deps = set(self.bar.values())
        for k in r:
            if k in self.lastw:
                deps.add(self.lastw[k])
        for k in w:
            if k in self.lastw:
                deps.add(self.lastw[k])
            deps.update(self.readers.get(k, {}).values())
        o = dict(stream=stream, fn=fn, deps=deps, dma=dma, cc=cc, needed=False, idx=idx)
        if dma:
            n = self.dma_n.get(stream, 0)
            self.dma_n[stream] = n + 1
            o["slot"] = n % NSLOT
            o["slotn"] = n // NSLOT + 1
            prev = self.slot_last.get((stream, o["slot"]))
            if prev is not None:
                deps.add(prev)
            self.slot_last[(stream, o["slot"])] = idx
        c = self._cls(o)
        o["cls"] = c
        for k in w:
            self.lastw[k] = idx
            self.readers[k] = {}
        for k in r:
            self.readers.setdefault(k, {})[c] = idx
        self.last_in_class[c] = idx
        self.ops.append(o)
        return idx

    def barrier(self):
        self.bar = dict(self.last_in_class)
        self.lastw = {}
        self.readers = {}

    def emit(self, nc, block, sems):
        ops = self.ops
        for o in ops:
            for d in o["deps"]:
                ops[d]["needed"] = True
        cnt = {}
        for o in ops:
            if o["dma"]:
                o["val"] = 16 * o["slotn"]
            elif o["needed"]:
                cnt[o["cls"]] = cnt.get(o["cls"], 0) + 1
                o["val"] = cnt[o["cls"]]

        def run(stream, eng):
            waited = {}
            for o in ops:
                if o["stream"] != stream:
                    continue
                need = {}
                for d in o["deps"]:
                    od = ops[d]
                    if od["cls"] == "pe" and stream == "pe" and not o["dma"]:
                        continue
                    c = od["cls"]
                    need[c] = max(need.get(c, 0), od["val"])
                for c, v in need.items():
                    if waited.get(c, 0) < v:
                        eng.wait_ge(sems[c], v)
                        waited[c] = v
                ins = o["fn"](eng)
                if ins is not None and (o["dma"] or o["needed"]):
                    ins.then_inc(sems[o["cls"]], 16 if o["dma"] else 1)

        @block.tensor
        def _(e):
            run("pe", e)

        @block.scalar
        def _(e):
            run("act", e)

        @block.vector
        def _(e):
            run("dve", e)

        @block.gpsimd
        def _(e):
            run("pool", e)

        @block.sync
        def _(e):
            run("sp", e)


class Arena:
    def __init__(self, nc, limit=229344):
        self.nc = nc
        self.off = 16512
        self.limit = limit
        self.n = 0

    def alloc(self, shape, dt):
        esz = 4 if dt in (F32, I32) else 2
        per = esz
        for s in shape[1:]:
            per *= s
        per = (per + 31) // 32 * 32
        assert self.off + per <= self.limit, ("SBUF overflow", self.off, per)
        self.n += 1
        t = self.nc.alloc_sbuf_tensor_at("sb%d" % self.n, list(shape), dt, offset=self.off)
        self.off += per
        return t


def build(S, L=2, debug=False, stop=None):
    NT = S // 512
    NB = S // 128
    NCH = S // 2048
    SQ = S // 4
    nc = bass.Bass("TRN2", target_bir_lowering=False)
    P = Prog()

    def din(name, shape, dt):
        return nc.dram_tensor(name, shape, dt, kind="ExternalInput")

    xT = din("xT", [128, S], F32)
    cT = din("cT", [128, 8], F32)
    pos = din("pos", [1, S], I32)
    invf = din("invf", [128, 1], F32)
    w_ada = din("w_ada", [L, 128, 8, 384], F32)
    b_ada = din("b_ada", [L, 128, 3], F32)
    norm_g = din("norm_g", [L, 128, 1], F32)
    w_in = din("w_in", [L, 128, 8, NW], F32)
    qng = din("qng", [L, 128, 2], F32)
    kvng = din("kvng", [L, 128, 1], F32)
    w_uq = din("w_uq", [L, 128, 2, 192], F32)
    w_ukv = din("w_ukv", [L, 128, 256], F32)
    w_out = din("w_out", [L, 128, 12, 128], F32)
    w_o = din("w_o", [L, 128, 8, 128], F32)
    fin_g = din("fin_g", [128, 1], F32)
    rel = din("rel", [32, 3], F32)
    oh5 = din("oh5", [32, 3, 2, 256], F32)
    m5 = din("m5", [128, 3, 2, 256], F32)
    tri_in = din("tri", [128, 128], BF16)
    outT = nc.dram_tensor("outT", [128, S], F32, kind="ExternalOutput")

    def dscr(name, shape, dt):
        return nc.dram_tensor(name, shape, dt)

    ss_send = dscr("ss_send", [1, S], F32)
    ss_all = dscr("ss_all", [8, S], F32)
    hT_send = dscr("hT_send", [128, S], BF16)
    hT_all = dscr("hT_all", [1024, S], BF16)
    y_send = dscr("y_send", [192, S], BF16)
    y_all = dscr("y_all", [1536, S], BF16)
    mg_send = dscr("mg_send", [128, S], BF16)
    mg_all = dscr("mg_all", [1024, S], BF16)
    xcur = dscr("xcur", [128, S], F32)
    qT = dscr("qT", [2, 96, S], BF16)
    kT = dscr("kT", [2, 96, S], BF16)
    vm = dscr("vm", [S, 128], BF16)
    dqk = dscr("dqk", [3, 128, S], BF16)
    dv = dscr("dv", [S, 192], BF16)
    gaz = dscr("gaz", [64, S], BF16)
    gmz = dscr("gmz", [128, S], BF16)
    gmg = dscr("gmg", [2, 128, S], BF16)
    cs = dscr("cs", [2, 32, S], F32)
    t5w = dscr("t5w", [3, 2, 128, 256], F32)
    dbg = {}
    if debug:
        for nm, shp, dt in (("dbg_hT", [128, S], BF16), ("dbg_y", [192, S], BF16), ("dbg_x", [128, S], F32),
                            ("dbg_q", [2, 96, S], BF16), ("dbg_k", [2, 96, S], BF16), ("dbg_dqk", [3, 128, S], BF16),
                            ("dbg_dv", [S, 192], BF16), ("dbg_T5", [128, 768], F32), ("dbg_gaz", [64, S], BF16),
                            ("dbg_hT2", [128, S], BF16), ("dbg_y2", [192, S], BF16), ("dbg_x2", [128, S], F32), ("dbg_mg2", [128, S], BF16)):
            dbg[nm] = nc.dram_tensor(nm, shp, dt, kind="ExternalOutput")

    def allk(name):
        return [(name, t) for t in range(NT)]

    A = Arena(nc)
    ps = [nc.alloc_psum_tensor("psb%d" % i, [128, 512], F32) for i in range(8)]

    def PSK(i):
        return ("ps", i)

    ones_f = A.alloc([128, 128], F32)
    tri = A.alloc([128, 128], BF16)
    T5 = A.alloc([128, 3, 2, 128], F32)
    modc = A.alloc([128, 8], F32)
    sc = A.alloc([128, 8], F32)
    small = A.alloc([128, 16], F32)
    base_off = A.off

    def dma(stream, out, in_, r, w):
        P.op(stream, I("dma_start", out=out, in_=in_), r=r, w=w, dma=True)

    def ag(send, recv, name_s, name_r):
        P.op("pool", I("collective_compute", "AllGather", ALU.bypass, replica_groups=[list(range(NCORE))],
                                                     ins=[send.ap()], outs=[recv.ap()]),
             r=allk(name_s), w=allk(name_r), cc=True)

    P.op("dve", I("memset", ones_f[:], 1.0), w=["ones"])
    dma("sp", tri[:], tri_in.ap(), [], ["tri"])
    dma("sp", small[:, 4:5], fin_g.ap(), [], ["small4"])
    dma("sp", small[:, 5:6], invf.ap(), [], ["small5"])
    ct = A.alloc([128, 8], F32)
    dma("sp", ct[:], cT.ap(), [], ["ct"])
    sg = A.alloc([128, 8], F32)
    P.op("act", I("activation", out=sg[:], in_=ct[:], func=AF.Sigmoid), r=["ct"], w=["sg"])
    P.op("dve", I("tensor_tensor", out=sc[:], in0=ct[:], in1=sg[:], op=ALU.mult), r=["ct", "sg"], w=["sc"])

    relt = A.alloc([32, 3], F32)
    oh = A.alloc([32, 3, 2, 256], F32)
    m5t = A.alloc([128, 3, 2, 256], F32)
    relbc = A.alloc([32, 3, 128], F32)
    f5 = A.alloc([128, 3, 2, 256], F32)
    dma("sp", relt[:], rel.ap(), [], ["relt"])
    dma("sp", oh[:], oh5.ap(), [], ["oh"])
    dma("sp", m5t[:], m5.ap(), [], ["m5t"])
    for g in range(3):
        P.op("dve", I("tensor_scalar", out=relbc[:, g, :], in0=ones_f[0:32, :], scalar1=relt[:, g:g + 1],
                                                   scalar2=None, op0=ALU.mult), r=["ones", "relt"], w=[("relbc", g)])
        for kd in range(2):
            b = (g * 2 + kd) % 2
            P.op("pe", I("matmul", ps[b][:, 0:256], relbc[:, g, :], oh[:, g, kd, :], start=True, stop=True),
                 r=[("relbc", g), "oh"], w=[PSK(b)])
            P.op("act", I("activation", out=f5[:, g, kd, :], in_=ps[b][:, 0:256], func=AF.Exp),
                 r=[PSK(b)], w=[("f5", g, kd)])
            P.op("dve", I("tensor_tensor", out=f5[:, g, kd, :], in0=f5[:, g, kd, :], in1=m5t[:, g, kd, :], op=ALU.mult),
                 r=[("f5", g, kd), "m5t"], w=[("f5", g, kd)])
            dma("sp", t5w.ap()[g, kd], f5[:, g, kd, :], [("f5", g, kd)], [("t5w", g, kd)])
            src = bass.AP(tensor=t5w, offset=(g * 2 + kd) * 128 * 256, ap=[[255, 128], [1, 128]])
            dma("sp", T5[:, g, kd, :], src, [("t5w", g, kd)], [("T5", g, kd)])

    posi = A.alloc([128, SQ], I32)
    ang = A.alloc([128, SQ], F32)
    kf = A.alloc([128, SQ], F32)
    rr = A.alloc([128, SQ], F32)
    tmp = A.alloc([128, SQ], F32)
    ki = posi
    for q in range(4):
        src = bass.AP(tensor=pos, offset=q * SQ, ap=[[0, 32], [1, SQ]])
        dma("sp", posi[32 * q:32 * q + 32, :], src, [], [("posi", q)])
    pk = [("posi", q) for q in range(4)]
    P.op("dve", I("tensor_copy", out=ang[:], in_=posi[:]), r=pk, w=["ang"])
    P.op("dve", I("tensor_scalar", out=ang[:], in0=ang[:], scalar1=small[:, 5:6], scalar2=None, op0=ALU.mult),
         r=["ang", "small5"], w=["ang"])
    P.op("dve", I("tensor_scalar", out=kf[:], in0=ang[:], scalar1=1.0 / TWO_PI, scalar2=None, op0=ALU.mult), r=["ang"], w=["kf"])
    P.op("dve", I("tensor_copy", out=ki[:], in_=kf[:]), r=["kf"] + pk, w=["ki"])
    P.op("dve", I("tensor_copy", out=kf[:], in_=ki[:]), r=["ki"], w=["kf"])
    C1 = 6.28125
    C2 = TWO_PI - C1
    P.op("dve", I("scalar_tensor_tensor", out=rr[:], in0=kf[:], scalar=-C1, in1=ang[:], op0=ALU.mult, op1=ALU.add),
         r=["kf", "ang"], w=["rr"])
    P.op("dve", I("scalar_tensor_tensor", out=rr[:], in0=kf[:], scalar=-C2, in1=rr[:], op0=ALU.mult, op1=ALU.add),
         r=["kf", "rr"], w=["rr"])

    def fold(buf, key):
        P.op("dve", I("tensor_scalar", out=tmp[:], in0=buf[:], scalar1=math.pi, scalar2=-TWO_PI, op0=ALU.is_gt, op1=ALU.mult),
             r=[key], w=["tmp"])
        P.op("dve", I("tensor_tensor", out=buf[:], in0=buf[:], in1=tmp[:], op=ALU.add), r=[key, "tmp"], w=[key])
        P.op("dve", I("tensor_scalar", out=tmp[:], in0=buf[:], scalar1=-math.pi, scalar2=TWO_PI, op0=ALU.is_lt, op1=ALU.mult),
             r=[key], w=["tmp"])
        P.op("dve", I("tensor_tensor", out=buf[:], in0=buf[:], in1=tmp[:], op=ALU.add), r=[key, "tmp"], w=[key])
        P.op("dve", I("tensor_scalar", out=buf[:], in0=buf[:], scalar1=math.pi, scalar2=-math.pi, op0=ALU.min, op1=ALU.max),
             r=[key], w=[key])

    fold(rr, "rr")
    P.op("act", I("activation", out=kf[:], in_=rr[:], func=AF.Sin), r=["rr"], w=["kf"])
    for q in range(4):
        dma("sp", cs.ap()[1, :, q * SQ:(q + 1) * SQ], kf[32 * q:32 * q + 32, :], ["kf"], [("cs", 1, q)])
    P.op("dve", I("tensor_scalar", out=rr[:], in0=rr[:], scalar1=math.pi / 2, scalar2=None, op0=ALU.add), r=["rr"], w=["rr"])
    fold(rr, "rr")
    P.op("act", I("activation", out=ang[:], in_=rr[:], func=AF.Sin), r=["rr"], w=["ang"])
    for q in range(4):
        dma("sp", cs.ap()[0, :, q * SQ:(q + 1) * SQ], ang[32 * q:32 * q + 32, :], ["ang"], [("cs", 0, q)])
    P.barrier()
    A.off = base_off
    STOPPED = [False]

    def chk(name):
        if stop == name:
            STOPPED[0] = True
        return STOPPED[0]

    def sumsq_seg(src_dram, src_name):
        m0 = A.off
        xt = [A.alloc([128, 512], F32) for _ in range(2)]
        sq = [A.alloc([128, 512], F32) for _ in range(2)]
        row = [A.alloc([1, 512], F32) for _ in range(2)]
        for t in range(NT):
            b = t % 2
            sl = slice(t * 512, (t + 1) * 512)
            dma("sp", xt[b][:], src_dram.ap()[:, sl], [(src_name, t)], [("xt", b)])
            P.op("dve", I("tensor_tensor", out=sq[b][:], in0=xt[b][:], in1=xt[b][:], op=ALU.mult), r=[("xt", b)], w=[("sq", b)])
            P.op("pe", I("matmul", ps[b][0:1, :], ones_f[:, 0:1], sq[b][:], start=True, stop=True), r=[("sq", b), "ones"], w=[PSK(b)])
            P.op("act", I("activation", out=row[b][:], in_=ps[b][0:1, :], func=AF.Copy), r=[PSK(b)], w=[("row", b)])
            dma("sp", ss_send.ap()[:, sl], row[b][:], [("row", b)], [("ss_send", t)])
        ag(ss_send, ss_all, "ss_send", "ss_all")
        P.barrier()
        A.off = m0

    def norm_seg(src_dram, src_name, dst_dram, dst_name, dst_dt, mulcol, addcol):
        m0 = A.off
        xt = [A.alloc([128, 512], F32) for _ in range(2)]
        s8 = [A.alloc([8, 512], F32) for _ in range(2)]
        rs = [A.alloc([128, 512], F32) for _ in range(2)]
        ot = [A.alloc([128, 512], dst_dt) for _ in range(2)]
        for t in range(NT):
            b = t % 2
            sl = slice(t * 512, (t + 1) * 512)
            dma("sp", xt[b][:], src_dram.ap()[:, sl], [(src_name, t)], [("xt", b)])
            dma("sp", s8[b][:], ss_all.ap()[:, sl], [("ss_all", t)], [("s8", b)])
            P.op("pe", I("matmul", ps[b][:, :], ones_f[0:8, :], s8[b][:], start=True, stop=True), r=[("s8", b), "ones"], w=[PSK(b)])
            P.op("dve", I("tensor_scalar", out=rs[b][:], in0=ps[b][:, :], scalar1=1.0 / D, scalar2=EPS, op0=ALU.mult, op1=ALU.add),
                 r=[PSK(b)], w=[("rs", b)])
            P.op("act", I("activation", out=rs[b][:], in_=rs[b][:], func=AF.Sqrt), r=[("rs", b)], w=[("rs", b)])
            P.op("dve", I("reciprocal", out=rs[b][:], in_=rs[b][:]), r=[("rs", b)], w=[("rs", b)])
            P.op("dve", I("tensor_tensor", out=xt[b][:], in0=xt[b][:], in1=rs[b][:], op=ALU.mult), r=[("xt", b), ("rs", b)], w=[("xt", b)])
            if addcol is None:
                P.op("dve", I("tensor_scalar", out=ot[b][:], in0=xt[b][:], scalar1=mulcol, scalar2=None, op0=ALU.mult),
                     r=[("xt", b), "modc", "small4"], w=[("ot", b)])
            else:
                P.op("dve", I("tensor_scalar", out=ot[b][:], in0=xt[b][:], scalar1=mulcol, scalar2=addcol, op0=ALU.mult, op1=ALU.add),
                     r=[("xt", b), "modc"], w=[("ot", b)])
            dma("sp", dst_dram.ap()[:, sl], ot[b][:], [("ot", b)], [(dst_name, t)])
        P.barrier()
        A.off = m0

    def load_cast(dst, src_ap, shape, key, parts=128):
        m0 = A.off
        st = A.alloc(shape, F32)
        dma("sp", st[0:parts], src_ap, [], [("stg", key)])
        P.op("pool", I("tensor_copy", out=dst, in_=st[0:parts]), r=[("stg", key)], w=[key])
        return m0

    x_dram, x_name = xT, "xT"
    if not chk("init"):
        sumsq_seg(x_dram, x_name)
    for l in range(L):
        if chk("sumsq"):
            break
        m0 = A.off
        wa = A.alloc([128, 8, 384], F32)
        ba = A.alloc([128, 3], F32)
        dma("sp", wa[:], w_ada.ap()[l], [], ["wa"])
        dma("sp", ba[:], b_ada.ap()[l], [], ["ba"])
        dma("sp", small[:, 0:1], norm_g.ap()[l], [], ["small0"])
        dma("sp", small[:, 1:3], qng.ap()[l], [], ["small1"])
        dma("sp", small[:, 3:4], kvng.ap()[l], [], ["small3"])
        for grp in range(3):
            for kc in range(8):
                P.op("pe", I("matmul", ps[0][:, grp:grp + 1], wa[:, kc, grp * 128:(grp + 1) * 128], sc[:, kc:kc + 1],
                                                              start=(kc == 0), stop=(kc == 7)), r=["wa", "sc"], w=[PSK(0)])
        P.op("dve", I("tensor_tensor", out=modc[:, 0:3], in0=ps[0][:, 0:3], in1=ba[:], op=ALU.add), r=[PSK(0), "ba"], w=["modc"])
        P.op("dve", I("scalar_tensor_tensor", out=modc[:, 3:4], in0=modc[:, 1:2], scalar=1.0, in1=small[:, 0:1], op0=ALU.add, op1=ALU.mult),
             r=["modc", "small0"], w=["modc"])
        P.barrier()
        A.off = m0
        norm_seg(x_dram, x_name, hT_send, "hT_send", BF16, modc[:, 3:4], modc[:, 0:1])
        ag(hT_send, hT_all, "hT_send", "hT_all")
        if debug and l == 0:
            dma("sp", dbg["dbg_hT"].ap(), hT_send.ap(), allk("hT_send"), [])
        if debug and l == 1:
            dma("sp", dbg["dbg_hT2"].ap(), hT_send.ap(), allk("hT_send"), [])
        P.barrier()
        if chk("norm"):
            break
        proj_seg(nc, P, A, ps, locals())
        if chk("proj"):
            break
        mla_seg(nc, P, A, ps, locals())
        if debug and l == 0 and stop == "mla":
            dma("sp", dbg["dbg_y"].ap(), y_send.ap(), [], [])
            dma("sp", dbg["dbg_q"].ap(), qT.ap(), [], [])
            dma("sp", dbg["dbg_k"].ap(), kT.ap(), [], [])
            P.barrier()
        if chk("mla"):
            break
        dil_seg(nc, P, A, ps, locals())
        if chk("dil"):
            break
        ag(y_send, y_all, "y_send", "y_all")
        if debug and l == 0:
            dma("sp", dbg["dbg_y"].ap(), y_send.ap(), allk("y_send"), [])
            dma("sp", dbg["dbg_q"].ap(), qT.ap(), allk("qT0") + allk("qT1"), [])
            dma("sp", dbg["dbg_k"].ap(), kT.ap(), allk("kT0") + allk("kT1"), [])
            dma("sp", dbg["dbg_dqk"].ap(), dqk.ap(), [], [])
            dma("sp", dbg["dbg_dv"].ap(), dv.ap(), [], [])
            dma("sp", dbg["dbg_gaz"].ap(), gaz.ap(), [], [])
            dma("sp", dbg["dbg_T5"].ap(), T5[:].rearrange("p a b c -> p (a b c)"), [], [])
        if debug and l == 1:
            dma("sp", dbg["dbg_y2"].ap(), y_send.ap(), allk("y_send"), [])
        P.barrier()
        m0 = A.off
        wo = A.alloc([128, 12, 128], BF16)
        load_cast(wo[:], w_out.ap()[l], [128, 12, 128], "wo")
        ym = [A.alloc([128, 8, 512], BF16) for _ in range(2)]
        yd = [A.alloc([128, 4, 512], BF16) for _ in range(2)]
        gg = [A.alloc([128, 2, 512], BF16) for _ in range(2)]
        t1 = [A.alloc([128, 512], F32) for _ in range(2)]
        t2 = [A.alloc([128, 512], F32) for _ in range(2)]
        mo = [A.alloc([128, 512], BF16) for _ in range(2)]
        for t in range(NT):
            b = t % 2
            sl = slice(t * 512, (t + 1) * 512)
            src = bass.AP(tensor=y_all, offset=t * 512, ap=[[S, 128], [192 * S, 8], [1, 512]])
            dma("sp", ym[b][:], src, [("y_all", t)], [("ym", b)])
            for hf in range(2):
                src = bass.AP(tensor=y_all, offset=(hf * 192 + 128) * S + t * 512, ap=[[S, 64], [384 * S, 4], [1, 512]])
                dma("sp", yd[b][64 * hf:64 * hf + 64], src, [("y_all", t)], [("yd", b, hf)])
            src = bass.AP(tensor=gmg, offset=t * 512, ap=[[S, 128], [128 * S, 2], [1, 512]])
            dma("sp", gg[b][:], src, [("gmg", t)], [("gg", b)])
            pa, pb = 2 * b, 2 * b + 1
            for c in range(4):
                P.op("pe", I("matmul", ps[pa][:, :], wo[:, 8 + c, :], yd[b][:, c, :], start=(c == 0), stop=(c == 3)),
                     r=["wo", ("yd", b, 0), ("yd", b, 1)], w=[PSK(pa)])
            for r_ in range(8):
                P.op("pe", I("matmul", ps[pb][:, :], wo[:, r_, :], ym[b][:, r_, :], start=(r_ == 0), stop=(r_ == 7)),
                     r=["wo", ("ym", b)], w=[PSK(pb)])
            P.op("dve", I("tensor_tensor", out=t1[b][:], in0=ps[pa][:, :], in1=gg[b][:, 0, :], op=ALU.mult),
                 r=[PSK(pa), ("gg", b)], w=[("t1", b)])
            P.op("dve", I("tensor_tensor", out=t2[b][:], in0=ps[pb][:, :], in1=gg[b][:, 1, :], op=ALU.mult),
                 r=[PSK(pb), ("gg", b)], w=[("t2", b)])
            P.op("pool", I("tensor_tensor", out=mo[b][:], in0=t1[b][:], in1=t2[b][:], op=ALU.add),
                 r=[("t1", b), ("t2", b)], w=[("mo", b)])
            dma("sp", mg_send.ap()[:, sl], mo[b][:], [("mo", b)], [("mg_send", t)])
        ag(mg_send, mg_all, "mg_send", "mg_all")
        P.barrier()
        A.off = m0
        m0 = A.off
        wot = A.alloc([128, 8, 128], BF16)
        load_cast(wot[:], w_o.ap()[l], [128, 8, 128], "wot")
        mt = [A.alloc([128, 8, 512], BF16) for _ in range(2)]
        xt = [A.alloc([128, 512], F32) for _ in range(2)]
        xn = [A.alloc([128, 512], F32) for _ in range(2)]
        sq = [A.alloc([128, 512], F32) for _ in range(2)]
        row = [A.alloc([1, 512], F32) for _ in range(2)]
        for t in range(NT):
            b = t % 2
            sl = slice(t * 512, (t + 1) * 512)
            src = bass.AP(tensor=mg_all, offset=t * 512, ap=[[S, 128], [128 * S, 8], [1, 512]])
            dma("sp", mt[b][:], src, [("mg_all", t)], [("mt", b)])
            dma("sp", xt[b][:], x_dram.ap()[:, sl], [(x_name, t)], [("xt", b)])
            pa, pb = 2 * b, 2 * b + 1
            for kc in range(8):
                P.op("pe", I("matmul", ps[pa][:, :], wot[:, kc, :], mt[b][:, kc, :], start=(kc == 0), stop=(kc == 7)),
                     r=["wot", ("mt", b)], w=[PSK(pa)])
            P.op("dve", I("scalar_tensor_tensor", out=xn[b][:], in0=ps[pa][:, :], scalar=modc[:, 2:3], in1=xt[b][:],
                                                                     op0=ALU.mult, op1=ALU.add), r=[PSK(pa), ("xt", b), "modc"], w=[("xn", b)])
            dma("sp", xcur.ap()[:, sl], xn[b][:], [("xn", b)], [("xcur", t)])
            P.op("pool", I("tensor_tensor", out=sq[b][:], in0=xn[b][:], in1=xn[b][:], op=ALU.mult), r=[("xn", b)], w=[("sq", b)])
            P.op("pe", I("matmul", ps[pb][0:1, :], ones_f[:, 0:1], sq[b][:], start=True, stop=True), r=[("sq", b), "ones"], w=[PSK(pb)])
            P.op("act", I("activation", out=row[b][:], in_=ps[pb][0:1, :], func=AF.Copy), r=[PSK(pb)], w=[("row", b)])
            dma("sp", ss_send.ap()[:, sl], row[b][:], [("row", b)], [("ss_send", t)])
        ag(ss_send, ss_all, "ss_send", "ss_all")
        P.barrier()
        A.off = m0
        x_dram, x_name = xcur, "xcur"
        if debug and l == 0:
            dma("sp", dbg["dbg_x"].ap(), xcur.ap(), allk("xcur"), [])
            P.barrier()
        if debug and l == 1:
            dma("sp", dbg["dbg_x2"].ap(), xcur.ap(), allk("xcur"), [])
            dma("sp", dbg["dbg_mg2"].ap(), mg_send.ap(), [], [])
            P.barrier()
    if not STOPPED[0]:
        norm_seg(x_dram, x_name, outT, "outT", F32, small[:, 4:5], None)
    P.barrier()
    P.op("sp", lambda e: None, r=[])

    sem_names = ["pe", "act", "dve", "pool", "sp", "cc"] + [("sp", i) for i in range(NSLOT)] + [("pool", i) for i in range(NSLOT)]
    sems = {}
    import contextlib
    with contextlib.ExitStack() as stk:
        for i, nm in enumerate(sem_names):
            sems[nm] = stk.enter_context(nc.semaphore("s%d" % i))
        block = stk.enter_context(nc.Block())
        P.emit(nc, block, sems)
    return nc


def proj_seg(nc, P, A, ps, env):
    S, NT, l = env["S"], env["NT"], env["l"]
    dma, load_cast = env["dma"], env["load_cast"]
    small, ones_f = env["small"], env["ones_f"]
    hT_all, w_in, w_uq, w_ukv, cs = env["hT_all"], env["w_in"], env["w_uq"], env["w_ukv"], env["cs"]
    qT, kT, vm, dqk, dv, gaz, gmz, gmg = (env[k] for k in ("qT", "kT", "vm", "dqk", "dv", "gaz", "gmz", "gmg"))
    PSK = env["PSK"]
    m0 = A.off
    win = A.alloc([128, 8, NW + 32], BF16)
    for pc in range(4):
        mm = A.off
        st = A.alloc([128, 8, 360], F32)
        dma("sp", st[:], w_in.ap()[l][:, :, pc * 360:(pc + 1) * 360], [], [("wst", pc)])
        P.op("pool" if pc % 2 else "dve", I("tensor_copy", out=win[:, :, pc * 360:(pc + 1) * 360], in_=st[:]),
             r=[("wst", pc)], w=[("win", pc)])
    winK = [("win", pc) for pc in range(4)]
    P.op("dve", I("tensor_scalar", out=win[:, :, NW:NW + 16], in0=win[:, :, 400:416], scalar1=-1.0, scalar2=None, op0=ALU.mult),
         r=winK, w=["winrot"])
    P.op("dve", I("tensor_copy", out=win[:, :, NW + 16:NW + 32], in_=win[:, :, 384:400]), r=winK, w=["winrot2"])
    winK = winK + ["winrot", "winrot2"]
    wuq = A.alloc([128, 2, 192 + 64], BF16)
    st = A.alloc([128, 2, 192], F32)
    dma("sp", st[:], w_uq.ap()[l], [], ["wuqs"])
    P.op("dve", I("tensor_copy", out=wuq[:, :, 0:192], in_=st[:]), r=["wuqs"], w=["wuq"])
    for h in range(2):
        P.op("dve", I("tensor_scalar", out=wuq[:, :, 192 + 32 * h:192 + 32 * h + 16], in0=st[:, :, 96 * h + 16:96 * h + 32],
                                                   scalar1=-1.0, scalar2=None, op0=ALU.mult), r=["wuqs"], w=[("wuqr", h)])
        P.op("dve", I("tensor_copy", out=wuq[:, :, 192 + 32 * h + 16:192 + 32 * h + 32], in_=st[:, :, 96 * h:96 * h + 16]),
             r=["wuqs"], w=[("wuqr2", h)])
    wuqK = ["wuq"] + [("wuqr", h) for h in range(2)] + [("wuqr2", h) for h in range(2)]
    wkv = A.alloc([128, 256], BF16)
    st2 = A.alloc([128, 256], F32)
    dma("sp", st2[:], w_ukv.ap()[l], [], ["wkvs"])
    P.op("dve", I("tensor_copy", out=wkv[:], in_=st2[:]), r=["wkvs"], w=["wkv"])

    NBUF = 2
    hT = [A.alloc([128, 8, 512], BF16) for _ in range(NBUF)]
    csq = [A.alloc([32, 2, 512], F32) for _ in range(NBUF)]
    oq = [A.alloc([128, 3, 512], BF16) for _ in range(NBUF)]
    ogz = [A.alloc([128, 4, 512], BF16) for _ in range(NBUF)]
    sgt = [A.alloc([128, 512], F32) for _ in range(NBUF)]
    cq2 = [A.alloc([128, 2, 512], F32) for _ in range(NBUF)]
    rsq = [A.alloc([128, 512], F32) for _ in range(NBUF)]
    rsk = [A.alloc([128, 512], F32) for _ in range(NBUF)]
    cqn = [A.alloc([128, 2, 512], BF16) for _ in range(NBUF)]
    ckn = [A.alloc([128, 512], BF16) for _ in range(NBUF)]
    krp = [A.alloc([32, 512], BF16) for _ in range(NBUF)]
    rt1 = [A.alloc([32, 512], F32) for _ in range(NBUF)]
    rt2 = [A.alloc([32, 512], F32) for _ in range(NBUF)]
    qo = [A.alloc([96, 2, 512], BF16) for _ in range(NBUF)]
    ko = [A.alloc([64, 2, 512], BF16) for _ in range(NBUF)]
    vdo = [A.alloc([128, 4, 192], BF16) for _ in range(NBUF)]
    vmo = [A.alloc([128, 4, 128], BF16) for _ in range(NBUF)]

    CQ, CKV, KR, DQK, DV, AZ, MZ, MG = 0, 256, 384, 416, 800, 992, 1056, 1184
    pcnt = [0]

    def nps():
        pcnt[0] = (pcnt[0] + 1) % 8
        return pcnt[0]

    def mm_feat(b, pi, rows, c0, ncols=None):
        for kc in range(8):
            P.op("pe", I("matmul", ps[pi][0:rows, :], win[:, kc, c0:c0 + rows], hT[b][:, kc, :], start=(kc == 0), stop=(kc == 7)),
                 r=winK + [("hT", b)], w=[PSK(pi)])

    for t in range(NT):
        b = t % NBUF
        sl = slice(t * 512, (t + 1) * 512)
        src = bass.AP(tensor=hT_all, offset=t * 512, ap=[[S, 128], [128 * S, 8], [1, 512]])
        dma("sp", hT[b][:], src, [("hT_all", t)], [("hT", b)])
        src = bass.AP(tensor=cs, offset=t * 512, ap=[[S, 32], [32 * S, 2], [1, 512]])
        dma("sp", csq[b][:], src, [("cs", 0, q) for q in range(4)] + [("cs", 1, q) for q in range(4)], [("csq", b)])
        for g, r_ in (enumerate((1, 4, 16)) if 'dil' in PARTS else []):
            pi = nps()
            mm_feat(b, pi, 128, DQK + 128 * g)
            if r_ == 1:
                P.op("act", I("activation", out=oq[b][:, g, :], in_=ps[pi][:, :], func=AF.Copy), r=[PSK(pi)], w=[("oq", b, g)])
            else:
                nm = 512 // r_
                inv = bass.AP(tensor=ps[pi], offset=0, ap=[[512, 128], [1, r_], [r_, nm]])
                if r_ == 4:
                    outv = bass.AP(tensor=oq[b], offset=g * 512, ap=[[1536, 128], [128, 4], [1, 128]])
                else:
                    outv = None
                if r_ == 4:
                    P.op("act", I("activation", out=outv, in_=inv, func=AF.Copy), r=[PSK(pi)], w=[("oq", b, g)])
                else:
                    outv = bass.AP(tensor=oq[b], offset=g * 512, ap=[[1536, 128], [32, 16], [1, 32]])
                    P.op("act", I("activation", out=outv, in_=inv, func=AF.Copy), r=[PSK(pi)], w=[("oq", b, g)])
            if r_ == 16:
                ch, sub = t // 4, t % 4
                dst = bass.AP(tensor=dqk, offset=g * 128 * S + ch * 2048 + 32 * sub, ap=[[S, 128], [128, 16], [1, 32]])
                srcv = bass.AP(tensor=oq[b], offset=g * 512, ap=[[1536, 128], [32, 16], [1, 32]])
                dma("sp", dst, srcv, [("oq", b, g)], [("dqk", g, t)])
            else:
                dma("sp", dqk.ap()[g, :, sl], oq[b][:, g, :], [("oq", b, g)], [("dqk", g, t)])
        for gi, (c0, rows, dst, nm) in enumerate(() if 'gates' not in PARTS else ((AZ, 64, gaz.ap()[:, sl], "gaz"), (MZ, 128, gmz.ap()[:, sl], "gmz"),
                                                 (MG, 128, gmg.ap()[0, :, sl], "gmg0"), (MG + 128, 128, gmg.ap()[1, :, sl], "gmg1"))):
            pi = nps()
            mm_feat(b, pi, rows, c0)
            if gi < 2:
                P.op("act", I("activation", out=sgt[b][0:rows, :], in_=ps[pi][0:rows, :], func=AF.Sigmoid),
                     r=[PSK(pi)], w=[("sgt", b)])
                P.op("dve", I("tensor_tensor", out=ogz[b][0:rows, gi, :], in0=ps[pi][0:rows, :], in1=sgt[b][0:rows, :], op=ALU.mult),
                     r=[PSK(pi), ("sgt", b)], w=[("ogz", b, gi)])
            else:
                P.op("act", I("activation", out=ogz[b][:, gi, :], in_=ps[pi][:, :], func=AF.Sigmoid), r=[PSK(pi)], w=[("ogz", b, gi)])
            dma("sp", dst, ogz[b][0:rows, gi, :], [("ogz", b, gi)], [(nm, t)])
        if PSTOP == 'gates':
            continue
        pq = [nps(), nps()]
        for i in range(2):
            mm_feat(b, pq[i], 128, CQ + 128 * i)
            P.op("act", I("activation", out=cq2[b][:, i, :], in_=ps[pq[i]][:, :], func=AF.Square),
                 r=[PSK(pq[i])], w=[("cq2", b, i)])
        pr = nps()
        for i in range(2):
            P.op("pe", I("matmul", ps[pr][:, :], ones_f[:, :], cq2[b][:, i, :], start=(i == 0), stop=(i == 1)),
                 r=["ones", ("cq2", b, 0), ("cq2", b, 1)], w=[PSK(pr)])
        P.op("dve", I("tensor_scalar", out=rsq[b][:], in0=ps[pr][:, :], scalar1=1.0 / 256, scalar2=EPS, op0=ALU.mult, op1=ALU.add),
             r=[PSK(pr)], w=[("rsq", b)])
        P.op("act", I("activation", out=rsq[b][:], in_=rsq[b][:], func=AF.Sqrt), r=[("rsq", b)], w=[("rsq", b)])
        P.op("dve", I("reciprocal", out=rsq[b][:], in_=rsq[b][:]), r=[("rsq", b)], w=[("rsq", b)])
        for i in range(2):
            P.op("dve", I("scalar_tensor_tensor", out=cqn[b][:, i, :], in0=ps[pq[i]][:, :], scalar=small[:, 1 + i:2 + i], in1=rsq[b][:],
                                                              op0=ALU.mult, op1=ALU.mult), r=[PSK(pq[i]), ("rsq", b), "small1"], w=[("cqn", b, i)])
        pk_ = nps()
        mm_feat(b, pk_, 128, CKV)
        P.op("act", I("activation", out=cq2[b][:, 0, :], in_=ps[pk_][:, :], func=AF.Square),
             r=[PSK(pk_)], w=[("cq2", b, 0)])
        pr2 = nps()
        P.op("pe", I("matmul", ps[pr2][:, :], ones_f[:, :], cq2[b][:, 0, :], start=True, stop=True),
             r=["ones", ("cq2", b, 0)], w=[PSK(pr2)])
        P.op("dve", I("tensor_scalar", out=rsk[b][:], in0=ps[pr2][:, :], scalar1=1.0 / 128, scalar2=EPS, op0=ALU.mult, op1=ALU.add),
             r=[PSK(pr2)], w=[("rsk", b)])
        P.op("act", I("activation", out=rsk[b][:], in_=rsk[b][:], func=AF.Sqrt), r=[("rsk", b)], w=[("rsk", b)])
        P.op("dve", I("reciprocal", out=rsk[b][:], in_=rsk[b][:]), r=[("rsk", b)], w=[("rsk", b)])
        P.op("dve", I("scalar_tensor_tensor", out=ckn[b][:], in0=ps[pk_][:, :], scalar=small[:, 3:4], in1=rsk[b][:],
                                                              op0=ALU.mult, op1=ALU.mult), r=[PSK(pk_), ("rsk", b), "small3"], w=[("ckn", b)])
        if PSTOP == 'lat':
            continue
        p1, p2 = nps(), nps()
        for kc in range(8):
            P.op("pe", I("matmul", ps[p1][0:32, :], win[:, kc, KR:KR + 32], hT[b][:, kc, :], start=(kc == 0), stop=(kc == 7)),
                 r=winK + [("hT", b)], w=[PSK(p1)])
        for kc in range(8):
            P.op("pe", I("matmul", ps[p2][0:32, :], win[:, kc, NW:NW + 32], hT[b][:, kc, :], start=(kc == 0), stop=(kc == 7)),
                 r=winK + [("hT", b)], w=[PSK(p2)])
        P.op("dve", I("tensor_tensor", out=rt1[b][:], in0=ps[p1][0:32, :], in1=csq[b][:, 0, :], op=ALU.mult), r=[PSK(p1), ("csq", b)], w=[("rt1", b)])
        P.op("dve", I("tensor_tensor", out=rt2[b][:], in0=ps[p2][0:32, :], in1=csq[b][:, 1, :], op=ALU.mult), r=[PSK(p2), ("csq", b)], w=[("rt2", b)])
        P.op("dve", I("tensor_tensor", out=krp[b][:], in0=rt1[b][:], in1=rt2[b][:], op=ALU.add), r=[("rt1", b), ("rt2", b)], w=[("krp", b)])
        for h in range(2):
            dma("sp", kT.ap()[h, 0:32, sl], krp[b][:], [("krp", b)], [("kT%d" % h, t, "r")])
        if PSTOP == 'krope':
            continue
        for h in range(2):
            pa_, pb_ = nps(), nps()
            for i in range(2):
                P.op("pe", I("matmul", ps[pa_][0:96, :], wuq[:, i, 96 * h:96 * h + 96], cqn[b][:, i, :], start=(i == 0), stop=(i == 1)),
                     r=wuqK + [("cqn", b, 0), ("cqn", b, 1)], w=[PSK(pa_)])
            for i in range(2):
                P.op("pe", I("matmul", ps[pb_][0:32, :], wuq[:, i, 192 + 32 * h:192 + 32 * h + 32], cqn[b][:, i, :], start=(i == 0), stop=(i == 1)),
                     r=wuqK + [("cqn", b, 0), ("cqn", b, 1)], w=[PSK(pb_)])
            P.op("act", I("activation", out=qo[b][32:64, h, :], in_=ps[pa_][32:64, :], func=AF.Copy), r=[PSK(pa_)], w=[("qo", b, h, "n")])
            P.op("act", I("activation", out=qo[b][64:96, h, :], in_=ps[pa_][64:96, :], func=AF.Copy), r=[PSK(pa_)], w=[("qo", b, h, "n2")])
            P.op("dve", I("tensor_tensor", out=rt1[b][:], in0=ps[pa_][0:32, :], in1=csq[b][:, 0, :], op=ALU.mult),
                 r=[PSK(pa_), ("csq", b)], w=[("rt1", b)])
            P.op("dve", I("tensor_tensor", out=rt2[b][:], in0=ps[pb_][0:32, :], in1=csq[b][:, 1, :], op=ALU.mult),
                 r=[PSK(pb_), ("csq", b)], w=[("rt2", b)])
            P.op("pool", I("tensor_tensor", out=qo[b][0:32, h, :], in0=rt1[b][:], in1=rt2[b][:], op=ALU.add),
                 r=[("rt1", b), ("rt2", b)], w=[("qo", b, h, "r")])
            dma("sp", qT.ap()[h, :, sl], qo[b][:, h, :], [("qo", b, h, "n"), ("qo", b, h, "n2"), ("qo", b, h, "r")], [("qT%d" % h, t)])
        if PSTOP == 'q':
            continue
        for h in range(2):
            pi = nps()
            P.op("pe", I("matmul", ps[pi][0:64, :], wkv[:, 64 * h:64 * h + 64], ckn[b][:], start=True, stop=True),
                 r=["wkv", ("ckn", b)], w=[PSK(pi)])
            P.op("act", I("activation", out=ko[b][:, h, :], in_=ps[pi][0:64, :], func=AF.Copy), r=[PSK(pi)], w=[("ko", b, h)])
            dma("sp", kT.ap()[h, 32:96, sl], ko[b][:, h, :], [("ko", b, h)], [("kT%d" % h, t, "n")])
        if PSTOP == 'k':
            continue
        pi = nps()
        for s4 in range(4):
            P.op("pe", I("matmul", ps[pi][:, 128 * s4:128 * s4 + 128], ckn[b][:, 128 * s4:128 * s4 + 128], wkv[:, 128:256], start=True, stop=True),
                 r=["wkv", ("ckn", b)], w=[PSK(pi)])
        P.op("act", I("activation", out=vmo[b][:, :, :], in_=ps[pi][:, :].rearrange("p (a c) -> p a c", a=4), func=AF.Copy),
             r=[PSK(pi)], w=[("vmo", b)])
        dst = bass.AP(tensor=vm, offset=t * 512 * 128, ap=[[128, 128], [128 * 128, 4], [1, 128]])
        dma("sp", dst, vmo[b][:, :, :], [("vmo", b)], [("vm", t)])
        pi = nps()
        pj = nps()
        for s4 in range(4):
            pp = pi if s4 < 2 else pj
            o0 = 192 * (s4 % 2)
            for kc in range(8):
                P.op("pe", I("matmul", ps[pp][:, o0:o0 + 192], hT[b][:, kc, 128 * s4:128 * s4 + 128], win[:, kc, DV:DV + 192],
                                                                         start=(kc == 0), stop=(kc == 7)), r=winK + [("hT", b)], w=[PSK(pp)])
        P.op("act", I("activation", out=vdo[b][:, 0:2, :], in_=ps[pi][:, 0:384].rearrange("p (a c) -> p a c", a=2), func=AF.Copy),
             r=[PSK(pi)], w=[("vdo", b, 0)])
        P.op("act", I("activation", out=vdo[b][:, 2:4, :], in_=ps[pj][:, 0:384].rearrange("p (a c) -> p a c", a=2), func=AF.Copy),
             r=[PSK(pj)], w=[("vdo", b, 1)])
        dst = bass.AP(tensor=dv, offset=t * 512 * 192, ap=[[192, 128], [128 * 192, 4], [1, 192]])
        dma("sp", dst, vdo[b][:, :, :], [("vdo", b, 0), ("vdo", b, 1)], [("dv", t)])
    P.barrier()
    A.off = m0


def mla_seg(nc, P, A, ps, env):
    S, NT, NB = env["S"], env["NT"], env["NB"]
    dma = env["dma"]
    ones_f, tri = env["ones_f"], env["tri"]
    qT, kT, vm, gmz, y_send = env["qT"], env["kT"], env["vm"], env["gmz"], env["y_send"]
    PSK = env["PSK"]
    m0 = A.off
    scale = 96.0 ** -0.5
    KT = A.alloc([96, S], BF16)
    VA = A.alloc([128, NB, 65], BF16)
    QT = [A.alloc([96, 512], BF16) for _ in range(2)]
    GZ = [A.alloc([64, 512], BF16) for _ in range(2)]
    PT = [A.alloc([128, 512], BF16) for _ in range(4)]
    lrow = A.alloc([65, 512], F32)
    rbc = A.alloc([64, 512], F32)
    yf = A.alloc([64, 512], F32)
    yo = [A.alloc([64, 512], BF16) for _ in range(2)]
    P.op("pool", I("memset", VA[:, :, 64:65], 1.0), w=["VAones"])
    blk = 0
    for h in range(2):
        allkT = [("kT%d" % h, t, "r") for t in range(NT)] + [("kT%d" % h, t, "n") for t in range(NT)]
        dma("sp", KT[:], kT.ap()[h], allkT, ["KT"])
        for v0 in range(0, NB, 16):
            src = bass.AP(tensor=vm, offset=64 * h + v0 * 128 * 128, ap=[[128, 128], [128 * 128, 16], [1, 64]])
            dma("sp", VA[:, v0:v0 + 16, 0:64], src, [("vm", t) for t in range(NT)], [("VA", v0)])
        for qt in range(NT):
            qb = qt % 2
            sl = slice(qt * 512, (qt + 1) * 512)
            dma("sp", QT[qb][:], qT.ap()[h, :, sl], [("qT%d" % h, qt)], [("QT", qb)])
            dma("sp", GZ[qb][:], gmz.ap()[64 * h:64 * h + 64, sl], [("gmz", qt)], [("GZ", qb)])
            po = 4 + qb
            nkb = 4 * qt + 4
            for kb in range(nkb):
                d = kb - 4 * qt
                c0 = 128 * d if d > 0 else 0
                sb_ = blk % 4
                pb_ = blk % 4
                blk += 1
                P.op("pe", I("matmul", ps[pb_][:, c0:512], KT[:, kb * 128:(kb + 1) * 128], QT[qb][:, c0:512], start=True, stop=True),
                     r=["KT", ("QT", qb)], w=[PSK(pb_)])
                P.op("act", I("activation", out=PT[sb_][:, c0:512], in_=ps[pb_][:, c0:512], func=AF.Exp, scale=scale),
                     r=[PSK(pb_)], w=[("PT", sb_)])
                if d >= 0:
                    P.op("dve", I("tensor_tensor", out=PT[sb_][:, c0:c0 + 128], in0=PT[sb_][:, c0:c0 + 128], in1=tri[:], op=ALU.mult),
                         r=[("PT", sb_), "tri"], w=[("PT", sb_)])
                P.op("pe", I("matmul", ps[po][0:65, c0:512], VA[:, kb, :], PT[sb_][:, c0:512], start=(kb == 0), stop=(kb == nkb - 1)),
                     r=[("VA", kb // 16 * 16), "VAones", ("PT", sb_)], w=[PSK(po)])
            P.op("act", I("activation", out=lrow[64:65, :], in_=ps[po][64:65, :], func=AF.Copy), r=[PSK(po)], w=["lrow"])
            P.op("pe", I("matmul", ps[6][0:64, :], ones_f[64:65, 0:64], lrow[64:65, :], start=True, stop=True), r=["lrow", "ones"], w=[PSK(6)])
            P.op("dve", I("reciprocal", out=rbc[:], in_=ps[6][0:64, :]), r=[PSK(6)], w=["rbc"])
            P.op("dve", I("tensor_tensor", out=yf[:], in0=ps[po][0:64, :], in1=rbc[:], op=ALU.mult), r=[PSK(po), "rbc"], w=["yf"])
            P.op("pool", I("tensor_tensor", out=yo[qb][:], in0=yf[:], in1=GZ[qb][:], op=ALU.mult), r=["yf", ("GZ", qb)], w=[("yo", qb)])
            dma("sp", y_send.ap()[64 * h:64 * h + 64, sl], yo[qb][:], [("yo", qb)], [("y_send", qt, h)])
    P.barrier()
    A.off = m0


def dil_seg(nc, P, A, ps, env):
    S, NT, NCH = env["S"], env["NT"], env["NCH"]
    dma = env["dma"]
    ones_f, T5 = env["ones_f"], env["T5"]
    dqk, dv, gaz, y_send = env["dqk"], env["dv"], env["gaz"], env["y_send"]
    PSK = env["PSK"]
    m0 = A.off
    DQ = [A.alloc([64, 3, 2048], BF16) for _ in range(2)]
    DK = [A.alloc([64, 3, 2048], BF16) for _ in range(2)]
    DVt = [A.alloc([128, 3, 16, 65], BF16) for _ in range(2)]
    ACC = A.alloc([65, 2048], F32)
    EX = [A.alloc([128, 512], F32) for _ in range(2)]
    PTd = [A.alloc([128, 512], BF16) for _ in range(2)]
    GA = A.alloc([64, 2048], BF16)
    lrow = A.alloc([65, 512], F32)
    rbc = A.alloc([64, 512], F32)
    yf = A.alloc([64, 512], F32)
    yo = [A.alloc([64, 512], BF16) for _ in range(2)]
    for pb in range(2):
        P.op("pool", I("memset", DVt[pb][:, :, :, 64:65], 1.0), w=[("DVones", pb)])
    RS = (1, 4, 16)
    cnt = 0
    for c in range(NCH):
        pb = c % 2
        t4 = [4 * c + i for i in range(4)]
        for g in range(3):
            dma("sp", DQ[pb][:, g, :], dqk.ap()[g, 0:64, c * 2048:(c + 1) * 2048], [("dqk", g, t) for t in t4], [("DQ", pb, g)])
            dma("sp", DK[pb][:, g, :], dqk.ap()[g, 64:128, c * 2048:(c + 1) * 2048], [("dqk", g, t) for t in t4], [("DK", pb, g)])
        rdv = [("dv", t) for t in t4]
        src = bass.AP(tensor=dv, offset=c * 2048 * 192, ap=[[192, 128], [128 * 192, 16], [1, 64]])
        dma("sp", DVt[pb][:, 0, :, 0:64], src, rdv, [("DV", pb, 0)])
        for nl in range(4):
            src = bass.AP(tensor=dv, offset=(c * 2048 + 512 * nl) * 192 + 64, ap=[[4 * 192, 128], [192, 4], [1, 64]])
            dma("sp", DVt[pb][:, 1, 4 * nl:4 * nl + 4, 0:64], src, rdv, [("DV", pb, 1, nl)])
        src = bass.AP(tensor=dv, offset=c * 2048 * 192 + 128, ap=[[16 * 192, 128], [192, 16], [1, 64]])
        dma("sp", DVt[pb][:, 2, :, 0:64], src, rdv, [("DV", pb, 2)])
        dma("sp", GA[:], gaz.ap()[:, c * 2048:(c + 1) * 2048], [("gaz", t) for t in t4], ["GA"])
        dvk = lambda p_, g: ([("DV", p_, g)] if g != 1 else [("DV", p_, 1, nl) for nl in range(4)]) + [("DVones", p_)]
        for g, r_ in (enumerate(RS) if DSTOP != 'load' else []):
            for qd in range(4):
                po = 4 + (cnt % 2)
                kinds = []
                a0 = 0
                if c == 0:
                    npv = [a for a in range(4) if 4 * qd + a >= r_]
                    if npv:
                        a0 = npv[0]
                        kinds.append(1)
                else:
                    kinds.append(1)
                kinds.append(0)
                kinfo = []
                for kd in kinds:
                    aa = a0 if kd == 1 else 0
                    pS = cnt % 4
                    eb = cnt % 2
                    cnt += 1
                    rk = {}
                    for a in range(aa, 4):
                        i = 4 * qd + a
                        if kd == 0:
                            ks, kp = i, pb
                        else:
                            ks, kp = (i - r_, pb) if i >= r_ else (i - r_ + 16, 1 - pb)
                        rk[a] = (ks, kp)
                    for a, (ks, kp) in rk.items():
                        i = 4 * qd + a
                        P.op("pe", I("matmul", ps[pS][:, 128 * a:128 * a + 128], DK[kp][:, g, 128 * ks:128 * ks + 128],
                                     DQ[pb][:, g, 128 * i:128 * i + 128], start=True, stop=True),
                             r=[("DK", kp, g), ("DQ", pb, g)], w=[PSK(pS)])
                    P.op("act", I("activation", out=EX[eb][:, 128 * aa:512], in_=ps[pS][:, 128 * aa:512], func=AF.Exp, scale=0.125),
                         r=[PSK(pS)], w=[("EX", eb)])
                    for a in range(aa, 4):
                        P.op("dve", I("tensor_tensor", out=PTd[eb][:, 128 * a:128 * a + 128], in0=EX[eb][:, 128 * a:128 * a + 128],
                                      in1=T5[:, g, kd, :], op=ALU.mult), r=[("EX", eb), ("T5", g, kd)], w=[("PTd", eb)])
                    kinfo.append((kd, eb, rk))
                for a in range(4):
                    seq = [(kd, eb, rk[a]) for (kd, eb, rk) in kinfo if a in rk]
                    for n_, (kd, eb, (ks, kp)) in enumerate(seq):
                        P.op("pe", I("matmul", ps[po][0:65, 128 * a:128 * a + 128], DVt[kp][:, g, ks, :], PTd[eb][:, 128 * a:128 * a + 128],
                                     start=(n_ == 0), stop=(n_ == len(seq) - 1)),
                             r=dvk(kp, g) + [("PTd", eb)], w=[PSK(po)])
                inv = bass.AP(tensor=ps[po], offset=0, ap=[[512, 65], [128, 4], [1, 128]])
                if DSTOP in ('qk', 'pv'):
                    continue
                for (p0, np_) in ((0, 64), (64, 1)):
                    invp = bass.AP(tensor=ps[po], offset=p0 * 512, ap=[[512, np_], [128, 4], [1, 128]])
                    if g == 0:
                        P.op("act", I("activation", out=ACC[p0:p0 + np_, 512 * qd:512 * qd + 512], in_=ps[po][p0:p0 + np_, :], func=AF.Copy),
                             r=[PSK(po)], w=[("ACC", qd, p0)])
                    elif g == 1:
                        av = bass.AP(tensor=ACC, offset=p0 * 2048 + 512 * qd, ap=[[2048, np_], [1, 4], [4, 128]])
                        P.op("dve", I("tensor_tensor", out=av, in0=invp, in1=av, op=ALU.add), r=[PSK(po), ("ACC", qd, p0)], w=[("ACC", qd, p0)])
                    else:
                        av = bass.AP(tensor=ACC, offset=p0 * 2048 + 4 * qd, ap=[[2048, np_], [1, 4], [16, 128]])
                        allacc = [("ACC", q_, p0) for q_ in range(4)]
                        P.op("dve", I("tensor_tensor", out=av, in0=invp, in1=av, op=ALU.add), r=[PSK(po)] + allacc, w=allacc)
        for qd in (range(4) if DSTOP is None else []):
            ob = qd % 2
            cs_ = slice(512 * qd, 512 * qd + 512)
            P.op("act", I("activation", out=lrow[64:65, :], in_=ACC[64:65, cs_], func=AF.Copy), r=[("ACC", q_, 64) for q_ in range(4)], w=["lrow"])
            P.op("pe", I("matmul", ps[6][0:64, :], ones_f[64:65, 0:64], lrow[64:65, :], start=True, stop=True),
                 r=["lrow", "ones"], w=[PSK(6)])
            P.op("dve", I("reciprocal", out=rbc[:], in_=ps[6][0:64, :]), r=[PSK(6)], w=["rbc"])
            P.op("dve", I("tensor_tensor", out=yf[:], in0=ACC[0:64, cs_], in1=rbc[:], op=ALU.mult), r=[("ACC", q_, 0) for q_ in range(4)] + ["rbc"], w=["yf"])
            P.op("dve", I("tensor_tensor", out=yo[ob][:], in0=yf[:], in1=GA[:, cs_], op=ALU.mult), r=["yf", "GA"], w=[("yo", ob)])
            dma("sp", y_send.ap()[128:192, c * 2048 + 512 * qd:c * 2048 + 512 * qd + 512], yo[ob][:], [("yo", ob)], [("y_send", 4 * c + qd, 2)])
    P.barrier()
    A.off = m0


def t5_bucket_np(dist):
    dist = np.asarray(dist, dtype=np.int64)
    d = np.maximum(dist, 1).astype(np.float32)
    large = 16 + (np.log(d / np.float32(16)) / np.float32(math.log(2048 / 16)) * np.float32(16)).astype(np.int32)
    large = np.minimum(large, 31)
    return np.where(dist < 16, dist, large)


def host_inputs(x, c, positions, w_ada, b_ada, norm_g, w_in, q_norm_g, w_uq, kv_norm_g, w_ukv,
                w_out_a, w_out_b, w_o, rel_bias, final_norm_g):
    S = x.shape[1]
    L = w_in.shape[0]
    xT_full = np.ascontiguousarray(x[0].T)
    f32 = np.float32
    inv_freq = (1.0 / (10000.0 ** (np.arange(0, 32, 2, dtype=f32) / f32(32)))).astype(f32)
    invf = np.tile(np.concatenate([inv_freq, inv_freq]), 4).reshape(128, 1).astype(f32)
    oh5 = np.zeros((32, 3, 2, 256), f32)
    m5 = np.zeros((128, 3, 2, 256), f32)
    for g, r in enumerate((1, 4, 16)):
        for i in range(128):
            oh5[t5_bucket_np(i * r), g, 0, i] = 1.0
            m5[:, g, 0, i] = 1.0
        oh5[t5_bucket_np(128 * r), g, 1, 0] = 1.0
        m5[:, g, 1, 0] = 1.0
        for i in range(129, 256):
            oh5[t5_bucket_np((i - 128) * r), g, 1, i] = 1.0
            m5[:, g, 1, i] = 1.0
    kk = np.arange(128)[:, None]
    cc = np.arange(128)[None, :]
    tri = (kk <= cc).astype(f32).astype(ml_dtypes.bfloat16)
    cT = np.ascontiguousarray(c[0].reshape(8, 128).T)

    def kchunk(w):
        return np.ascontiguousarray(w.reshape(8, 128, -1).transpose(1, 0, 2))

    maps = []
    for j in range(NCORE):
        C = slice(128 * j, 128 * j + 128)
        cols = np.concatenate([np.arange(128 * j, 128 * j + 128), 1024 + np.arange(128 * j, 128 * j + 128), 2048 + np.arange(128 * j, 128 * j + 128)])
        wsel = []
        wsel.append(np.arange(5120, 5120 + 416))
        for g in range(3):
            wsel.append((0 * 3 + g) * 512 + j * 64 + np.arange(64))
            wsel.append((1 * 3 + g) * 512 + j * 64 + np.arange(64))
        for g in range(3):
            wsel.append((2 * 3 + g) * 512 + j * 64 + np.arange(64))
        wsel.append(4608 + j * 64 + np.arange(64))
        wsel.append(5536 + j * 128 + np.arange(128))
        wsel.append(6560 + 128 * j + np.arange(128))
        wsel.append(6560 + 1024 + 128 * j + np.arange(128))
        wsel = np.concatenate(wsel)
        assert wsel.size == NW
        uq_cols = np.concatenate([np.concatenate([hh * 96 + 64 + np.arange(32), hh * 96 + np.arange(64)]) for hh in (2 * j, 2 * j + 1)])
        ukv_cols = np.concatenate([(2 * j) * 128 + np.arange(64), (2 * j + 1) * 128 + np.arange(64),
                                   (2 * j) * 128 + 64 + np.arange(64), (2 * j + 1) * 128 + 64 + np.arange(64)])
        m = {
            "xT": np.ascontiguousarray(xT_full[C]),
            "cT": cT,
            "pos": np.ascontiguousarray(positions.astype(np.int32).reshape(1, S)),
            "invf": invf,
            "w_ada": np.stack([kchunk(w_ada[l][:, cols]) for l in range(L)]),
            "b_ada": np.stack([np.ascontiguousarray(b_ada[l][cols].reshape(3, 128).T) for l in range(L)]),
            "norm_g": np.stack([norm_g[l][C].reshape(128, 1) for l in range(L)]),
            "w_in": np.stack([kchunk(w_in[l][:, wsel]) for l in range(L)]),
            "qng": np.stack([np.ascontiguousarray(q_norm_g[l].reshape(2, 128).T) for l in range(L)]),
            "kvng": np.stack([kv_norm_g[l].reshape(128, 1) for l in range(L)]),
            "w_uq": np.stack([np.ascontiguousarray(w_uq[l][:, uq_cols].reshape(2, 128, 192).transpose(1, 0, 2)) for l in range(L)]),
            "w_ukv": np.stack([np.ascontiguousarray(w_ukv[l][:, ukv_cols]) for l in range(L)]),
            "w_out": np.stack([np.ascontiguousarray(np.concatenate([w_out_b[l][:, C].reshape(8, 128, 128), w_out_a[l][:, C].reshape(4, 128, 128)], 0).transpose(1, 0, 2))
                               for l in range(L)]),
            "w_o": np.stack([kchunk(w_o[l][:, C]) for l in range(L)]),
            "fin_g": final_norm_g[C].reshape(128, 1).astype(f32),
            "rel": np.ascontiguousarray(rel_bias[:, [g * 8 + j for g in range(3)]]),
            "oh5": oh5, "m5": m5, "tri": tri,
        }
        maps.append({k: (v if v.dtype != np.float64 else v.astype(f32)) for k, v in m.items()})
    return maps


_CACHE = {}


def kernel(x, c, positions, w_ada, b_ada, norm_g, w_in, q_norm_g, w_uq, kv_norm_g, w_ukv,
           w_out_a, w_out_b, w_o, rel_bias, final_norm_g, _debug=False, _stop=None):
    args = [np.asarray(a) for a in (x, c, positions, w_ada, b_ada, norm_g, w_in, q_norm_g, w_uq, kv_norm_g, w_ukv,
                                    w_out_a, w_out_b, w_o, rel_bias, final_norm_g)]
    S = args[0].shape[1]
    L = args[6].shape[0]
    maps = host_inputs(*args)
    key = (S, L, _debug, _stop)
    if key not in _CACHE:
        _CACHE[key] = build(S, L, _debug, _stop)
    nc = _CACHE[key]
    res = run_bass_kernel_spmd(nc, maps, core_ids=list(range(NCORE)))
    outT = np.concatenate([res.results[j]["outT"] for j in range(NCORE)], axis=0)
    out = np.ascontiguousarray(outT.T).reshape(1, S, D).astype(np.float32)
    if _debug:
        return out, res
    return out
```

```python
import math
import numpy as np
import ml_dtypes
import concourse.bass as bass
import concourse.mybir as mybir
from concourse.bass_utils import run_bass_kernel_spmd

F32 = mybir.dt.float32
BF16 = mybir.dt.bfloat16
I32 = mybir.dt.int32
AF = mybir.ActivationFunctionType
ALU = mybir.AluOpType

D = 1024
NCORE = 8
NSLOT = 8
EPS = 1e-6
NW = 1440
TWO_PI = 2.0 * math.pi
import os as _os
PARTS = set(_os.environ.get('PARTS', 'dil,gates,lat,krope,q,k,vm,vd').split(','))
PSTOP = _os.environ.get('PSTOP')
DSTOP = _os.environ.get('DSTOP')


def I(method, *args, **kw):
    return lambda e: getattr(e, method)(*args, **kw)


class Prog:
    STREAMS = ["pe", "act", "dve", "pool", "sp"]

    def __init__(self):
        self.ops = []
        self.lastw = {}
        self.readers = {}
        self.dma_n = {}
        self.slot_last = {}
        self.bar = {}
        self.last_in_class = {}

    def _cls(self, o):
        if o["dma"]:
            return (o["stream"], o["slot"])
        if o["cc"]:
            return "cc"
        return o["stream"]

    def op(self, stream, fn, r=(), w=(), dma=False, cc=False):
        idx = len(self.ops)
        deps = set(self.bar.values())
        for k in r:
            if k in self.lastw:
                deps.add(self.lastw[k])
        for k in w:
            if k in self.lastw:
                deps.add(self.lastw[k])
            deps.update(self.readers.get(k, {}).values())
        o = dict(stream=stream, fn=fn, deps=deps, dma=dma, cc=cc, needed=False, idx=idx)
        if dma:
            n = self.dma_n.get(stream, 0)
            self.dma_n[stream] = n + 1
            o["slot"] = n % NSLOT
            o["slotn"] = n // NSLOT + 1
            prev = self.slot_last.get((stream, o["slot"]))
            if prev is not None:
                deps.add(prev)
            self.slot_last[(stream, o["slot"])] = idx
        c = self._cls(o)
        o["cls"] = c
        for k in w:
            self.lastw[k] = idx
            self.readers[k] = {}
        for k in r:
            self.readers.setdefault(k, {})[c] = idx
        self.last_in_class[c] = idx
        self.ops.append(o)
        return idx

    def barrier(self):
        self.bar = dict(self.last_in_class)
        self.lastw = {}
        self.readers = {}

    def emit(self, nc, block, sems):
        ops = self.ops
        for o in ops:
            for d in o["deps"]:
                ops[d]["needed"] = True
        cnt = {}
        for o in ops:
            if o["dma"]:
                o["val"] = 16 * o["slotn"]
            elif o["needed"]:
                cnt[o["cls"]] = cnt.get(o["cls"], 0) + 1
                o["val"] = cnt[o["cls"]]

        def run(stream, eng):
            waited = {}
            for o in ops:
                if o["stream"] != stream:
                    continue
                need = {}
                for d in o["deps"]:
                    od = ops[d]
                    if od["cls"] == "pe" and stream == "pe" and not o["dma"]:
                        continue
                    c = od["cls"]
                    need[c] = max(need.get(c, 0), od["val"])
                for c, v in need.items():
                    if waited.get(c, 0) < v:
                        eng.wait_ge(sems[c], v)
                        waited[c] = v
                ins = o["fn"](eng)
                if ins is not None and (o["dma"] or o["needed"]):
                    ins.then_inc(sems[o["cls"]], 16 if o["dma"] else 1)

        @block.tensor
        def _(e):
            run("pe", e)

        @block.scalar
        def _(e):
            run("act", e)

        @block.vector
        def _(e):
            run("dve", e)

        @block.gpsimd
        def _(e):
            run("pool", e)

        @block.sync
        def _(e):
            run("sp", e)


SEG_IO = {"ss0": (("xT",), ("ss_send",)), "norm": (("xT", "ss_all"), ("hT_send",)), "attn": (("hT_all",), ("y_send", "gmg")),
          "mrg": (("y_all", "gmg"), ("mg_send",)), "xnew": (("mg_all", "xT"), ("xcur", "ss_send")), "final": (("xT", "ss_all"), ("outT",))}
SEG_W = {"ss0": (), "norm": (), "attn": ("w_in", "w_uq", "w_ukv"), "mrg": ("w_out",), "xnew": ("w_o",), "final": ()}
COMMON = ("cT", "pos", "invf", "fin_g", "rel", "oh5", "m5", "tri", "w_ada", "b_ada", "norm_g", "qng", "kvng")
FUSED = False


class Arena:
    def __init__(self, nc, limit=229344):
        self.nc = nc
        self.off = 16512
        self.limit = limit
        self.n = 0

    def alloc(self, shape, dt):
        esz = 4 if dt in (F32, I32) else 2
        per = esz
        for s in shape[1:]:
            per *= s
        per = (per + 31) // 32 * 32
        assert self.off + per <= self.limit, ("SBUF overflow", self.off, per)
        self.n += 1
        t = self.nc.alloc_sbuf_tensor_at("sb%d" % self.n, list(shape), dt, offset=self.off)
        self.off += per
        return t


def build(S, L=2, debug=False, stop=None, seg=None):
    NT = S // 512
    NB = S // 128
    NCH = S // 2048
    SQ = S // 4
    nc = bass.Bass("TRN2", target_bir_lowering=False)
    P = Prog()

    ext_in = set() if seg is None else set(SEG_IO[seg[0]][0]) | set(SEG_W[seg[0]]) | set(COMMON)
    ext_out = set() if seg is None else set(SEG_IO[seg[0]][1])

    def want(kind, l_):
        return seg is None or (seg[0] == kind and (kind in ("ss0", "final") or seg[1] == l_))

    def din(name, shape, dt):
        if seg is not None and name not in ext_in:
            return nc.dram_tensor(name, shape, dt)
        return nc.dram_tensor(name, shape, dt, kind="ExternalInput")

    xT = din("xT", [128, S], F32)
    cT = din("cT", [128, 8], F32)
    pos = din("pos", [1, S], I32)
    invf = din("invf", [128, 1], F32)
    w_ada = din("w_ada", [L, 128, 8, 384], F32)
    b_ada = din("b_ada", [L, 128, 3], F32)
    norm_g = din("norm_g", [L, 128, 1], F32)
    w_in = din("w_in", [L, 128, 8, NW], F32)
    qng = din("qng", [L, 128, 2], F32)
    kvng = din("kvng", [L, 128, 1], F32)
    w_uq = din("w_uq", [L, 128, 2, 192], F32)
    w_ukv = din("w_ukv", [L, 128, 256], F32)
    w_out = din("w_out", [L, 128, 12, 128], F32)
    w_o = din("w_o", [L, 128, 8, 128], F32)
    fin_g = din("fin_g", [128, 1], F32)
    rel = din("rel", [32, 3], F32)
    oh5 = din("oh5", [32, 3, 2, 256], F32)
    m5 = din("m5", [128, 3, 2, 256], F32)
    tri_in = din("tri", [128, 128], BF16)
    def dscr(name, shape, dt):
        if name in ext_in:
            return nc.dram_tensor(name, shape, dt, kind="ExternalInput")
        if name in ext_out:
            return nc.dram_tensor(name, shape, dt, kind="ExternalOutput")
        return nc.dram_tensor(name, shape, dt)

    outT = nc.dram_tensor("outT", [128, S], F32, kind="ExternalOutput") if seg is None else dscr("outT", [128, S], F32)

    ss_send = dscr("ss_send", [1, S], F32)
    ss_all = dscr("ss_all", [8, S], F32)
    hT_send = dscr("hT_send", [128, S], BF16)
    hT_all = dscr("hT_all", [1024, S], BF16)
    y_send = dscr("y_send", [192, S], BF16)
    y_all = dscr("y_all", [1536, S], BF16)
    mg_send = dscr("mg_send", [128, S], BF16)
    mg_all = dscr("mg_all", [1024, S], BF16)
    xcur = dscr("xcur", [128, S], F32)
    qT = dscr("qT", [2, 96, S], BF16)
    kT = dscr("kT", [2, 96, S], BF16)
    vm = dscr("vm", [S, 128], BF16)
    dqk = dscr("dqk", [3, 128, S], BF16)
    dv = dscr("dv", [S, 192], BF16)
    gaz = dscr("gaz", [64, S], BF16)
    gmz = dscr("gmz", [128, S], BF16)
    gmg = dscr("gmg", [2, 128, S], BF16)
    cs = dscr("cs", [2, 32, S], F32)
    t5w = dscr("t5w", [3, 2, 128, 256], F32)
    dbg = {}
    if debug:
        for nm, shp, dt in (("dbg_hT", [128, S], BF16), ("dbg_y", [192, S], BF16), ("dbg_x", [128, S], F32),
                            ("dbg_q", [2, 96, S], BF16), ("dbg_k", [2, 96, S], BF16), ("dbg_dqk", [3, 128, S], BF16),
                            ("dbg_dv", [S, 192], BF16), ("dbg_T5", [128, 768], F32), ("dbg_gaz", [64, S], BF16),
                            ("dbg_hT2", [128, S], BF16), ("dbg_y2", [192, S], BF16), ("dbg_x2", [128, S], F32), ("dbg_mg2", [128, S], BF16)):
            dbg[nm] = nc.dram_tensor(nm, shp, dt, kind="ExternalOutput")

    def allk(name):
        return [(name, t) for t in range(NT)]

    A = Arena(nc)
    ps = [nc.alloc_psum_tensor("psb%d" % i, [128, 512], F32) for i in range(8)]

    def PSK(i):
        return ("ps", i)

    ones_f = A.alloc([128, 128], F32)
    tri = A.alloc([128, 128], BF16)
    T5 = A.alloc([128, 3, 2, 128], F32)
    modc = A.alloc([128, 8], F32)
    sc = A.alloc([128, 8], F32)
    small = A.alloc([128, 16], F32)
    base_off = A.off

    def dma(stream, out, in_, r, w):
        P.op(stream, I("dma_start", out=out, in_=in_), r=r, w=w, dma=True)

    def ag(send, recv, name_s, name_r):
        if seg is not None:
            return
        P.op("pool", I("collective_compute", "AllGather", ALU.bypass, replica_groups=[list(range(NCORE))],
                                                     ins=[send.ap()], outs=[recv.ap()]),
             r=allk(name_s), w=allk(name_r), cc=True)

    P.op("dve", I("memset", ones_f[:], 1.0), w=["ones"])
    dma("sp", tri[:], tri_in.ap(), [], ["tri"])
    dma("sp", small[:, 4:5], fin_g.ap(), [], ["small4"])
    dma("sp", small[:, 5:6], invf.ap(), [], ["small5"])
    ct = A.alloc([128, 8], F32)
    dma("sp", ct[:], cT.ap(), [], ["ct"])
    sg = A.alloc([128, 8], F32)
    P.op("act", I("activation", out=sg[:], in_=ct[:], func=AF.Sigmoid), r=["ct"], w=["sg"])
    P.op("dve", I("tensor_tensor", out=sc[:], in0=ct[:], in1=sg[:], op=ALU.mult), r=["ct", "sg"], w=["sc"])

    relt = A.alloc([32, 3], F32)
    oh = A.alloc([32, 3, 2, 256], F32)
    m5t = A.alloc([128, 3, 2, 256], F32)
    relbc = A.alloc([32, 3, 128], F32)
    f5 = A.alloc([128, 3, 2, 256], F32)
    dma("sp", relt[:], rel.ap(), [], ["relt"])
    dma("sp", oh[:], oh5.ap(), [], ["oh"])
    dma("sp", m5t[:], m5.ap(), [], ["m5t"])
    for g in range(3):
        P.op("dve", I("tensor_scalar", out=relbc[:, g, :], in0=ones_f[0:32, :], scalar1=relt[:, g:g + 1],
                                                   scalar2=None, op0=ALU.mult), r=["ones", "relt"], w=[("relbc", g)])
        for kd in range(2):
            b = (g * 2 + kd) % 2
            P.op("pe", I("matmul", ps[b][:, 0:256], relbc[:, g, :], oh[:, g, kd, :], start=True, stop=True),
                 r=[("relbc", g), "oh"], w=[PSK(b)])
            P.op("act", I("activation", out=f5[:, g, kd, :], in_=ps[b][:, 0:256], func=AF.Exp),
                 r=[PSK(b)], w=[("f5", g, kd)])
            P.op("dve", I("tensor_tensor", out=f5[:, g, kd, :], in0=f5[:, g, kd, :], in1=m5t[:, g, kd, :], op=ALU.mult),
                 r=[("f5", g, kd), "m5t"], w=[("f5", g, kd)])
            dma("sp", t5w.ap()[g, kd], f5[:, g, kd, :], [("f5", g, kd)], [("t5w", g, kd)])
            src = bass.AP(tensor=t5w, offset=(g * 2 + kd) * 128 * 256, ap=[[255, 128], [1, 128]])
            dma("sp", T5[:, g, kd, :], src, [("t5w", g, kd)], [("T5", g, kd)])

    posi = A.alloc([128, SQ], I32)
    ang = A.alloc([128, SQ], F32)
    kf = A.alloc([128, SQ], F32)
    rr = A.alloc([128, SQ], F32)
    tmp = A.alloc([128, SQ], F32)
    ki = posi
    for q in range(4):
        src = bass.AP(tensor=pos, offset=q * SQ, ap=[[0, 32], [1, SQ]])
        dma("sp", posi[32 * q:32 * q + 32, :], src, [], [("posi", q)])
    pk = [("posi", q) for q in range(4)]
    P.op("dve", I("tensor_copy", out=ang[:], in_=posi[:]), r=pk, w=["ang"])
    P.op("dve", I("tensor_scalar", out=ang[:], in0=ang[:], scalar1=small[:, 5:6], scalar2=None, op0=ALU.mult),
         r=["ang", "small5"], w=["ang"])
    P.op("dve", I("tensor_scalar", out=kf[:], in0=ang[:], scalar1=1.0 / TWO_PI, scalar2=None, op0=ALU.mult), r=["ang"], w=["kf"])
    P.op("dve", I("tensor_copy", out=ki[:], in_=kf[:]), r=["kf"] + pk, w=["ki"])
    P.op("dve", I("tensor_copy", out=kf[:], in_=ki[:]), r=["ki"], w=["kf"])
    C1 = 6.28125
    C2 = TWO_PI - C1
    P.op("dve", I("scalar_tensor_tensor", out=rr[:], in0=kf[:], scalar=-C1, in1=ang[:], op0=ALU.mult, op1=ALU.add),
         r=["kf", "ang"], w=["rr"])
    P.op("dve", I("scalar_tensor_tensor", out=rr[:], in0=kf[:], scalar=-C2, in1=rr[:], op0=ALU.mult, op1=ALU.add),
         r=["kf", "rr"], w=["rr"])

    def fold(buf, key):
        P.op("dve", I("tensor_scalar", out=tmp[:], in0=buf[:], scalar1=math.pi, scalar2=-TWO_PI, op0=ALU.is_gt, op1=ALU.mult),
             r=[key], w=["tmp"])
        P.op("dve", I("tensor_tensor", out=buf[:], in0=buf[:], in1=tmp[:], op=ALU.add), r=[key, "tmp"], w=[key])
        P.op("dve", I("tensor_scalar", out=tmp[:], in0=buf[:], scalar1=-math.pi, scalar2=TWO_PI, op0=ALU.is_lt, op1=ALU.mult),
             r=[key], w=["tmp"])
        P.op("dve", I("tensor_tensor", out=buf[:], in0=buf[:], in1=tmp[:], op=ALU.add), r=[key, "tmp"], w=[key])
        P.op("dve", I("tensor_scalar", out=buf[:], in0=buf[:], scalar1=math.pi, scalar2=-math.pi, op0=ALU.min, op1=ALU.max),
             r=[key], w=[key])

    fold(rr, "rr")
    P.op("act", I("activation", out=kf[:], in_=rr[:], func=AF.Sin), r=["rr"], w=["kf"])
    for q in range(4):
        dma("sp", cs.ap()[1, :, q * SQ:(q + 1) * SQ], kf[32 * q:32 * q + 32, :], ["kf"], [("cs", 1, q)])
    P.op("dve", I("tensor_scalar", out=rr[:], in0=rr[:], scalar1=math.pi / 2, scalar2=None, op0=ALU.add), r=["rr"], w=["rr"])
    fold(rr, "rr")
    P.op("act", I("activation", out=ang[:], in_=rr[:], func=AF.Sin), r=["rr"], w=["ang"])
    for q in range(4):
        dma("sp", cs.ap()[0, :, q * SQ:(q + 1) * SQ], ang[32 * q:32 * q + 32, :], ["ang"], [("cs", 0, q)])
    P.barrier()
    A.off = base_off
    STOPPED = [False]

    def chk(name):
        if stop == name:
            STOPPED[0] = True
        return STOPPED[0]

    def sumsq_seg(src_dram, src_name):
        m0 = A.off
        xt = [A.alloc([128, 512], F32) for _ in range(2)]
        sq = [A.alloc([128, 512], F32) for _ in range(2)]
        row = [A.alloc([1, 512], F32) for _ in range(2)]
        for t in range(NT):
            b = t % 2
            sl = slice(t * 512, (t + 1) * 512)
            dma("sp", xt[b][:], src_dram.ap()[:, sl], [(src_name, t)], [("xt", b)])
            P.op("dve", I("tensor_tensor", out=sq[b][:], in0=xt[b][:], in1=xt[b][:], op=ALU.mult), r=[("xt", b)], w=[("sq", b)])
            P.op("pe", I("matmul", ps[b][0:1, :], ones_f[:, 0:1], sq[b][:], start=True, stop=True), r=[("sq", b), "ones"], w=[PSK(b)])
            P.op("act", I("activation", out=row[b][:], in_=ps[b][0:1, :], func=AF.Copy), r=[PSK(b)], w=[("row", b)])
            dma("sp", ss_send.ap()[:, sl], row[b][:], [("row", b)], [("ss_send", t)])
        ag(ss_send, ss_all, "ss_send", "ss_all")
        P.barrier()
        A.off = m0

    def norm_seg(src_dram, src_name, dst_dram, dst_name, dst_dt, mulcol, addcol):
        m0 = A.off
        xt = [A.alloc([128, 512], F32) for _ in range(2)]
        s8 = [A.alloc([8, 512], F32) for _ in range(2)]
        rs = [A.alloc([128, 512], F32) for _ in range(2)]
        ot = [A.alloc([128, 512], dst_dt) for _ in range(2)]
        for t in range(NT):
            b = t % 2
            sl = slice(t * 512, (t + 1) * 512)
            dma("sp", xt[b][:], src_dram.ap()[:, sl], [(src_name, t)], [("xt", b)])
            dma("sp", s8[b][:], ss_all.ap()[:, sl], [("ss_all", t)], [("s8", b)])
            P.op("pe", I("matmul", ps[b][:, :], ones_f[0:8, :], s8[b][:], start=True, stop=True), r=[("s8", b), "ones"], w=[PSK(b)])
            P.op("dve", I("tensor_scalar", out=rs[b][:], in0=ps[b][:, :], scalar1=1.0 / D, scalar2=EPS, op0=ALU.mult, op1=ALU.add),
                 r=[PSK(b)], w=[("rs", b)])
            P.op("act", I("activation", out=rs[b][:], in_=rs[b][:], func=AF.Sqrt), r=[("rs", b)], w=[("rs", b)])
            P.op("dve", I("reciprocal", out=rs[b][:], in_=rs[b][:]), r=[("rs", b)], w=[("rs", b)])
            P.op("dve", I("tensor_tensor", out=xt[b][:], in0=xt[b][:], in1=rs[b][:], op=ALU.mult), r=[("xt", b), ("rs", b)], w=[("xt", b)])
            if addcol is None:
                P.op("dve", I("tensor_scalar", out=ot[b][:], in0=xt[b][:], scalar1=mulcol, scalar2=None, op0=ALU.mult),
                     r=[("xt", b), "modc", "small4"], w=[("ot", b)])
            else:
                P.op("dve", I("tensor_scalar", out=ot[b][:], in0=xt[b][:], scalar1=mulcol, scalar2=addcol, op0=ALU.mult, op1=ALU.add),
                     r=[("xt", b), "modc"], w=[("ot", b)])
            dma("sp", dst_dram.ap()[:, sl], ot[b][:], [("ot", b)], [(dst_name, t)])
        P.barrier()
        A.off = m0

    def load_cast(dst, src_ap, shape, key, parts=128):
        m0 = A.off
        st = A.alloc(shape, F32)
        dma("sp", st[0:parts], src_ap, [], [("stg", key)])
        P.op("pool", I("tensor_copy", out=dst, in_=st[0:parts]), r=[("stg", key)], w=[key])
        return m0

    x_dram, x_name = xT, "xT"
    if want("ss0", 0) and not chk("init"):
        sumsq_seg(x_dram, x_name)
    for l in range(L):
        if seg is not None and (seg[0] in ("ss0", "final") or seg[1] != l):
            continue
        if chk("sumsq"):
            break
        m0 = A.off
        wa = A.alloc([128, 8, 384], F32)
        ba = A.alloc([128, 3], F32)
        dma("sp", wa[:], w_ada.ap()[l], [], ["wa"])
        dma("sp", ba[:], b_ada.ap()[l], [], ["ba"])
        dma("sp", small[:, 0:1], norm_g.ap()[l], [], ["small0"])
        dma("sp", small[:, 1:3], qng.ap()[l], [], ["small1"])
        dma("sp", small[:, 3:4], kvng.ap()[l], [], ["small3"])
        for grp in range(3):
            for kc in range(8):
                P.op("pe", I("matmul", ps[0][:, grp:grp + 1], wa[:, kc, grp * 128:(grp + 1) * 128], sc[:, kc:kc + 1],
                                                              start=(kc == 0), stop=(kc == 7)), r=["wa", "sc"], w=[PSK(0)])
        P.op("dve", I("tensor_tensor", out=modc[:, 0:3], in0=ps[0][:, 0:3], in1=ba[:], op=ALU.add), r=[PSK(0), "ba"], w=["modc"])
        P.op("dve", I("scalar_tensor_tensor", out=modc[:, 3:4], in0=modc[:, 1:2], scalar=1.0, in1=small[:, 0:1], op0=ALU.add, op1=ALU.mult),
             r=["modc", "small0"], w=["modc"])
        P.barrier()
        A.off = m0
        if want("norm", l):
            norm_seg(x_dram, x_name, hT_send, "hT_send", BF16, modc[:, 3:4], modc[:, 0:1])
            ag(hT_send, hT_all, "hT_send", "hT_all")
            if debug and l == 0:
                dma("sp", dbg["dbg_hT"].ap(), hT_send.ap(), allk("hT_send"), [])
            if debug and l == 1:
                dma("sp", dbg["dbg_hT2"].ap(), hT_send.ap(), allk("hT_send"), [])
            P.barrier()
            if chk("norm"):
                break
        if want("attn", l):
            proj_seg(nc, P, A, ps, locals())
            if chk("proj"):
                break
            mla_seg(nc, P, A, ps, locals())
            if debug and l == 0 and stop == "mla":
                dma("sp", dbg["dbg_y"].ap(), y_send.ap(), [], [])
                dma("sp", dbg["dbg_q"].ap(), qT.ap(), [], [])
                dma("sp", dbg["dbg_k"].ap(), kT.ap(), [], [])
                P.barrier()
            if chk("mla"):
                break
            dil_seg(nc, P, A, ps, locals())
            if chk("dil"):
                break
            ag(y_send, y_all, "y_send", "y_all")
            if debug and l == 0:
                dma("sp", dbg["dbg_y"].ap(), y_send.ap(), allk("y_send"), [])
                dma("sp", dbg["dbg_q"].ap(), qT.ap(), allk("qT0") + allk("qT1"), [])
                dma("sp", dbg["dbg_k"].ap(), kT.ap(), allk("kT0") + allk("kT1"), [])
                dma("sp", dbg["dbg_dqk"].ap(), dqk.ap(), [], [])
                dma("sp", dbg["dbg_dv"].ap(), dv.ap(), [], [])
                dma("sp", dbg["dbg_gaz"].ap(), gaz.ap(), [], [])
                dma("sp", dbg["dbg_T5"].ap(), T5[:].rearrange("p a b c -> p (a b c)"), [], [])
            if debug and l == 1:
                dma("sp", dbg["dbg_y2"].ap(), y_send.ap(), allk("y_send"), [])
            P.barrier()
        if want("mrg", l):
            m0 = A.off
            wo = A.alloc([128, 12, 128], BF16)
            load_cast(wo[:], w_out.ap()[l], [128, 12, 128], "wo")
            ym = [A.alloc([128, 8, 512], BF16) for _ in range(2)]
            yd = [A.alloc([128, 4, 512], BF16) for _ in range(2)]
            gg = [A.alloc([128, 2, 512], BF16) for _ in range(2)]
            t1 = [A.alloc([128, 512], F32) for _ in range(2)]
            t2 = [A.alloc([128, 512], F32) for _ in range(2)]
            mo = [A.alloc([128, 512], BF16) for _ in range(2)]
            for t in range(NT):
                b = t % 2
                sl = slice(t * 512, (t + 1) * 512)
                src = bass.AP(tensor=y_all, offset=t * 512, ap=[[S, 128], [192 * S, 8], [1, 512]])
                dma("sp", ym[b][:], src, [("y_all", t)], [("ym", b)])
                for hf in range(2):
                    src = bass.AP(tensor=y_all, offset=(hf * 192 + 128) * S + t * 512, ap=[[S, 64], [384 * S, 4], [1, 512]])
                    dma("sp", yd[b][64 * hf:64 * hf + 64], src, [("y_all", t)], [("yd", b, hf)])
                src = bass.AP(tensor=gmg, offset=t * 512, ap=[[S, 128], [128 * S, 2], [1, 512]])
                dma("sp", gg[b][:], src, [("gmg", t)], [("gg", b)])
                pa, pb = 2 * b, 2 * b + 1
                for c in range(4):
                    P.op("pe", I("matmul", ps[pa][:, :], wo[:, 8 + c, :], yd[b][:, c, :], start=(c == 0), stop=(c == 3)),
                         r=["wo", ("yd", b, 0), ("yd", b, 1)], w=[PSK(pa)])
                for r_ in range(8):
                    P.op("pe", I("matmul", ps[pb][:, :], wo[:, r_, :], ym[b][:, r_, :], start=(r_ == 0), stop=(r_ == 7)),
                         r=["wo", ("ym", b)], w=[PSK(pb)])
                P.op("dve", I("tensor_tensor", out=t1[b][:], in0=ps[pa][:, :], in1=gg[b][:, 0, :], op=ALU.mult),
                     r=[PSK(pa), ("gg", b)], w=[("t1", b)])
                P.op("dve", I("tensor_tensor", out=t2[b][:], in0=ps[pb][:, :], in1=gg[b][:, 1, :], op=ALU.mult),
                     r=[PSK(pb), ("gg", b)], w=[("t2", b)])
                P.op("pool", I("tensor_tensor", out=mo[b][:], in0=t1[b][:], in1=t2[b][:], op=ALU.add),
                     r=[("t1", b), ("t2", b)], w=[("mo", b)])
                dma("sp", mg_send.ap()[:, sl], mo[b][:], [("mo", b)], [("mg_send", t)])
            ag(mg_send, mg_all, "mg_send", "mg_all")
            P.barrier()
            A.off = m0
        if want("xnew", l):
            m0 = A.off
            wot = A.alloc([128, 8, 128], BF16)
            load_cast(wot[:], w_o.ap()[l], [128, 8, 128], "wot")
            mt = [A.alloc([128, 8, 512], BF16) for _ in range(2)]
            xt = [A.alloc([128, 512], F32) for _ in range(2)]
            xn = [A.alloc([128, 512], F32) for _ in range(2)]
            sq = [A.alloc([128, 512], F32) for _ in range(2)]
            row = [A.alloc([1, 512], F32) for _ in range(2)]
            for t in range(NT):
                b = t % 2
                sl = slice(t * 512, (t + 1) * 512)
                src = bass.AP(tensor=mg_all, offset=t * 512, ap=[[S, 128], [128 * S, 8], [1, 512]])
                dma("sp", mt[b][:], src, [("mg_all", t)], [("mt", b)])
                dma("sp", xt[b][:], x_dram.ap()[:, sl], [(x_name, t)], [("xt", b)])
                pa, pb = 2 * b, 2 * b + 1
                for kc in range(8):
                    P.op("pe", I("matmul", ps[pa][:, :], wot[:, kc, :], mt[b][:, kc, :], start=(kc == 0), stop=(kc == 7)),
                         r=["wot", ("mt", b)], w=[PSK(pa)])
                P.op("dve", I("scalar_tensor_tensor", out=xn[b][:], in0=ps[pa][:, :], scalar=modc[:, 2:3], in1=xt[b][:],
                                                                         op0=ALU.mult, op1=ALU.add), r=[PSK(pa), ("xt", b), "modc"], w=[("xn", b)])
                dma("sp", xcur.ap()[:, sl], xn[b][:], [("xn", b)], [("xcur", t)])
                P.op("pool", I("tensor_tensor", out=sq[b][:], in0=xn[b][:], in1=xn[b][:], op=ALU.mult), r=[("xn", b)], w=[("sq", b)])
                P.op("pe", I("matmul", ps[pb][0:1, :], ones_f[:, 0:1], sq[b][:], start=True, stop=True), r=[("sq", b), "ones"], w=[PSK(pb)])
                P.op("act", I("activation", out=row[b][:], in_=ps[pb][0:1, :], func=AF.Copy), r=[PSK(pb)], w=[("row", b)])
                dma("sp", ss_send.ap()[:, sl], row[b][:], [("row", b)], [("ss_send", t)])
            ag(ss_send, ss_all, "ss_send", "ss_all")
            P.barrier()
            A.off = m0
        if seg is None:
            x_dram, x_name = xcur, "xcur"
        if debug and l == 0:
            dma("sp", dbg["dbg_x"].ap(), xcur.ap(), allk("xcur"), [])
            P.barrier()
        if debug and l == 1:
            dma("sp", dbg["dbg_x2"].ap(), xcur.ap(), allk("xcur"), [])
            dma("sp", dbg["dbg_mg2"].ap(), mg_send.ap(), [], [])
            P.barrier()
    if want("final", 0) and not STOPPED[0]:
        norm_seg(x_dram, x_name, outT, "outT", F32, small[:, 4:5], None)
    P.barrier()
    P.op("sp", lambda e: None, r=[])

    sem_names = ["pe", "act", "dve", "pool", "sp", "cc"] + [("sp", i) for i in range(NSLOT)] + [("pool", i) for i in range(NSLOT)]
    sems = {}
    import contextlib
    with contextlib.ExitStack() as stk:
        for i, nm in enumerate(sem_names):
            sems[nm] = stk.enter_context(nc.semaphore("s%d" % i))
        block = stk.enter_context(nc.Block())
        P.emit(nc, block, sems)
    return nc


def proj_seg(nc, P, A, ps, env):
    S, NT, l = env["S"], env["NT"], env["l"]
    dma, load_cast = env["dma"], env["load_cast"]
    small, ones_f = env["small"], env["ones_f"]
    hT_all, w_in, w_uq, w_ukv, cs = env["hT_all"], env["w_in"], env["w_uq"], env["w_ukv"], env["cs"]
    qT, kT, vm, dqk, dv, gaz, gmz, gmg = (env[k] for k in ("qT", "kT", "vm", "dqk", "dv", "gaz", "gmz", "gmg"))
    PSK = env["PSK"]
    m0 = A.off
    win = A.alloc([128, 8, NW + 32], BF16)
    for pc in range(4):
        mm = A.off
        st = A.alloc([128, 8, 360], F32)
        dma("sp", st[:], w_in.ap()[l][:, :, pc * 360:(pc + 1) * 360], [], [("wst", pc)])
        P.op("pool" if pc % 2 else "dve", I("tensor_copy", out=win[:, :, pc * 360:(pc + 1) * 360], in_=st[:]),
             r=[("wst", pc)], w=[("win", pc)])
    winK = [("win", pc) for pc in range(4)]
    P.op("dve", I("tensor_scalar", out=win[:, :, NW:NW + 16], in0=win[:, :, 400:416], scalar1=-1.0, scalar2=None, op0=ALU.mult),
         r=winK, w=["winrot"])
    P.op("dve", I("tensor_copy", out=win[:, :, NW + 16:NW + 32], in_=win[:, :, 384:400]), r=winK, w=["winrot2"])
    winK = winK + ["winrot", "winrot2"]
    wuq = A.alloc([128, 2, 192 + 64], BF16)
    st = A.alloc([128, 2, 192], F32)
    dma("sp", st[:], w_uq.ap()[l], [], ["wuqs"])
    P.op("dve", I("tensor_copy", out=wuq[:, :, 0:192], in_=st[:]), r=["wuqs"], w=["wuq"])
    for h in range(2):
        P.op("dve", I("tensor_scalar", out=wuq[:, :, 192 + 32 * h:192 + 32 * h + 16], in0=st[:, :, 96 * h + 16:96 * h + 32],
                                                   scalar1=-1.0, scalar2=None, op0=ALU.mult), r=["wuqs"], w=[("wuqr", h)])
        P.op("dve", I("tensor_copy", out=wuq[:, :, 192 + 32 * h + 16:192 + 32 * h + 32], in_=st[:, :, 96 * h:96 * h + 16]),
             r=["wuqs"], w=[("wuqr2", h)])
    wuqK = ["wuq"] + [("wuqr", h) for h in range(2)] + [("wuqr2", h) for h in range(2)]
    wkv = A.alloc([128, 256], BF16)
    st2 = A.alloc([128, 256], F32)
    dma("sp", st2[:], w_ukv.ap()[l], [], ["wkvs"])
    P.op("dve", I("tensor_copy", out=wkv[:], in_=st2[:]), r=["wkvs"], w=["wkv"])

    NBUF = 2
    hT = [A.alloc([128, 8, 512], BF16) for _ in range(NBUF)]
    csq = [A.alloc([32, 2, 512], F32) for _ in range(NBUF)]
    oq = [A.alloc([128, 3, 512], BF16) for _ in range(NBUF)]
    ogz = [A.alloc([128, 4, 512], BF16) for _ in range(NBUF)]
    sgt = [A.alloc([128, 512], F32) for _ in range(NBUF)]
    cq2 = [A.alloc([128, 2, 512], F32) for _ in range(NBUF)]
    rsq = [A.alloc([128, 512], F32) for _ in range(NBUF)]
    rsk = [A.alloc([128, 512], F32) for _ in range(NBUF)]
    cqn = [A.alloc([128, 2, 512], BF16) for _ in range(NBUF)]
    ckn = [A.alloc([128, 512], BF16) for _ in range(NBUF)]
    krp = [A.alloc([32, 512], BF16) for _ in range(NBUF)]
    rt1 = [A.alloc([32, 512], F32) for _ in range(NBUF)]
    rt2 = [A.alloc([32, 512], F32) for _ in range(NBUF)]
    qo = [A.alloc([96, 2, 512], BF16) for _ in range(NBUF)]
    ko = [A.alloc([64, 2, 512], BF16) for _ in range(NBUF)]
    vdo = [A.alloc([128, 4, 192], BF16) for _ in range(NBUF)]
    vmo = [A.alloc([128, 4, 128], BF16) for _ in range(NBUF)]

    CQ, CKV, KR, DQK, DV, AZ, MZ, MG = 0, 256, 384, 416, 800, 992, 1056, 1184
    pcnt = [0]

    def nps():
        pcnt[0] = (pcnt[0] + 1) % 8
        return pcnt[0]

    def mm_feat(b, pi, rows, c0, ncols=None):
        for kc in range(8):
            P.op("pe", I("matmul", ps[pi][0:rows, :], win[:, kc, c0:c0 + rows], hT[b][:, kc, :], start=(kc == 0), stop=(kc == 7)),
                 r=winK + [("hT", b)], w=[PSK(pi)])

    for t in range(NT):
        b = t % NBUF
        sl = slice(t * 512, (t + 1) * 512)
        src = bass.AP(tensor=hT_all, offset=t * 512, ap=[[S, 128], [128 * S, 8], [1, 512]])
        dma("sp", hT[b][:], src, [("hT_all", t)], [("hT", b)])
        src = bass.AP(tensor=cs, offset=t * 512, ap=[[S, 32], [32 * S, 2], [1, 512]])
        dma("sp", csq[b][:], src, [("cs", 0, q) for q in range(4)] + [("cs", 1, q) for q in range(4)], [("csq", b)])
        for g, r_ in (enumerate((1, 4, 16)) if 'dil' in PARTS else []):
            pi = nps()
            mm_feat(b, pi, 128, DQK + 128 * g)
            if r_ == 1:
                P.op("act", I("activation", out=oq[b][:, g, :], in_=ps[pi][:, :], func=AF.Copy), r=[PSK(pi)], w=[("oq", b, g)])
            else:
                nm = 512 // r_
                inv = bass.AP(tensor=ps[pi], offset=0, ap=[[512, 128], [1, r_], [r_, nm]])
                if r_ == 4:
                    outv = bass.AP(tensor=oq[b], offset=g * 512, ap=[[1536, 128], [128, 4], [1, 128]])
                else:
                    outv = None
                if r_ == 4:
                    P.op("act", I("activation", out=outv, in_=inv, func=AF.Copy), r=[PSK(pi)], w=[("oq", b, g)])
                else:
                    outv = bass.AP(tensor=oq[b], offset=g * 512, ap=[[1536, 128], [32, 16], [1, 32]])
                    P.op("act", I("activation", out=outv, in_=inv, func=AF.Copy), r=[PSK(pi)], w=[("oq", b, g)])
            if r_ == 16:
                ch, sub = t // 4, t % 4
                dst = bass.AP(tensor=dqk, offset=g * 128 * S + ch * 2048 + 32 * sub, ap=[[S, 128], [128, 16], [1, 32]])
                srcv = bass.AP(tensor=oq[b], offset=g * 512, ap=[[1536, 128], [32, 16], [1, 32]])
                dma("sp", dst, srcv, [("oq", b, g)], [("dqk", g, t)])
            else:
                dma("sp", dqk.ap()[g, :, sl], oq[b][:, g, :], [("oq", b, g)], [("dqk", g, t)])
        for gi, (c0, rows, dst, nm) in enumerate(() if 'gates' not in PARTS else ((AZ, 64, gaz.ap()[:, sl], "gaz"), (MZ, 128, gmz.ap()[:, sl], "gmz"),
                                                 (MG, 128, gmg.ap()[0, :, sl], "gmg0"), (MG + 128, 128, gmg.ap()[1, :, sl], "gmg1"))):
            pi = nps()
            mm_feat(b, pi, rows, c0)
            if gi < 2:
                P.op("act", I("activation", out=sgt[b][0:rows, :], in_=ps[pi][0:rows, :], func=AF.Sigmoid),
                     r=[PSK(pi)], w=[("sgt", b)])
                P.op("dve", I("tensor_tensor", out=ogz[b][0:rows, gi, :], in0=ps[pi][0:rows, :], in1=sgt[b][0:rows, :], op=ALU.mult),
                     r=[PSK(pi), ("sgt", b)], w=[("ogz", b, gi)])
            else:
                P.op("act", I("activation", out=ogz[b][:, gi, :], in_=ps[pi][:, :], func=AF.Sigmoid), r=[PSK(pi)], w=[("ogz", b, gi)])
            dma("sp", dst, ogz[b][0:rows, gi, :], [("ogz", b, gi)], [(nm, t)])
        if PSTOP == 'gates':
            continue
        pq = [nps(), nps()]
        for i in range(2):
            mm_feat(b, pq[i], 128, CQ + 128 * i)
            P.op("act", I("activation", out=cq2[b][:, i, :], in_=ps[pq[i]][:, :], func=AF.Square),
                 r=[PSK(pq[i])], w=[("cq2", b, i)])
        pr = nps()
        for i in range(2):
            P.op("pe", I("matmul", ps[pr][:, :], ones_f[:, :], cq2[b][:, i, :], start=(i == 0), stop=(i == 1)),
                 r=["ones", ("cq2", b, 0), ("cq2", b, 1)], w=[PSK(pr)])
        P.op("dve", I("tensor_scalar", out=rsq[b][:], in0=ps[pr][:, :], scalar1=1.0 / 256, scalar2=EPS, op0=ALU.mult, op1=ALU.add),
             r=[PSK(pr)], w=[("rsq", b)])
        P.op("act", I("activation", out=rsq[b][:], in_=rsq[b][:], func=AF.Sqrt), r=[("rsq", b)], w=[("rsq", b)])
        P.op("dve", I("reciprocal", out=rsq[b][:], in_=rsq[b][:]), r=[("rsq", b)], w=[("rsq", b)])
        for i in range(2):
            P.op("dve", I("scalar_tensor_tensor", out=cqn[b][:, i, :], in0=ps[pq[i]][:, :], scalar=small[:, 1 + i:2 + i], in1=rsq[b][:],
                                                              op0=ALU.mult, op1=ALU.mult), r=[PSK(pq[i]), ("rsq", b), "small1"], w=[("cqn", b, i)])
        pk_ = nps()
        mm_feat(b, pk_, 128, CKV)
        P.op("act", I("activation", out=cq2[b][:, 0, :], in_=ps[pk_][:, :], func=AF.Square),
             r=[PSK(pk_)], w=[("cq2", b, 0)])
        pr2 = nps()
        P.op("pe", I("matmul", ps[pr2][:, :], ones_f[:, :], cq2[b][:, 0, :], start=True, stop=True),
             r=["ones", ("cq2", b, 0)], w=[PSK(pr2)])
        P.op("dve", I("tensor_scalar", out=rsk[b][:], in0=ps[pr2][:, :], scalar1=1.0 / 128, scalar2=EPS, op0=ALU.mult, op1=ALU.add),
             r=[PSK(pr2)], w=[("rsk", b)])
        P.op("act", I("activation", out=rsk[b][:], in_=rsk[b][:], func=AF.Sqrt), r=[("rsk", b)], w=[("rsk", b)])
        P.op("dve", I("reciprocal", out=rsk[b][:], in_=rsk[b][:]), r=[("rsk", b)], w=[("rsk", b)])
        P.op("dve", I("scalar_tensor_tensor", out=ckn[b][:], in0=ps[pk_][:, :], scalar=small[:, 3:4], in1=rsk[b][:],
                                                              op0=ALU.mult, op1=ALU.mult), r=[PSK(pk_), ("rsk", b), "small3"], w=[("ckn", b)])
        if PSTOP == 'lat':
            continue
        p1, p2 = nps(), nps()
        for kc in range(8):
            P.op("pe", I("matmul", ps[p1][0:32, :], win[:, kc, KR:KR + 32], hT[b][:, kc, :], start=(kc == 0), stop=(kc == 7)),
                 r=winK + [("hT", b)], w=[PSK(p1)])
        for kc in range(8):
            P.op("pe", I("matmul", ps[p2][0:32, :], win[:, kc, NW:NW + 32], hT[b][:, kc, :], start=(kc == 0), stop=(kc == 7)),
                 r=winK + [("hT", b)], w=[PSK(p2)])
        P.op("dve", I("tensor_tensor", out=rt1[b][:], in0=ps[p1][0:32, :], in1=csq[b][:, 0, :], op=ALU.mult), r=[PSK(p1), ("csq", b)], w=[("rt1", b)])
        P.op("dve", I("tensor_tensor", out=rt2[b][:], in0=ps[p2][0:32, :], in1=csq[b][:, 1, :], op=ALU.mult), r=[PSK(p2), ("csq", b)], w=[("rt2", b)])
        P.op("dve", I("tensor_tensor", out=krp[b][:], in0=rt1[b][:], in1=rt2[b][:], op=ALU.add), r=[("rt1", b), ("rt2", b)], w=[("krp", b)])
        for h in range(2):
            dma("sp", kT.ap()[h, 0:32, sl], krp[b][:], [("krp", b)], [("kT%d" % h, t, "r")])
        if PSTOP == 'krope':
            continue
        for h in range(2):
            pa_, pb_ = nps(), nps()
            for i in range(2):
                P.op("pe", I("matmul", ps[pa_][0:96, :], wuq[:, i, 96 * h:96 * h + 96], cqn[b][:, i, :], start=(i == 0), stop=(i == 1)),
                     r=wuqK + [("cqn", b, 0), ("cqn", b, 1)], w=[PSK(pa_)])
            for i in range(2):
                P.op("pe", I("matmul", ps[pb_][0:32, :], wuq[:, i, 192 + 32 * h:192 + 32 * h + 32], cqn[b][:, i, :], start=(i == 0), stop=(i == 1)),
                     r=wuqK + [("cqn", b, 0), ("cqn", b, 1)], w=[PSK(pb_)])
            P.op("act", I("activation", out=qo[b][32:64, h, :], in_=ps[pa_][32:64, :], func=AF.Copy), r=[PSK(pa_)], w=[("qo", b, h, "n")])
            P.op("act", I("activation", out=qo[b][64:96, h, :], in_=ps[pa_][64:96, :], func=AF.Copy), r=[PSK(pa_)], w=[("qo", b, h, "n2")])
            P.op("dve", I("tensor_tensor", out=rt1[b][:], in0=ps[pa_][0:32, :], in1=csq[b][:, 0, :], op=ALU.mult),
                 r=[PSK(pa_), ("csq", b)], w=[("rt1", b)])
            P.op("dve", I("tensor_tensor", out=rt2[b][:], in0=ps[pb_][0:32, :], in1=csq[b][:, 1, :], op=ALU.mult),
                 r=[PSK(pb_), ("csq", b)], w=[("rt2", b)])
            P.op("pool", I("tensor_tensor", out=qo[b][0:32, h, :], in0=rt1[b][:], in1=rt2[b][:], op=ALU.add),
                 r=[("rt1", b), ("rt2", b)], w=[("qo", b, h, "r")])
            dma("sp", qT.ap()[h, :, sl], qo[b][:, h, :], [("qo", b, h, "n"), ("qo", b, h, "n2"), ("qo", b, h, "r")], [("qT%d" % h, t)])
        if PSTOP == 'q':
            continue
        for h in range(2):
            pi = nps()
            P.op("pe", I("matmul", ps[pi][0:64, :], wkv[:, 64 * h:64 * h + 64], ckn[b][:], start=True, stop=True),
                 r=["wkv", ("ckn", b)], w=[PSK(pi)])
            P.op("act", I("activation", out=ko[b][:, h, :], in_=ps[pi][0:64, :], func=AF.Copy), r=[PSK(pi)], w=[("ko", b, h)])
            dma("sp", kT.ap()[h, 32:96, sl], ko[b][:, h, :], [("ko", b, h)], [("kT%d" % h, t, "n")])
        if PSTOP == 'k':
            continue
        pi = nps()
        for s4 in range(4):
            P.op("pe", I("matmul", ps[pi][:, 128 * s4:128 * s4 + 128], ckn[b][:, 128 * s4:128 * s4 + 128], wkv[:, 128:256], start=True, stop=True),
                 r=["wkv", ("ckn", b)], w=[PSK(pi)])
        P.op("act", I("activation", out=vmo[b][:, :, :], in_=ps[pi][:, :].rearrange("p (a c) -> p a c", a=4), func=AF.Copy),
             r=[PSK(pi)], w=[("vmo", b)])
        dst = bass.AP(tensor=vm, offset=t * 512 * 128, ap=[[128, 128], [128 * 128, 4], [1, 128]])
        dma("sp", dst, vmo[b][:, :, :], [("vmo", b)], [("vm", t)])
        pi = nps()
        pj = nps()
        for s4 in range(4):
            pp = pi if s4 < 2 else pj
            o0 = 192 * (s4 % 2)
            for kc in range(8):
                P.op("pe", I("matmul", ps[pp][:, o0:o0 + 192], hT[b][:, kc, 128 * s4:128 * s4 + 128], win[:, kc, DV:DV + 192],
                                                                         start=(kc == 0), stop=(kc == 7)), r=winK + [("hT", b)], w=[PSK(pp)])
        P.op("act", I("activation", out=vdo[b][:, 0:2, :], in_=ps[pi][:, 0:384].rearrange("p (a c) -> p a c", a=2), func=AF.Copy),
             r=[PSK(pi)], w=[("vdo", b, 0)])
        P.op("act", I("activation", out=vdo[b][:, 2:4, :], in_=ps[pj][:, 0:384].rearrange("p (a c) -> p a c", a=2), func=AF.Copy),
             r=[PSK(pj)], w=[("vdo", b, 1)])
        dst = bass.AP(tensor=dv, offset=t * 512 * 192, ap=[[192, 128], [128 * 192, 4], [1, 192]])
        dma("sp", dst, vdo[b][:, :, :], [("vdo", b, 0), ("vdo", b, 1)], [("dv", t)])
    P.barrier()
    A.off = m0


def mla_seg(nc, P, A, ps, env):
    S, NT, NB = env["S"], env["NT"], env["NB"]
    dma = env["dma"]
    ones_f, tri = env["ones_f"], env["tri"]
    qT, kT, vm, gmz, y_send = env["qT"], env["kT"], env["vm"], env["gmz"], env["y_send"]
    PSK = env["PSK"]
    m0 = A.off
    scale = 96.0 ** -0.5
    KT = A.alloc([96, S], BF16)
    VA = A.alloc([128, NB, 65], BF16)
    QT = [A.alloc([96, 512], BF16) for _ in range(2)]
    GZ = [A.alloc([64, 512], BF16) for _ in range(2)]
    PT = [A.alloc([128, 512], BF16) for _ in range(4)]
    lrow = A.alloc([65, 512], F32)
    rbc = A.alloc([64, 512], F32)
    yf = A.alloc([64, 512], F32)
    yo = [A.alloc([64, 512], BF16) for _ in range(2)]
    P.op("pool", I("memset", VA[:, :, 64:65], 1.0), w=["VAones"])
    blk = 0
    for h in range(2):
        allkT = [("kT%d" % h, t, "r") for t in range(NT)] + [("kT%d" % h, t, "n") for t in range(NT)]
        dma("sp", KT[:], kT.ap()[h], allkT, ["KT"])
        for v0 in range(0, NB, 16):
            src = bass.AP(tensor=vm, offset=64 * h + v0 * 128 * 128, ap=[[128, 128], [128 * 128, 16], [1, 64]])
            dma("sp", VA[:, v0:v0 + 16, 0:64], src, [("vm", t) for t in range(NT)], [("VA", v0)])
        for qt in range(NT):
            qb = qt % 2
            sl = slice(qt * 512, (qt + 1) * 512)
            dma("sp", QT[qb][:], qT.ap()[h, :, sl], [("qT%d" % h, qt)], [("QT", qb)])
            dma("sp", GZ[qb][:], gmz.ap()[64 * h:64 * h + 64, sl], [("gmz", qt)], [("GZ", qb)])
            po = 4 + qb
            nkb = 4 * qt + 4
            for kb in range(nkb):
                d = kb - 4 * qt
                c0 = 128 * d if d > 0 else 0
                sb_ = blk % 4
                pb_ = blk % 4
                blk += 1
                P.op("pe", I("matmul", ps[pb_][:, c0:512], KT[:, kb * 128:(kb + 1) * 128], QT[qb][:, c0:512], start=True, stop=True),
                     r=["KT", ("QT", qb)], w=[PSK(pb_)])
                P.op("act", I("activation", out=PT[sb_][:, c0:512], in_=ps[pb_][:, c0:512], func=AF.Exp, scale=scale),
                     r=[PSK(pb_)], w=[("PT", sb_)])
                if d >= 0:
                    P.op("dve", I("tensor_tensor", out=PT[sb_][:, c0:c0 + 128], in0=PT[sb_][:, c0:c0 + 128], in1=tri[:], op=ALU.mult),
                         r=[("PT", sb_), "tri"], w=[("PT", sb_)])
                P.op("pe", I("matmul", ps[po][0:65, c0:512], VA[:, kb, :], PT[sb_][:, c0:512], start=(kb == 0), stop=(kb == nkb - 1)),
                     r=[("VA", kb // 16 * 16), "VAones", ("PT", sb_)], w=[PSK(po)])
            P.op("act", I("activation", out=lrow[64:65, :], in_=ps[po][64:65, :], func=AF.Copy), r=[PSK(po)], w=["lrow"])
            P.op("pe", I("matmul", ps[6][0:64, :], ones_f[64:65, 0:64], lrow[64:65, :], start=True, stop=True), r=["lrow", "ones"], w=[PSK(6)])
            P.op("dve", I("reciprocal", out=rbc[:], in_=ps[6][0:64, :]), r=[PSK(6)], w=["rbc"])
            P.op("dve", I("tensor_tensor", out=yf[:], in0=ps[po][0:64, :], in1=rbc[:], op=ALU.mult), r=[PSK(po), "rbc"], w=["yf"])
            P.op("pool", I("tensor_tensor", out=yo[qb][:], in0=yf[:], in1=GZ[qb][:], op=ALU.mult), r=["yf", ("GZ", qb)], w=[("yo", qb)])
            dma("sp", y_send.ap()[64 * h:64 * h + 64, sl], yo[qb][:], [("yo", qb)], [("y_send", qt, h)])
    P.barrier()
    A.off = m0


def dil_seg(nc, P, A, ps, env):
    S, NT, NCH = env["S"], env["NT"], env["NCH"]
    dma = env["dma"]
    ones_f, T5 = env["ones_f"], env["T5"]
    dqk, dv, gaz, y_send = env["dqk"], env["dv"], env["gaz"], env["y_send"]
    PSK = env["PSK"]
    m0 = A.off
    DQ = [A.alloc([64, 3, 2048], BF16) for _ in range(2)]
    DK = [A.alloc([64, 3, 2048], BF16) for _ in range(2)]
    DVt = [A.alloc([128, 3, 16, 65], BF16) for _ in range(2)]
    ACC = A.alloc([65, 2048], F32)
    EX = [A.alloc([128, 512], F32) for _ in range(2)]
    PTd = [A.alloc([128, 512], BF16) for _ in range(2)]
    GA = A.alloc([64, 2048], BF16)
    lrow = A.alloc([65, 512], F32)
    rbc = A.alloc([64, 512], F32)
    yf = A.alloc([64, 512], F32)
    yo = [A.alloc([64, 512], BF16) for _ in range(2)]
    for pb in range(2):
        P.op("pool", I("memset", DVt[pb][:, :, :, 64:65], 1.0), w=[("DVones", pb)])
    RS = (1, 4, 16)
    cnt = 0
    for c in range(NCH):
        pb = c % 2
        t4 = [4 * c + i for i in range(4)]
        for g in range(3):
            dma("sp", DQ[pb][:, g, :], dqk.ap()[g, 0:64, c * 2048:(c + 1) * 2048], [("dqk", g, t) for t in t4], [("DQ", pb, g)])
            dma("sp", DK[pb][:, g, :], dqk.ap()[g, 64:128, c * 2048:(c + 1) * 2048], [("dqk", g, t) for t in t4], [("DK", pb, g)])
        rdv = [("dv", t) for t in t4]
        src = bass.AP(tensor=dv, offset=c * 2048 * 192, ap=[[192, 128], [128 * 192, 16], [1, 64]])
        dma("sp", DVt[pb][:, 0, :, 0:64], src, rdv, [("DV", pb, 0)])
        for nl in range(4):
            src = bass.AP(tensor=dv, offset=(c * 2048 + 512 * nl) * 192 + 64, ap=[[4 * 192, 128], [192, 4], [1, 64]])
            dma("sp", DVt[pb][:, 1, 4 * nl:4 * nl + 4, 0:64], src, rdv, [("DV", pb, 1, nl)])
        src = bass.AP(tensor=dv, offset=c * 2048 * 192 + 128, ap=[[16 * 192, 128], [192, 16], [1, 64]])
        dma("sp", DVt[pb][:, 2, :, 0:64], src, rdv, [("DV", pb, 2)])
        dma("sp", GA[:], gaz.ap()[:, c * 2048:(c + 1) * 2048], [("gaz", t) for t in t4], ["GA"])
        dvk = lambda p_, g: ([("DV", p_, g)] if g != 1 else [("DV", p_, 1, nl) for nl in range(4)]) + [("DVones", p_)]
        for g, r_ in (enumerate(RS) if DSTOP != 'load' else []):
            for qd in range(4):
                po = 4 + (cnt % 2)
                kinds = []
                a0 = 0
                if c == 0:
                    npv = [a for a in range(4) if 4 * qd + a >= r_]
                    if npv:
                        a0 = npv[0]
                        kinds.append(1)
                else:
                    kinds.append(1)
                kinds.append(0)
                kinfo = []
                for kd in kinds:
                    aa = a0 if kd == 1 else 0
                    pS = cnt % 4
                    eb = cnt % 2
                    cnt += 1
                    rk = {}
                    for a in range(aa, 4):
                        i = 4 * qd + a
                        if kd == 0:
                            ks, kp = i, pb
                        else:
                            ks, kp = (i - r_, pb) if i >= r_ else (i - r_ + 16, 1 - pb)
                        rk[a] = (ks, kp)
                    for a, (ks, kp) in rk.items():
                        i = 4 * qd + a
                        P.op("pe", I("matmul", ps[pS][:, 128 * a:128 * a + 128], DK[kp][:, g, 128 * ks:128 * ks + 128],
                                     DQ[pb][:, g, 128 * i:128 * i + 128], start=True, stop=True),
                             r=[("DK", kp, g), ("DQ", pb, g)], w=[PSK(pS)])
                    P.op("act", I("activation", out=EX[eb][:, 128 * aa:512], in_=ps[pS][:, 128 * aa:512], func=AF.Exp, scale=0.125),
                         r=[PSK(pS)], w=[("EX", eb)])
                    for a in range(aa, 4):
                        P.op("dve", I("tensor_tensor", out=PTd[eb][:, 128 * a:128 * a + 128], in0=EX[eb][:, 128 * a:128 * a + 128],
                                      in1=T5[:, g, kd, :], op=ALU.mult), r=[("EX", eb), ("T5", g, kd)], w=[("PTd", eb)])
                    kinfo.append((kd, eb, rk))
                for a in range(4):
                    seq = [(kd, eb, rk[a]) for (kd, eb, rk) in kinfo if a in rk]
                    for n_, (kd, eb, (ks, kp)) in enumerate(seq):
                        P.op("pe", I("matmul", ps[po][0:65, 128 * a:128 * a + 128], DVt[kp][:, g, ks, :], PTd[eb][:, 128 * a:128 * a + 128],
                                     start=(n_ == 0), stop=(n_ == len(seq) - 1)),
                             r=dvk(kp, g) + [("PTd", eb)], w=[PSK(po)])
                inv = bass.AP(tensor=ps[po], offset=0, ap=[[512, 65], [128, 4], [1, 128]])
                if DSTOP in ('qk', 'pv'):
                    continue
                for (p0, np_) in ((0, 64), (64, 1)):
                    invp = bass.AP(tensor=ps[po], offset=p0 * 512, ap=[[512, np_], [128, 4], [1, 128]])
                    if g == 0:
                        P.op("act", I("activation", out=ACC[p0:p0 + np_, 512 * qd:512 * qd + 512], in_=ps[po][p0:p0 + np_, :], func=AF.Copy),
                             r=[PSK(po)], w=[("ACC", qd, p0)])
                    elif g == 1:
                        av = bass.AP(tensor=ACC, offset=p0 * 2048 + 512 * qd, ap=[[2048, np_], [1, 4], [4, 128]])
                        P.op("dve", I("tensor_tensor", out=av, in0=invp, in1=av, op=ALU.add), r=[PSK(po), ("ACC", qd, p0)], w=[("ACC", qd, p0)])
                    else:
                        av = bass.AP(tensor=ACC, offset=p0 * 2048 + 4 * qd, ap=[[2048, np_], [1, 4], [16, 128]])
                        allacc = [("ACC", q_, p0) for q_ in range(4)]
                        P.op("dve", I("tensor_tensor", out=av, in0=invp, in1=av, op=ALU.add), r=[PSK(po)] + allacc, w=allacc)
        for qd in (range(4) if DSTOP is None else []):
            ob = qd % 2
            cs_ = slice(512 * qd, 512 * qd + 512)
            P.op("act", I("activation", out=lrow[64:65, :], in_=ACC[64:65, cs_], func=AF.Copy), r=[("ACC", q_, 64) for q_ in range(4)], w=["lrow"])
            P.op("pe", I("matmul", ps[6][0:64, :], ones_f[64:65, 0:64], lrow[64:65, :], start=True, stop=True),
                 r=["lrow", "ones"], w=[PSK(6)])
            P.op("dve", I("reciprocal", out=rbc[:], in_=ps[6][0:64, :]), r=[PSK(6)], w=["rbc"])
            P.op("dve", I("tensor_tensor", out=yf[:], in0=ACC[0:64, cs_], in1=rbc[:], op=ALU.mult), r=[("ACC", q_, 0) for q_ in range(4)] + ["rbc"], w=["yf"])
            P.op("dve", I("tensor_tensor", out=yo[ob][:], in0=yf[:], in1=GA[:, cs_], op=ALU.mult), r=["yf", "GA"], w=[("yo", ob)])
            dma("sp", y_send.ap()[128:192, c * 2048 + 512 * qd:c * 2048 + 512 * qd + 512], yo[ob][:], [("yo", ob)], [("y_send", 4 * c + qd, 2)])
    P.barrier()
    A.off = m0


def t5_bucket_np(dist):
    dist = np.asarray(dist, dtype=np.int64)
    d = np.maximum(dist, 1).astype(np.float32)
    large = 16 + (np.log(d / np.float32(16)) / np.float32(math.log(2048 / 16)) * np.float32(16)).astype(np.int32)
    large = np.minimum(large, 31)
    return np.where(dist < 16, dist, large)


def host_inputs(x, c, positions, w_ada, b_ada, norm_g, w_in, q_norm_g, w_uq, kv_norm_g, w_ukv,
                w_out_a, w_out_b, w_o, rel_bias, final_norm_g):
    S = x.shape[1]
    L = w_in.shape[0]
    xT_full = np.ascontiguousarray(x[0].T)
    f32 = np.float32
    inv_freq = (1.0 / (10000.0 ** (np.arange(0, 32, 2, dtype=f32) / f32(32)))).astype(f32)
    invf = np.tile(np.concatenate([inv_freq, inv_freq]), 4).reshape(128, 1).astype(f32)
    oh5 = np.zeros((32, 3, 2, 256), f32)
    m5 = np.zeros((128, 3, 2, 256), f32)
    for g, r in enumerate((1, 4, 16)):
        for i in range(128):
            oh5[t5_bucket_np(i * r), g, 0, i] = 1.0
            m5[:, g, 0, i] = 1.0
        oh5[t5_bucket_np(128 * r), g, 1, 0] = 1.0
        m5[:, g, 1, 0] = 1.0
        for i in range(129, 256):
            oh5[t5_bucket_np((i - 128) * r), g, 1, i] = 1.0
            m5[:, g, 1, i] = 1.0
    kk = np.arange(128)[:, None]
    cc = np.arange(128)[None, :]
    tri = (kk <= cc).astype(f32).astype(ml_dtypes.bfloat16)
    cT = np.ascontiguousarray(c[0].reshape(8, 128).T)

    def kchunk(w):
        return np.ascontiguousarray(w.reshape(8, 128, -1).transpose(1, 0, 2))

    maps = []
    for j in range(NCORE):
        C = slice(128 * j, 128 * j + 128)
        cols = np.concatenate([np.arange(128 * j, 128 * j + 128), 1024 + np.arange(128 * j, 128 * j + 128), 2048 + np.arange(128 * j, 128 * j + 128)])
        wsel = []
        wsel.append(np.arange(5120, 5120 + 416))
        for g in range(3):
            wsel.append((0 * 3 + g) * 512 + j * 64 + np.arange(64))
            wsel.append((1 * 3 + g) * 512 + j * 64 + np.arange(64))
        for g in range(3):
            wsel.append((2 * 3 + g) * 512 + j * 64 + np.arange(64))
        wsel.append(4608 + j * 64 + np.arange(64))
        wsel.append(5536 + j * 128 + np.arange(128))
        wsel.append(6560 + 128 * j + np.arange(128))
        wsel.append(6560 + 1024 + 128 * j + np.arange(128))
        wsel = np.concatenate(wsel)
        assert wsel.size == NW
        uq_cols = np.concatenate([np.concatenate([hh * 96 + 64 + np.arange(32), hh * 96 + np.arange(64)]) for hh in (2 * j, 2 * j + 1)])
        ukv_cols = np.concatenate([(2 * j) * 128 + np.arange(64), (2 * j + 1) * 128 + np.arange(64),
                                   (2 * j) * 128 + 64 + np.arange(64), (2 * j + 1) * 128 + 64 + np.arange(64)])
        m = {
            "xT": np.ascontiguousarray(xT_full[C]),
            "cT": cT,
            "pos": np.ascontiguousarray(positions.astype(np.int32).reshape(1, S)),
            "invf": invf,
            "w_ada": np.stack([kchunk(w_ada[l][:, cols]) for l in range(L)]),
            "b_ada": np.stack([np.ascontiguousarray(b_ada[l][cols].reshape(3, 128).T) for l in range(L)]),
            "norm_g": np.stack([norm_g[l][C].reshape(128, 1) for l in range(L)]),
            "w_in": np.stack([kchunk(w_in[l][:, wsel]) for l in range(L)]),
            "qng": np.stack([np.ascontiguousarray(q_norm_g[l].reshape(2, 128).T) for l in range(L)]),
            "kvng": np.stack([kv_norm_g[l].reshape(128, 1) for l in range(L)]),
            "w_uq": np.stack([np.ascontiguousarray(w_uq[l][:, uq_cols].reshape(2, 128, 192).transpose(1, 0, 2)) for l in range(L)]),
            "w_ukv": np.stack([np.ascontiguousarray(w_ukv[l][:, ukv_cols]) for l in range(L)]),
            "w_out": np.stack([np.ascontiguousarray(np.concatenate([w_out_b[l][:, C].reshape(8, 128, 128), w_out_a[l][:, C].reshape(4, 128, 128)], 0).transpose(1, 0, 2))
                               for l in range(L)]),
            "w_o": np.stack([kchunk(w_o[l][:, C]) for l in range(L)]),
            "fin_g": final_norm_g[C].reshape(128, 1).astype(f32),
            "rel": np.ascontiguousarray(rel_bias[:, [g * 8 + j for g in range(3)]]),
            "oh5": oh5, "m5": m5, "tri": tri,
        }
        maps.append({k: (v if v.dtype != np.float64 else v.astype(f32)) for k, v in m.items()})
    return maps


_CACHE = {}


def _run(S, L, seg, maps, extra):
    key = (S, L, seg)
    if key not in _CACHE:
        _CACHE[key] = build(S, L, False, None, seg)
    nc = _CACHE[key]
    names = set(SEG_IO[seg[0]][0]) | set(SEG_W[seg[0]]) | set(COMMON)
    in_maps = []
    for j in range(NCORE):
        m = {}
        for k in names:
            if k in extra:
                v = extra[k]
                m[k] = v[j] if isinstance(v, list) else v
            else:
                m[k] = maps[j][k]
        in_maps.append(m)
    res = run_bass_kernel_spmd(nc, in_maps, core_ids=list(range(NCORE)))
    return res.results


def kernel(x, c, positions, w_ada, b_ada, norm_g, w_in, q_norm_g, w_uq, kv_norm_g, w_ukv,
           w_out_a, w_out_b, w_o, rel_bias, final_norm_g, _debug=False, _stop=None):
    args = [np.asarray(a) for a in (x, c, positions, w_ada, b_ada, norm_g, w_in, q_norm_g, w_uq, kv_norm_g, w_ukv,
                                    w_out_a, w_out_b, w_o, rel_bias, final_norm_g)]
    S = args[0].shape[1]
    L = args[6].shape[0]
    maps = host_inputs(*args)
    if FUSED or _debug:
        key = (S, L, _debug, _stop)
        if key not in _CACHE:
            _CACHE[key] = build(S, L, _debug, _stop)
        nc = _CACHE[key]
        res = run_bass_kernel_spmd(nc, maps, core_ids=list(range(NCORE)))
        outT = np.concatenate([res.results[j]["outT"] for j in range(NCORE)], axis=0)
        out = np.ascontiguousarray(outT.T).reshape(1, S, D).astype(np.float32)
        if _debug:
            return out, res
        return out
    gather = lambda r, nm: np.concatenate([r[j][nm] for j in range(NCORE)], axis=0)
    xs = [maps[j]["xT"] for j in range(NCORE)]
    r = _run(S, L, ("ss0", 0), maps, {"xT": xs})
    ss_all = gather(r, "ss_send")
    for l in range(L):
        r = _run(S, L, ("norm", l), maps, {"xT": xs, "ss_all": ss_all})
        hT_all = gather(r, "hT_send")
        r = _run(S, L, ("attn", l), maps, {"hT_all": hT_all})
        y_all = gather(r, "y_send")
        gm = [r[j]["gmg"] for j in range(NCORE)]
        r = _run(S, L, ("mrg", l), maps, {"y_all": y_all, "gmg": gm})
        mg_all = gather(r, "mg_send")
        r = _run(S, L, ("xnew", l), maps, {"mg_all": mg_all, "xT": xs})
        xs = [r[j]["xcur"] for j in range(NCORE)]
        ss_all = gather(r, "ss_send")
    r = _run(S, L, ("final", 0), maps, {"xT": xs, "ss_all": ss_all})
    outT = np.concatenate([r[j]["outT"] for j in range(NCORE)], axis=0)
    return np.ascontiguousarray(outT.T).reshape(1, S, D).astype(np.float32)
```

```python
import math
import numpy as np
import ml_dtypes
import concourse.bass as bass
import concourse.mybir as mybir
from concourse.bass_utils import run_bass_kernel_spmd

F32 = mybir.dt.float32
BF16 = mybir.dt.bfloat16
I32 = mybir.dt.int32
AF = mybir.ActivationFunctionType
ALU = mybir.AluOpType

D = 1024
NCORE = 8
NSLOT = 8
EPS = 1e-6
NW = 1440
TWO_PI = 2.0 * math.pi
import os as _os
PARTS = set(_os.environ.get('PARTS', 'dil,gates,lat,krope,q,k,vm,vd').split(','))
PSTOP = _os.environ.get('PSTOP')
DSTOP = _os.environ.get('DSTOP')


def I(method, *args, **kw):
    return lambda e: getattr(e, method)(*args, **kw)


class Prog:
    STREAMS = ["pe", "act", "dve", "pool", "sp"]

    def __init__(self):
        self.ops = []
        self.lastw = {}
        self.readers = {}
        self.dma_n = {}
        self.slot_last = {}
        self.bar = {}
        self.last_in_class = {}

    def _cls(self, o):
        if o["dma"]:
            return (o["stream"], o["slot"])
        if o["cc"]:
            return "cc"
        return o["stream"]

    def op(self, stream, fn, r=(), w=(), dma=False, cc=False):
        idx = len(self.ops)
        deps = set(self.bar.values())
        for k in r:
            if k in self.lastw:
                deps.add(self.lastw[k])
        for k in w:
            if k in self.lastw:
                deps.add(self.lastw[k])
            deps.update(self.readers.get(k, {}).values())
        o = dict(stream=stream, fn=fn, deps=deps, dma=dma, cc=cc, needed=False, idx=idx)
        if dma:
            n = self.dma_n.get(stream, 0)
            self.dma_n[stream] = n + 1
            o["slot"] = n % NSLOT
            o["slotn"] = n // NSLOT + 1
            prev = self.slot_last.get((stream, o["slot"]))
            if prev is not None:
                deps.add(prev)
            self.slot_last[(stream, o["slot"])] = idx
        c = self._cls(o)
        o["cls"] = c
        for k in w:
            self.lastw[k] = idx
            self.readers[k] = {}
        for k in r:
            self.readers.setdefault(k, {})[c] = idx
        self.last_in_class[c] = idx
        self.ops.append(o)
        return idx

    def barrier(self):
        self.bar = dict(self.last_in_class)
        self.lastw = {}
        self.readers = {}

    def emit(self, nc, block, sems):
        ops = self.ops
        for o in ops:
            for d in o["deps"]:
                ops[d]["needed"] = True
        cnt = {}
        for o in ops:
            if o["dma"]:
                o["val"] = 16 * o["slotn"]
            elif o["needed"]:
                cnt[o["cls"]] = cnt.get(o["cls"], 0) + 1
                o["val"] = cnt[o["cls"]]

        def run(stream, eng):
            waited = {}
            for o in ops:
                if o["stream"] != stream:
                    continue
                need = {}
                for d in o["deps"]:
                    od = ops[d]
                    if od["cls"] == "pe" and stream == "pe" and not o["dma"]:
                        continue
                    c = od["cls"]
                    need[c] = max(need.get(c, 0), od["val"])
                for c, v in need.items():
                    if waited.get(c, 0) < v:
                        eng.wait_ge(sems[c], v)
                        waited[c] = v
                ins = o["fn"](eng)
                if ins is not None and (o["dma"] or o["needed"]):
                    ins.then_inc(sems[o["cls"]], 16 if o["dma"] else 1)

        @block.tensor
        def _(e):
            run("pe", e)

        @block.scalar
        def _(e):
            run("act", e)

        @block.vector
        def _(e):
            run("dve", e)

        @block.gpsimd
        def _(e):
            run("pool", e)

        @block.sync
        def _(e):
            run("sp", e)


SEG_IO = {"ss0": (("xT",), ("ss_send",)), "norm": (("xT", "ss_all"), ("hT_send",)), "attn": (("hT_all",), ("y_send", "gmg")),
          "mrg": (("y_all", "gmg"), ("mg_send",)), "xnew": (("mg_all", "xT"), ("xcur", "ss_send")), "final": (("xT", "ss_all"), ("outT",))}
SEG_W = {"ss0": (), "norm": (), "attn": ("w_in", "w_uq", "w_ukv"), "mrg": ("w_out",), "xnew": ("w_o",), "final": ()}
COMMON = ("cT", "pos", "invf", "fin_g", "rel", "oh5", "m5", "tri", "w_ada", "b_ada", "norm_g", "qng", "kvng")
FUSED = False


class Arena:
    def __init__(self, nc, limit=229344):
        self.nc = nc
        self.off = 16512
        self.limit = limit
        self.n = 0

    def alloc(self, shape, dt):
        esz = 4 if dt in (F32, I32) else 2
        per = esz
        for s in shape[1:]:
            per *= s
        per = (per + 31) // 32 * 32
        assert self.off + per <= self.limit, ("SBUF overflow", self.off, per)
        self.n += 1
        t = self.nc.alloc_sbuf_tensor_at("sb%d" % self.n, list(shape), dt, offset=self.off)
        self.off += per
        return t


def build(S, L=2, debug=False, stop=None, seg=None):
    NT = S // 512
    NB = S // 128
    NCH = S // 2048
    SQ = S // 4
    nc = bass.Bass("TRN2", target_bir_lowering=False)
    P = Prog()

    ext_in = set() if seg is None else set(SEG_IO[seg[0]][0]) | set(SEG_W[seg[0]]) | set(COMMON)
    ext_out = set() if seg is None else set(SEG_IO[seg[0]][1])

    def want(kind, l_):
        return seg is None or (seg[0] == kind and (kind in ("ss0", "final") or seg[1] == l_))

    def din(name, shape, dt):
        if seg is not None and name not in ext_in:
            return nc.dram_tensor(name, shape, dt)
        return nc.dram_tensor(name, shape, dt, kind="ExternalInput")

    xT = din("xT", [128, S], F32)
    cT = din("cT", [128, 8], F32)
    pos = din("pos", [1, S], I32)
    invf = din("invf", [128, 1], F32)
    w_ada = din("w_ada", [L, 128, 8, 384], F32)
    b_ada = din("b_ada", [L, 128, 3], F32)
    norm_g = din("norm_g", [L, 128, 1], F32)
    w_in = din("w_in", [L, 128, 8, NW], F32)
    qng = din("qng", [L, 128, 2], F32)
    kvng = din("kvng", [L, 128, 1], F32)
    w_uq = din("w_uq", [L, 128, 2, 192], F32)
    w_ukv = din("w_ukv", [L, 128, 256], F32)
    w_out = din("w_out", [L, 128, 12, 128], F32)
    w_o = din("w_o", [L, 128, 8, 128], F32)
    fin_g = din("fin_g", [128, 1], F32)
    rel = din("rel", [32, 3], F32)
    oh5 = din("oh5", [32, 3, 2, 256], F32)
    m5 = din("m5", [128, 3, 2, 256], F32)
    tri_in = din("tri", [128, 128], BF16)
    def dscr(name, shape, dt):
        if name in ext_in:
            return nc.dram_tensor(name, shape, dt, kind="ExternalInput")
        if name in ext_out:
            return nc.dram_tensor(name, shape, dt, kind="ExternalOutput")
        return nc.dram_tensor(name, shape, dt)

    outT = nc.dram_tensor("outT", [128, S], F32, kind="ExternalOutput") if seg is None else dscr("outT", [128, S], F32)

    ss_send = dscr("ss_send", [1, S], F32)
    ss_all = dscr("ss_all", [8, S], F32)
    hT_send = dscr("hT_send", [128, S], BF16)
    hT_all = dscr("hT_all", [1024, S], BF16)
    y_send = dscr("y_send", [192, S], BF16)
    y_all = dscr("y_all", [1536, S], BF16)
    mg_send = dscr("mg_send", [128, S], BF16)
    mg_all = dscr("mg_all", [1024, S], BF16)
    xcur = dscr("xcur", [128, S], F32)
    qT = dscr("qT", [2, 96, S], BF16)
    kT = dscr("kT", [2, 96, S], BF16)
    vm = dscr("vm", [S, 128], BF16)
    dqk = dscr("dqk", [3, 128, S], BF16)
    dv = dscr("dv", [S, 192], BF16)
    gaz = dscr("gaz", [64, S], BF16)
    gmz = dscr("gmz", [128, S], BF16)
    gmg = dscr("gmg", [2, 128, S], BF16)
    cs = dscr("cs", [2, 32, S], F32)
    t5w = dscr("t5w", [3, 2, 128, 256], F32)
    dbg = {}
    if debug:
        for nm, shp, dt in (("dbg_hT", [128, S], BF16), ("dbg_y", [192, S], BF16), ("dbg_x", [128, S], F32),
                            ("dbg_q", [2, 96, S], BF16), ("dbg_k", [2, 96, S], BF16), ("dbg_dqk", [3, 128, S], BF16),
                            ("dbg_dv", [S, 192], BF16), ("dbg_T5", [128, 768], F32), ("dbg_gaz", [64, S], BF16),
                            ("dbg_hT2", [128, S], BF16), ("dbg_y2", [192, S], BF16), ("dbg_x2", [128, S], F32), ("dbg_mg2", [128, S], BF16)):
            dbg[nm] = nc.dram_tensor(nm, shp, dt, kind="ExternalOutput")

    def allk(name):
        return [(name, t) for t in range(NT)]

    A = Arena(nc)
    ps = [nc.alloc_psum_tensor("psb%d" % i, [128, 512], F32) for i in range(8)]

    def PSK(i):
        return ("ps", i)

    ones_f = A.alloc([128, 128], F32)
    tri = A.alloc([128, 128], BF16)
    T5 = A.alloc([128, 3, 2, 128], F32)
    modc = A.alloc([128, 8], F32)
    sc = A.alloc([128, 8], F32)
    small = A.alloc([128, 16], F32)
    base_off = A.off

    def dma(stream, out, in_, r, w):
        P.op(stream, I("dma_start", out=out, in_=in_), r=r, w=w, dma=True)

    def ag(send, recv, name_s, name_r):
        if seg is not None:
            return
        P.op("pool", I("collective_compute", "AllGather", ALU.bypass, replica_groups=[list(range(NCORE))],
                                                     ins=[send.ap()], outs=[recv.ap()]),
             r=allk(name_s), w=allk(name_r), cc=True)

    P.op("dve", I("memset", ones_f[:], 1.0), w=["ones"])
    dma("sp", tri[:], tri_in.ap(), [], ["tri"])
    dma("sp", small[:, 4:5], fin_g.ap(), [], ["small4"])
    dma("sp", small[:, 5:6], invf.ap(), [], ["small5"])
    ct = A.alloc([128, 8], F32)
    dma("sp", ct[:], cT.ap(), [], ["ct"])
    sg = A.alloc([128, 8], F32)
    P.op("act", I("activation", out=sg[:], in_=ct[:], func=AF.Sigmoid), r=["ct"], w=["sg"])
    P.op("dve", I("tensor_tensor", out=sc[:], in0=ct[:], in1=sg[:], op=ALU.mult), r=["ct", "sg"], w=["sc"])

    relt = A.alloc([32, 3], F32)
    oh = A.alloc([32, 3, 2, 256], F32)
    m5t = A.alloc([128, 3, 2, 256], F32)
    relbc = A.alloc([32, 3, 128], F32)
    f5 = A.alloc([128, 3, 2, 256], F32)
    dma("sp", relt[:], rel.ap(), [], ["relt"])
    dma("sp", oh[:], oh5.ap(), [], ["oh"])
    dma("sp", m5t[:], m5.ap(), [], ["m5t"])
    for g in range(3):
        P.op("dve", I("tensor_scalar", out=relbc[:, g, :], in0=ones_f[0:32, :], scalar1=relt[:, g:g + 1],
                                                   scalar2=None, op0=ALU.mult), r=["ones", "relt"], w=[("relbc", g)])
        for kd in range(2):
            b = (g * 2 + kd) % 2
            P.op("pe", I("matmul", ps[b][:, 0:256], relbc[:, g, :], oh[:, g, kd, :], start=True, stop=True),
                 r=[("relbc", g), "oh"], w=[PSK(b)])
            P.op("act", I("activation", out=f5[:, g, kd, :], in_=ps[b][:, 0:256], func=AF.Exp),
                 r=[PSK(b)], w=[("f5", g, kd)])
            P.op("dve", I("tensor_tensor", out=f5[:, g, kd, :], in0=f5[:, g, kd, :], in1=m5t[:, g, kd, :], op=ALU.mult),
                 r=[("f5", g, kd), "m5t"], w=[("f5", g, kd)])
            dma("sp", t5w.ap()[g, kd], f5[:, g, kd, :], [("f5", g, kd)], [("t5w", g, kd)])
            src = bass.AP(tensor=t5w, offset=(g * 2 + kd) * 128 * 256, ap=[[255, 128], [1, 128]])
            dma("sp", T5[:, g, kd, :], src, [("t5w", g, kd)], [("T5", g, kd)])

    posi = A.alloc([128, SQ], I32)
    ang = A.alloc([128, SQ], F32)
    kf = A.alloc([128, SQ], F32)
    rr = A.alloc([128, SQ], F32)
    tmp = A.alloc([128, SQ], F32)
    ki = posi
    for q in range(4):
        src = bass.AP(tensor=pos, offset=q * SQ, ap=[[0, 32], [1, SQ]])
        dma("sp", posi[32 * q:32 * q + 32, :], src, [], [("posi", q)])
    pk = [("posi", q) for q in range(4)]
    P.op("dve", I("tensor_copy", out=ang[:], in_=posi[:]), r=pk, w=["ang"])
    P.op("dve", I("tensor_scalar", out=ang[:], in0=ang[:], scalar1=small[:, 5:6], scalar2=None, op0=ALU.mult),
         r=["ang", "small5"], w=["ang"])
    P.op("dve", I("tensor_scalar", out=kf[:], in0=ang[:], scalar1=1.0 / TWO_PI, scalar2=None, op0=ALU.mult), r=["ang"], w=["kf"])
    P.op("dve", I("tensor_copy", out=ki[:], in_=kf[:]), r=["kf"] + pk, w=["ki"])
    P.op("dve", I("tensor_copy", out=kf[:], in_=ki[:]), r=["ki"], w=["kf"])
    C1 = 6.28125
    C2 = TWO_PI - C1
    P.op("dve", I("scalar_tensor_tensor", out=rr[:], in0=kf[:], scalar=-C1, in1=ang[:], op0=ALU.mult, op1=ALU.add),
         r=["kf", "ang"], w=["rr"])
    P.op("dve", I("scalar_tensor_tensor", out=rr[:], in0=kf[:], scalar=-C2, in1=rr[:], op0=ALU.mult, op1=ALU.add),
         r=["kf", "rr"], w=["rr"])

    def fold(buf, key):
        P.op("dve", I("tensor_scalar", out=tmp[:], in0=buf[:], scalar1=math.pi, scalar2=-TWO_PI, op0=ALU.is_gt, op1=ALU.mult),
             r=[key], w=["tmp"])
        P.op("dve", I("tensor_tensor", out=buf[:], in0=buf[:], in1=tmp[:], op=ALU.add), r=[key, "tmp"], w=[key])
        P.op("dve", I("tensor_scalar", out=tmp[:], in0=buf[:], scalar1=-math.pi, scalar2=TWO_PI, op0=ALU.is_lt, op1=ALU.mult),
             r=[key], w=["tmp"])
        P.op("dve", I("tensor_tensor", out=buf[:], in0=buf[:], in1=tmp[:], op=ALU.add), r=[key, "tmp"], w=[key])
        P.op("dve", I("tensor_scalar", out=buf[:], in0=buf[:], scalar1=math.pi, scalar2=-math.pi, op0=ALU.min, op1=ALU.max),
             r=[key], w=[key])

    fold(rr, "rr")
    P.op("act", I("activation", out=kf[:], in_=rr[:], func=AF.Sin), r=["rr"], w=["kf"])
    for q in range(4):
        dma("sp", cs.ap()[1, :, q * SQ:(q + 1) * SQ], kf[32 * q:32 * q + 32, :], ["kf"], [("cs", 1, q)])
    P.op("dve", I("tensor_scalar", out=rr[:], in0=rr[:], scalar1=math.pi / 2, scalar2=None, op0=ALU.add), r=["rr"], w=["rr"])
    fold(rr, "rr")
    P.op("act", I("activation", out=ang[:], in_=rr[:], func=AF.Sin), r=["rr"], w=["ang"])
    for q in range(4):
        dma("sp", cs.ap()[0, :, q * SQ:(q + 1) * SQ], ang[32 * q:32 * q + 32, :], ["ang"], [("cs", 0, q)])
    P.barrier()
    A.off = base_off
    STOPPED = [False]

    def chk(name):
        if stop == name:
            STOPPED[0] = True
        return STOPPED[0]

    def sumsq_seg(src_dram, src_name):
        m0 = A.off
        xt = [A.alloc([128, 512], F32) for _ in range(2)]
        sq = [A.alloc([128, 512], F32) for _ in range(2)]
        row = [A.alloc([1, 512], F32) for _ in range(2)]
        for t in range(NT):
            b = t % 2
            sl = slice(t * 512, (t + 1) * 512)
            dma("sp", xt[b][:], src_dram.ap()[:, sl], [(src_name, t)], [("xt", b)])
            P.op("dve", I("tensor_tensor", out=sq[b][:], in0=xt[b][:], in1=xt[b][:], op=ALU.mult), r=[("xt", b)], w=[("sq", b)])
            P.op("pe", I("matmul", ps[b][0:1, :], ones_f[:, 0:1], sq[b][:], start=True, stop=True), r=[("sq", b), "ones"], w=[PSK(b)])
            P.op("act", I("activation", out=row[b][:], in_=ps[b][0:1, :], func=AF.Copy), r=[PSK(b)], w=[("row", b)])
            dma("sp", ss_send.ap()[:, sl], row[b][:], [("row", b)], [("ss_send", t)])
        ag(ss_send, ss_all, "ss_send", "ss_all")
        P.barrier()
        A.off = m0

    def norm_seg(src_dram, src_name, dst_dram, dst_name, dst_dt, mulcol, addcol):
        m0 = A.off
        xt = [A.alloc([128, 512], F32) for _ in range(2)]
        s8 = [A.alloc([8, 512], F32) for _ in range(2)]
        rs = [A.alloc([128, 512], F32) for _ in range(2)]
        ot = [A.alloc([128, 512], dst_dt) for _ in range(2)]
        for t in range(NT):
            b = t % 2
            sl = slice(t * 512, (t + 1) * 512)
            dma("sp", xt[b][:], src_dram.ap()[:, sl], [(src_name, t)], [("xt", b)])
            dma("sp", s8[b][:], ss_all.ap()[:, sl], [("ss_all", t)], [("s8", b)])
            P.op("pe", I("matmul", ps[b][:, :], ones_f[0:8, :], s8[b][:], start=True, stop=True), r=[("s8", b), "ones"], w=[PSK(b)])
            P.op("dve", I("tensor_scalar", out=rs[b][:], in0=ps[b][:, :], scalar1=1.0 / D, scalar2=EPS, op0=ALU.mult, op1=ALU.add),
                 r=[PSK(b)], w=[("rs", b)])
            P.op("act", I("activation", out=rs[b][:], in_=rs[b][:], func=AF.Sqrt), r=[("rs", b)], w=[("rs", b)])
            P.op("dve", I("reciprocal", out=rs[b][:], in_=rs[b][:]), r=[("rs", b)], w=[("rs", b)])
            P.op("dve", I("tensor_tensor", out=xt[b][:], in0=xt[b][:], in1=rs[b][:], op=ALU.mult), r=[("xt", b), ("rs", b)], w=[("xt", b)])
            if addcol is None:
                P.op("dve", I("tensor_scalar", out=ot[b][:], in0=xt[b][:], scalar1=mulcol, scalar2=None, op0=ALU.mult),
                     r=[("xt", b), "modc", "small4"], w=[("ot", b)])
            else:
                P.op("dve", I("tensor_scalar", out=ot[b][:], in0=xt[b][:], scalar1=mulcol, scalar2=addcol, op0=ALU.mult, op1=ALU.add),
                     r=[("xt", b), "modc"], w=[("ot", b)])
            dma("sp", dst_dram.ap()[:, sl], ot[b][:], [("ot", b)], [(dst_name, t)])
        P.barrier()
        A.off = m0

    def load_cast(dst, src_ap, shape, key, parts=128):
        m0 = A.off
        st = A.alloc(shape, F32)
        dma("sp", st[0:parts], src_ap, [], [("stg", key)])
        P.op("pool", I("tensor_copy", out=dst, in_=st[0:parts]), r=[("stg", key)], w=[key])
        return m0

    x_dram, x_name = xT, "xT"
    if want("ss0", 0) and not chk("init"):
        sumsq_seg(x_dram, x_name)
    for l in range(L):
        if seg is not None and (seg[0] in ("ss0", "final") or seg[1] != l):
            continue
        if chk("sumsq"):
            break
        m0 = A.off
        wa = A.alloc([128, 8, 384], F32)
        ba = A.alloc([128, 3], F32)
        dma("sp", wa[:], w_ada.ap()[l], [], ["wa"])
        dma("sp", ba[:], b_ada.ap()[l], [], ["ba"])
        dma("sp", small[:, 0:1], norm_g.ap()[l], [], ["small0"])
        dma("sp", small[:, 1:3], qng.ap()[l], [], ["small1"])
        dma("sp", small[:, 3:4], kvng.ap()[l], [], ["small3"])
        for grp in range(3):
            for kc in range(8):
                P.op("pe", I("matmul", ps[0][:, grp:grp + 1], wa[:, kc, grp * 128:(grp + 1) * 128], sc[:, kc:kc + 1],
                                                              start=(kc == 0), stop=(kc == 7)), r=["wa", "sc"], w=[PSK(0)])
        P.op("dve", I("tensor_tensor", out=modc[:, 0:3], in0=ps[0][:, 0:3], in1=ba[:], op=ALU.add), r=[PSK(0), "ba"], w=["modc"])
        P.op("dve", I("scalar_tensor_tensor", out=modc[:, 3:4], in0=modc[:, 1:2], scalar=1.0, in1=small[:, 0:1], op0=ALU.add, op1=ALU.mult),
             r=["modc", "small0"], w=["modc"])
        P.barrier()
        A.off = m0
        if want("norm", l):
            norm_seg(x_dram, x_name, hT_send, "hT_send", BF16, modc[:, 3:4], modc[:, 0:1])
            ag(hT_send, hT_all, "hT_send", "hT_all")
            if debug and l == 0:
                dma("sp", dbg["dbg_hT"].ap(), hT_send.ap(), allk("hT_send"), [])
            if debug and l == 1:
                dma("sp", dbg["dbg_hT2"].ap(), hT_send.ap(), allk("hT_send"), [])
            P.barrier()
            if chk("norm"):
                break
        if want("attn", l):
            proj_seg(nc, P, A, ps, locals())
            if chk("proj"):
                break
            mla_seg(nc, P, A, ps, locals())
            if debug and l == 0 and stop == "mla":
                dma("sp", dbg["dbg_y"].ap(), y_send.ap(), [], [])
                dma("sp", dbg["dbg_q"].ap(), qT.ap(), [], [])
                dma("sp", dbg["dbg_k"].ap(), kT.ap(), [], [])
                P.barrier()
            if chk("mla"):
                break
            dil_seg(nc, P, A, ps, locals())
            if chk("dil"):
                break
            ag(y_send, y_all, "y_send", "y_all")
            if debug and l == 0:
                dma("sp", dbg["dbg_y"].ap(), y_send.ap(), allk("y_send"), [])
                dma("sp", dbg["dbg_q"].ap(), qT.ap(), allk("qT0") + allk("qT1"), [])
                dma("sp", dbg["dbg_k"].ap(), kT.ap(), allk("kT0") + allk("kT1"), [])
                dma("sp", dbg["dbg_dqk"].ap(), dqk.ap(), [], [])
                dma("sp", dbg["dbg_dv"].ap(), dv.ap(), [], [])
                dma("sp", dbg["dbg_gaz"].ap(), gaz.ap(), [], [])
                dma("sp", dbg["dbg_T5"].ap(), T5[:].rearrange("p a b c -> p (a b c)"), [], [])
            if debug and l == 1:
                dma("sp", dbg["dbg_y2"].ap(), y_send.ap(), allk("y_send"), [])
            P.barrier()
        if want("mrg", l):
            m0 = A.off
            wo = A.alloc([128, 12, 128], BF16)
            load_cast(wo[:], w_out.ap()[l], [128, 12, 128], "wo")
            ym = [A.alloc([128, 8, 512], BF16) for _ in range(2)]
            yd = [A.alloc([128, 4, 512], BF16) for _ in range(2)]
            gg = [A.alloc([128, 2, 512], BF16) for _ in range(2)]
            t1 = [A.alloc([128, 512], F32) for _ in range(2)]
            t2 = [A.alloc([128, 512], F32) for _ in range(2)]
            mo = [A.alloc([128, 512], BF16) for _ in range(2)]
            for t in range(NT):
                b = t % 2
                sl = slice(t * 512, (t + 1) * 512)
                src = bass.AP(tensor=y_all, offset=t * 512, ap=[[S, 128], [192 * S, 8], [1, 512]])
                dma("sp", ym[b][:], src, [("y_all", t)], [("ym", b)])
                for hf in range(2):
                    src = bass.AP(tensor=y_all, offset=(hf * 192 + 128) * S + t * 512, ap=[[S, 64], [384 * S, 4], [1, 512]])
                    dma("sp", yd[b][64 * hf:64 * hf + 64], src, [("y_all", t)], [("yd", b, hf)])
                src = bass.AP(tensor=gmg, offset=t * 512, ap=[[S, 128], [128 * S, 2], [1, 512]])
                dma("sp", gg[b][:], src, [("gmg", t)], [("gg", b)])
                pa, pb = 2 * b, 2 * b + 1
                for c in range(4):
                    P.op("pe", I("matmul", ps[pa][:, :], wo[:, 8 + c, :], yd[b][:, c, :], start=(c == 0), stop=(c == 3)),
                         r=["wo", ("yd", b, 0), ("yd", b, 1)], w=[PSK(pa)])
                for r_ in range(8):
                    P.op("pe", I("matmul", ps[pb][:, :], wo[:, r_, :], ym[b][:, r_, :], start=(r_ == 0), stop=(r_ == 7)),
                         r=["wo", ("ym", b)], w=[PSK(pb)])
                P.op("dve", I("tensor_tensor", out=t1[b][:], in0=ps[pa][:, :], in1=gg[b][:, 0, :], op=ALU.mult),
                     r=[PSK(pa), ("gg", b)], w=[("t1", b)])
                P.op("dve", I("tensor_tensor", out=t2[b][:], in0=ps[pb][:, :], in1=gg[b][:, 1, :], op=ALU.mult),
                     r=[PSK(pb), ("gg", b)], w=[("t2", b)])
                P.op("pool", I("tensor_tensor", out=mo[b][:], in0=t1[b][:], in1=t2[b][:], op=ALU.add),
                     r=[("t1", b), ("t2", b)], w=[("mo", b)])
                dma("sp", mg_send.ap()[:, sl], mo[b][:], [("mo", b)], [("mg_send", t)])
            ag(mg_send, mg_all, "mg_send", "mg_all")
            P.barrier()
            A.off = m0
        if want("xnew", l):
            m0 = A.off
            wot = A.alloc([128, 8, 128], BF16)
            load_cast(wot[:], w_o.ap()[l], [128, 8, 128], "wot")
            mt = [A.alloc([128, 8, 512], BF16) for _ in range(2)]
            xt = [A.alloc([128, 512], F32) for _ in range(2)]
            xn = [A.alloc([128, 512], F32) for _ in range(2)]
            sq = [A.alloc([128, 512], F32) for _ in range(2)]
            row = [A.alloc([1, 512], F32) for _ in range(2)]
            for t in range(NT):
                b = t % 2
                sl = slice(t * 512, (t + 1) * 512)
                src = bass.AP(tensor=mg_all, offset=t * 512, ap=[[S, 128], [128 * S, 8], [1, 512]])
                dma("sp", mt[b][:], src, [("mg_all", t)], [("mt", b)])
                dma("sp", xt[b][:], x_dram.ap()[:, sl], [(x_name, t)], [("xt", b)])
                pa, pb = 2 * b, 2 * b + 1
                for kc in range(8):
                    P.op("pe", I("matmul", ps[pa][:, :], wot[:, kc, :], mt[b][:, kc, :], start=(kc == 0), stop=(kc == 7)),
                         r=["wot", ("mt", b)], w=[PSK(pa)])
                P.op("dve", I("scalar_tensor_tensor", out=xn[b][:], in0=ps[pa][:, :], scalar=modc[:, 2:3], in1=xt[b][:],
                                                                         op0=ALU.mult, op1=ALU.add), r=[PSK(pa), ("xt", b), "modc"], w=[("xn", b)])
                dma("sp", xcur.ap()[:, sl], xn[b][:], [("xn", b)], [("xcur", t)])
                P.op("pool", I("tensor_tensor", out=sq[b][:], in0=xn[b][:], in1=xn[b][:], op=ALU.mult), r=[("xn", b)], w=[("sq", b)])
                P.op("pe", I("matmul", ps[pb][0:1, :], ones_f[:, 0:1], sq[b][:], start=True, stop=True), r=[("sq", b), "ones"], w=[PSK(pb)])
                P.op("act", I("activation", out=row[b][:], in_=ps[pb][0:1, :], func=AF.Copy), r=[PSK(pb)], w=[("row", b)])
                dma("sp", ss_send.ap()[:, sl], row[b][:], [("row", b)], [("ss_send", t)])
            ag(ss_send, ss_all, "ss_send", "ss_all")
            P.barrier()
            A.off = m0
        if seg is None:
            x_dram, x_name = xcur, "xcur"
        if debug and l == 0:
            dma("sp", dbg["dbg_x"].ap(), xcur.ap(), allk("xcur"), [])
            P.barrier()
        if debug and l == 1:
            dma("sp", dbg["dbg_x2"].ap(), xcur.ap(), allk("xcur"), [])
            dma("sp", dbg["dbg_mg2"].ap(), mg_send.ap(), [], [])
            P.barrier()
    if want("final", 0) and not STOPPED[0]:
        norm_seg(x_dram, x_name, outT, "outT", F32, small[:, 4:5], None)
    P.barrier()
    P.op("sp", lambda e: None, r=[])

    sem_names = ["pe", "act", "dve", "pool", "sp", "cc"] + [("sp", i) for i in range(NSLOT)] + [("pool", i) for i in range(NSLOT)]
    sems = {}
    import contextlib
    with contextlib.ExitStack() as stk:
        for i, nm in enumerate(sem_names):
            sems[nm] = stk.enter_context(nc.semaphore("s%d" % i))
        block = stk.enter_context(nc.Block())
        P.emit(nc, block, sems)
    return nc


def proj_seg(nc, P, A, ps, env):
    S, NT, l = env["S"], env["NT"], env["l"]
    dma, load_cast = env["dma"], env["load_cast"]
    small, ones_f = env["small"], env["ones_f"]
    hT_all, w_in, w_uq, w_ukv, cs = env["hT_all"], env["w_in"], env["w_uq"], env["w_ukv"], env["cs"]
    qT, kT, vm, dqk, dv, gaz, gmz, gmg = (env[k] for k in ("qT", "kT", "vm", "dqk", "dv", "gaz", "gmz", "gmg"))
    PSK = env["PSK"]
    m0 = A.off
    win = A.alloc([128, 8, NW + 32], BF16)
    for pc in range(4):
        mm = A.off
        st = A.alloc([128, 8, 360], F32)
        dma("sp", st[:], w_in.ap()[l][:, :, pc * 360:(pc + 1) * 360], [], [("wst", pc)])
        P.op("pool" if pc % 2 else "dve", I("tensor_copy", out=win[:, :, pc * 360:(pc + 1) * 360], in_=st[:]),
             r=[("wst", pc)], w=[("win", pc)])
    winK = [("win", pc) for pc in range(4)]
    P.op("dve", I("tensor_scalar", out=win[:, :, NW:NW + 16], in0=win[:, :, 400:416], scalar1=-1.0, scalar2=None, op0=ALU.mult),
         r=winK, w=["winrot"])
    P.op("dve", I("tensor_copy", out=win[:, :, NW + 16:NW + 32], in_=win[:, :, 384:400]), r=winK, w=["winrot2"])
    winK = winK + ["winrot", "winrot2"]
    wuq = A.alloc([128, 2, 192 + 64], BF16)
    st = A.alloc([128, 2, 192], F32)
    dma("sp", st[:], w_uq.ap()[l], [], ["wuqs"])
    P.op("dve", I("tensor_copy", out=wuq[:, :, 0:192], in_=st[:]), r=["wuqs"], w=["wuq"])
    for h in range(2):
        P.op("dve", I("tensor_scalar", out=wuq[:, :, 192 + 32 * h:192 + 32 * h + 16], in0=st[:, :, 96 * h + 16:96 * h + 32],
                                                   scalar1=-1.0, scalar2=None, op0=ALU.mult), r=["wuqs"], w=[("wuqr", h)])
        P.op("dve", I("tensor_copy", out=wuq[:, :, 192 + 32 * h + 16:192 + 32 * h + 32], in_=st[:, :, 96 * h:96 * h + 16]),
             r=["wuqs"], w=[("wuqr2", h)])
    wuqK = ["wuq"] + [("wuqr", h) for h in range(2)] + [("wuqr2", h) for h in range(2)]
    wkv = A.alloc([128, 256], BF16)
    st2 = A.alloc([128, 256], F32)
    dma("sp", st2[:], w_ukv.ap()[l], [], ["wkvs"])
    P.op("dve", I("tensor_copy", out=wkv[:], in_=st2[:]), r=["wkvs"], w=["wkv"])

    NBUF = 2
    hT = [A.alloc([128, 8, 512], BF16) for _ in range(NBUF)]
    csq = [A.alloc([32, 2, 512], F32) for _ in range(NBUF)]
    oq = [A.alloc([128, 3, 512], BF16) for _ in range(NBUF)]
    ogz = [A.alloc([128, 4, 512], BF16) for _ in range(NBUF)]
    sgt = [A.alloc([128, 512], F32) for _ in range(NBUF)]
    cq2 = [A.alloc([128, 2, 512], F32) for _ in range(NBUF)]
    rsq = [A.alloc([128, 512], F32) for _ in range(NBUF)]
    rsk = [A.alloc([128, 512], F32) for _ in range(NBUF)]
    cqn = [A.alloc([128, 2, 512], BF16) for _ in range(NBUF)]
    ckn = [A.alloc([128, 512], BF16) for _ in range(NBUF)]
    krp = [A.alloc([32, 512], BF16) for _ in range(NBUF)]
    rt1 = [A.alloc([32, 512], F32) for _ in range(NBUF)]
    rt2 = [A.alloc([32, 512], F32) for _ in range(NBUF)]
    qo = [A.alloc([96, 2, 512], BF16) for _ in range(NBUF)]
    ko = [A.alloc([64, 2, 512], BF16) for _ in range(NBUF)]
    vdo = [A.alloc([128, 4, 192], BF16) for _ in range(NBUF)]
    vmo = [A.alloc([128, 4, 128], BF16) for _ in range(NBUF)]

    CQ, CKV, KR, DQK, DV, AZ, MZ, MG = 0, 256, 384, 416, 800, 992, 1056, 1184
    pcnt = [0]

    def nps():
        pcnt[0] = (pcnt[0] + 1) % 8
        return pcnt[0]

    def mm_feat(b, pi, rows, c0, ncols=None):
        for kc in range(8):
            P.op("pe", I("matmul", ps[pi][0:rows, :], win[:, kc, c0:c0 + rows], hT[b][:, kc, :], start=(kc == 0), stop=(kc == 7)),
                 r=winK + [("hT", b)], w=[PSK(pi)])

    for t in range(NT):
        b = t % NBUF
        sl = slice(t * 512, (t + 1) * 512)
        src = bass.AP(tensor=hT_all, offset=t * 512, ap=[[S, 128], [128 * S, 8], [1, 512]])
        dma("sp", hT[b][:], src, [("hT_all", t)], [("hT", b)])
        src = bass.AP(tensor=cs, offset=t * 512, ap=[[S, 32], [32 * S, 2], [1, 512]])
        dma("sp", csq[b][:], src, [("cs", 0, q) for q in range(4)] + [("cs", 1, q) for q in range(4)], [("csq", b)])
        for g, r_ in (enumerate((1, 4, 16)) if 'dil' in PARTS else []):
            pi = nps()
            mm_feat(b, pi, 128, DQK + 128 * g)
            if r_ == 1:
                P.op("act", I("activation", out=oq[b][:, g, :], in_=ps[pi][:, :], func=AF.Copy), r=[PSK(pi)], w=[("oq", b, g)])
            else:
                nm = 512 // r_
                inv = bass.AP(tensor=ps[pi], offset=0, ap=[[512, 128], [1, r_], [r_, nm]])
                if r_ == 4:
                    outv = bass.AP(tensor=oq[b], offset=g * 512, ap=[[1536, 128], [128, 4], [1, 128]])
                else:
                    outv = None
                if r_ == 4:
                    P.op("act", I("activation", out=outv, in_=inv, func=AF.Copy), r=[PSK(pi)], w=[("oq", b, g)])
                else:
                    outv = bass.AP(tensor=oq[b], offset=g * 512, ap=[[1536, 128], [32, 16], [1, 32]])
                    P.op("act", I("activation", out=outv, in_=inv, func=AF.Copy), r=[PSK(pi)], w=[("oq", b, g)])
            if r_ == 16:
                ch, sub = t // 4, t % 4
                dst = bass.AP(tensor=dqk, offset=g * 128 * S + ch * 2048 + 32 * sub, ap=[[S, 128], [128, 16], [1, 32]])
                srcv = bass.AP(tensor=oq[b], offset=g * 512, ap=[[1536, 128], [32, 16], [1, 32]])
                dma("sp", dst, srcv, [("oq", b, g)], [("dqk", g, t)])
            else:
                dma("sp", dqk.ap()[g, :, sl], oq[b][:, g, :], [("oq", b, g)], [("dqk", g, t)])
        for gi, (c0, rows, dst, nm) in enumerate(() if 'gates' not in PARTS else ((AZ, 64, gaz.ap()[:, sl], "gaz"), (MZ, 128, gmz.ap()[:, sl], "gmz"),
                                                 (MG, 128, gmg.ap()[0, :, sl], "gmg0"), (MG + 128, 128, gmg.ap()[1, :, sl], "gmg1"))):
            pi = nps()
            mm_feat(b, pi, rows, c0)
            if gi < 2:
                P.op("act", I("activation", out=sgt[b][0:rows, :], in_=ps[pi][0:rows, :], func=AF.Sigmoid),
                     r=[PSK(pi)], w=[("sgt", b)])
                P.op("dve", I("tensor_tensor", out=ogz[b][0:rows, gi, :], in0=ps[pi][0:rows, :], in1=sgt[b][0:rows, :], op=ALU.mult),
                     r=[PSK(pi), ("sgt", b)], w=[("ogz", b, gi)])
            else:
                P.op("act", I("activation", out=ogz[b][:, gi, :], in_=ps[pi][:, :], func=AF.Sigmoid), r=[PSK(pi)], w=[("ogz", b, gi)])
            dma("sp", dst, ogz[b][0:rows, gi, :], [("ogz", b, gi)], [(nm, t)])
        if PSTOP == 'gates':
            continue
        pq = [nps(), nps()]
        for i in range(2):
            mm_feat(b, pq[i], 128, CQ + 128 * i)
            P.op("act", I("activation", out=cq2[b][:, i, :], in_=ps[pq[i]][:, :], func=AF.Square),
                 r=[PSK(pq[i])], w=[("cq2", b, i)])
        pr = nps()
        for i in range(2):
            P.op("pe", I("matmul", ps[pr][:, :], ones_f[:, :], cq2[b][:, i, :], start=(i == 0), stop=(i == 1)),
                 r=["ones", ("cq2", b, 0), ("cq2", b, 1)], w=[PSK(pr)])
        P.op("dve", I("tensor_scalar", out=rsq[b][:], in0=ps[pr][:, :], scalar1=1.0 / 256, scalar2=EPS, op0=ALU.mult, op1=ALU.add),
             r=[PSK(pr)], w=[("rsq", b)])
        P.op("act", I("activation", out=rsq[b][:], in_=rsq[b][:], func=AF.Sqrt), r=[("rsq", b)], w=[("rsq", b)])
        P.op("dve", I("reciprocal", out=rsq[b][:], in_=rsq[b][:]), r=[("rsq", b)], w=[("rsq", b)])
        for i in range(2):
            P.op("dve", I("scalar_tensor_tensor", out=cqn[b][:, i, :], in0=ps[pq[i]][:, :], scalar=small[:, 1 + i:2 + i], in1=rsq[b][:],
                                                              op0=ALU.mult, op1=ALU.mult), r=[PSK(pq[i]), ("rsq", b), "small1"], w=[("cqn", b, i)])
        pk_ = nps()
        mm_feat(b, pk_, 128, CKV)
        P.op("act", I("activation", out=cq2[b][:, 0, :], in_=ps[pk_][:, :], func=AF.Square),
             r=[PSK(pk_)], w=[("cq2", b, 0)])
        pr2 = nps()
        P.op("pe", I("matmul", ps[pr2][:, :], ones_f[:, :], cq2[b][:, 0, :], start=True, stop=True),
             r=["ones", ("cq2", b, 0)], w=[PSK(pr2)])
        P.op("dve", I("tensor_scalar", out=rsk[b][:], in0=ps[pr2][:, :], scalar1=1.0 / 128, scalar2=EPS, op0=ALU.mult, op1=ALU.add),
             r=[PSK(pr2)], w=[("rsk", b)])
        P.op("act", I("activation", out=rsk[b][:], in_=rsk[b][:], func=AF.Sqrt), r=[("rsk", b)], w=[("rsk", b)])
        P.op("dve", I("reciprocal", out=rsk[b][:], in_=rsk[b][:]), r=[("rsk", b)], w=[("rsk", b)])
        P.op("dve", I("scalar_tensor_tensor", out=ckn[b][:], in0=ps[pk_][:, :], scalar=small[:, 3:4], in1=rsk[b][:],
                                                              op0=ALU.mult, op1=ALU.mult), r=[PSK(pk_), ("rsk", b), "small3"], w=[("ckn", b)])
        if PSTOP == 'lat':
            continue
        p1, p2 = nps(), nps()
        for kc in range(8):
            P.op("pe", I("matmul", ps[p1][0:32, :], win[:, kc, KR:KR + 32], hT[b][:, kc, :], start=(kc == 0), stop=(kc == 7)),
                 r=winK + [("hT", b)], w=[PSK(p1)])
        for kc in range(8):
            P.op("pe", I("matmul", ps[p2][0:32, :], win[:, kc, NW:NW + 32], hT[b][:, kc, :], start=(kc == 0), stop=(kc == 7)),
                 r=winK + [("hT", b)], w=[PSK(p2)])
        P.op("dve", I("tensor_tensor", out=rt1[b][:], in0=ps[p1][0:32, :], in1=csq[b][:, 0, :], op=ALU.mult), r=[PSK(p1), ("csq", b)], w=[("rt1", b)])
        P.op("dve", I("tensor_tensor", out=rt2[b][:], in0=ps[p2][0:32, :], in1=csq[b][:, 1, :], op=ALU.mult), r=[PSK(p2), ("csq", b)], w=[("rt2", b)])
        P.op("dve", I("tensor_tensor", out=krp[b][:], in0=rt1[b][:], in1=rt2[b][:], op=ALU.add), r=[("rt1", b), ("rt2", b)], w=[("krp", b)])
        for h in range(2):
            dma("sp", kT.ap()[h, 0:32, sl], krp[b][:], [("krp", b)], [("kT%d" % h, t, "r")])
        if PSTOP == 'krope':
            continue
        for h in range(2):
            pa_, pb_ = nps(), nps()
            for i in range(2):
                P.op("pe", I("matmul", ps[pa_][0:96, :], wuq[:, i, 96 * h:96 * h + 96], cqn[b][:, i, :], start=(i == 0), stop=(i == 1)),
                     r=wuqK + [("cqn", b, 0), ("cqn", b, 1)], w=[PSK(pa_)])
            for i in range(2):
                P.op("pe", I("matmul", ps[pb_][0:32, :], wuq[:, i, 192 + 32 * h:192 + 32 * h + 32], cqn[b][:, i, :], start=(i == 0), stop=(i == 1)),
                     r=wuqK + [("cqn", b, 0), ("cqn", b, 1)], w=[PSK(pb_)])
            P.op("act", I("activation", out=qo[b][32:64, h, :], in_=ps[pa_][32:64, :], func=AF.Copy), r=[PSK(pa_)], w=[("qo", b, h, "n")])
            P.op("act", I("activation", out=qo[b][64:96, h, :], in_=ps[pa_][64:96, :], func=AF.Copy), r=[PSK(pa_)], w=[("qo", b, h, "n2")])
            P.op("dve", I("tensor_tensor", out=rt1[b][:], in0=ps[pa_][0:32, :], in1=csq[b][:, 0, :], op=ALU.mult),
                 r=[PSK(pa_), ("csq", b)], w=[("rt1", b)])
            P.op("dve", I("tensor_tensor", out=rt2[b][:], in0=ps[pb_][0:32, :], in1=csq[b][:, 1, :], op=ALU.mult),
                 r=[PSK(pb_), ("csq", b)], w=[("rt2", b)])
            P.op("pool", I("tensor_tensor", out=qo[b][0:32, h, :], in0=rt1[b][:], in1=rt2[b][:], op=ALU.add),
                 r=[("rt1", b), ("rt2", b)], w=[("qo", b, h, "r")])
            dma("sp", qT.ap()[h, :, sl], qo[b][:, h, :], [("qo", b, h, "n"), ("qo", b, h, "n2"), ("qo", b, h, "r")], [("qT%d" % h, t)])
        if PSTOP == 'q':
            continue
        for h in range(2):
            pi = nps()
            P.op("pe", I("matmul", ps[pi][0:64, :], wkv[:, 64 * h:64 * h + 64], ckn[b][:], start=True, stop=True),
                 r=["wkv", ("ckn", b)], w=[PSK(pi)])
            P.op("act", I("activation", out=ko[b][:, h, :], in_=ps[pi][0:64, :], func=AF.Copy), r=[PSK(pi)], w=[("ko", b, h)])
            dma("sp", kT.ap()[h, 32:96, sl], ko[b][:, h, :], [("ko", b, h)], [("kT%d" % h, t, "n")])
        if PSTOP == 'k':
            continue
        pi = nps()
        for s4 in range(4):
            P.op("pe", I("matmul", ps[pi][:, 128 * s4:128 * s4 + 128], ckn[b][:, 128 * s4:128 * s4 + 128], wkv[:, 128:256], start=True, stop=True),
                 r=["wkv", ("ckn", b)], w=[PSK(pi)])
        P.op("act", I("activation", out=vmo[b][:, :, :], in_=ps[pi][:, :].rearrange("p (a c) -> p a c", a=4), func=AF.Copy),
             r=[PSK(pi)], w=[("vmo", b)])
        dst = bass.AP(tensor=vm, offset=t * 512 * 128, ap=[[128, 128], [128 * 128, 4], [1, 128]])
        dma("sp", dst, vmo[b][:, :, :], [("vmo", b)], [("vm", t)])
        pi = nps()
        pj = nps()
        for s4 in range(4):
            pp = pi if s4 < 2 else pj
            o0 = 192 * (s4 % 2)
            for kc in range(8):
                P.op("pe", I("matmul", ps[pp][:, o0:o0 + 192], hT[b][:, kc, 128 * s4:128 * s4 + 128], win[:, kc, DV:DV + 192],
                                                                         start=(kc == 0), stop=(kc == 7)), r=winK + [("hT", b)], w=[PSK(pp)])
        P.op("act", I("activation", out=vdo[b][:, 0:2, :], in_=ps[pi][:, 0:384].rearrange("p (a c) -> p a c", a=2), func=AF.Copy),
             r=[PSK(pi)], w=[("vdo", b, 0)])
        P.op("act", I("activation", out=vdo[b][:, 2:4, :], in_=ps[pj][:, 0:384].rearrange("p (a c) -> p a c", a=2), func=AF.Copy),
             r=[PSK(pj)], w=[("vdo", b, 1)])
        dst = bass.AP(tensor=dv, offset=t * 512 * 192, ap=[[192, 128], [128 * 192, 4], [1, 192]])
        dma("sp", dst, vdo[b][:, :, :], [("vdo", b, 0), ("vdo", b, 1)], [("dv", t)])
    P.barrier()
    A.off = m0


def mla_seg(nc, P, A, ps, env):
    S, NT, NB = env["S"], env["NT"], env["NB"]
    dma = env["dma"]
    ones_f, tri = env["ones_f"], env["tri"]
    qT, kT, vm, gmz, y_send = env["qT"], env["kT"], env["vm"], env["gmz"], env["y_send"]
    PSK = env["PSK"]
    m0 = A.off
    scale = 96.0 ** -0.5
    KT = A.alloc([96, S], BF16)
    VA = A.alloc([128, NB, 65], BF16)
    QT = [A.alloc([96, 512], BF16) for _ in range(2)]
    GZ = [A.alloc([64, 512], BF16) for _ in range(3)]
    PT = [A.alloc([128, 512], BF16) for _ in range(4)]
    lrow = A.alloc([65, 512], F32)
    rbc = A.alloc([64, 512], F32)
    yf = A.alloc([64, 512], F32)
    yo = [A.alloc([64, 512], BF16) for _ in range(2)]
    P.op("pool", I("memset", VA[:, :, 64:65], 1.0), w=["VAones"])
    blk = 0
    for h in range(2):
        allkT = [("kT%d" % h, t, "r") for t in range(NT)] + [("kT%d" % h, t, "n") for t in range(NT)]
        dma("sp", KT[:], kT.ap()[h], allkT, ["KT"])
        for v0 in range(0, NB, 16):
            src = bass.AP(tensor=vm, offset=64 * h + v0 * 128 * 128, ap=[[128, 128], [128 * 128, 16], [1, 64]])
            dma("sp", VA[:, v0:v0 + 16, 0:64], src, [("vm", t) for t in range(NT)], [("VA", v0)])
        PD = 2
        blocks = [(qt, kb) for qt in range(NT) for kb in range(4 * qt + 4)]
        nblk = len(blocks)

        def load_q(qt):
            qb = qt % 2
            sl = slice(qt * 512, (qt + 1) * 512)
            dma("sp", QT[qb][:], qT.ap()[h, :, sl], [("qT%d" % h, qt)], [("QT", qb)])
            dma("sp", GZ[qt % 3][:], gmz.ap()[64 * h:64 * h + 64, sl], [("gmz", qt)], [("GZ", qt % 3)])

        load_q(0)
        for i in range(nblk + PD):
            if i < nblk:
                qt, kb = blocks[i]
                qb = qt % 2
                if kb == 0 and qt + 1 < NT:
                    load_q(qt + 1)
                d = kb - 4 * qt
                c0 = 128 * d if d > 0 else 0
                sb_ = i % 4
                P.op("pe", I("matmul", ps[sb_][:, c0:512], KT[:, kb * 128:(kb + 1) * 128], QT[qb][:, c0:512], start=True, stop=True),
                     r=["KT", ("QT", qb)], w=[PSK(sb_)])
                P.op("act", I("activation", out=PT[sb_][:, c0:512], in_=ps[sb_][:, c0:512], func=AF.Exp, scale=scale),
                     r=[PSK(sb_)], w=[("PT", sb_)])
                if d >= 0:
                    P.op("dve", I("tensor_tensor", out=PT[sb_][:, c0:c0 + 128], in0=PT[sb_][:, c0:c0 + 128], in1=tri[:], op=ALU.mult),
                         r=[("PT", sb_), "tri"], w=[("PT", sb_)])
            j = i - PD
            if j >= 0:
                qt, kb = blocks[j]
                qb = qt % 2
                po = 4 + qb
                nkb = 4 * qt + 4
                d = kb - 4 * qt
                c0 = 128 * d if d > 0 else 0
                sb_ = j % 4
                sl = slice(qt * 512, (qt + 1) * 512)
                P.op("pe", I("matmul", ps[po][0:65, c0:512], VA[:, kb, :], PT[sb_][:, c0:512], start=(kb == 0), stop=(kb == nkb - 1)),
                     r=[("VA", kb // 16 * 16), "VAones", ("PT", sb_)], w=[PSK(po)])
                if kb == nkb - 1:
                    P.op("act", I("activation", out=lrow[64:65, :], in_=ps[po][64:65, :], func=AF.Copy), r=[PSK(po)], w=["lrow"])
                    P.op("pe", I("matmul", ps[6][0:64, :], ones_f[64:65, 0:64], lrow[64:65, :], start=True, stop=True), r=["lrow", "ones"], w=[PSK(6)])
                    P.op("dve", I("reciprocal", out=rbc[:], in_=ps[6][0:64, :]), r=[PSK(6)], w=["rbc"])
                    P.op("dve", I("tensor_tensor", out=yf[:], in0=ps[po][0:64, :], in1=rbc[:], op=ALU.mult), r=[PSK(po), "rbc"], w=["yf"])
                    P.op("pool", I("tensor_tensor", out=yo[qb][:], in0=yf[:], in1=GZ[qt % 3][:], op=ALU.mult), r=["yf", ("GZ", qt % 3)], w=[("yo", qb)])
                    dma("sp", y_send.ap()[64 * h:64 * h + 64, sl], yo[qb][:], [("yo", qb)], [("y_send", qt, h)])
    P.barrier()
    A.off = m0


def dil_seg(nc, P, A, ps, env):
    S, NT, NCH = env["S"], env["NT"], env["NCH"]
    dma = env["dma"]
    ones_f, T5 = env["ones_f"], env["T5"]
    dqk, dv, gaz, y_send = env["dqk"], env["dv"], env["gaz"], env["y_send"]
    PSK = env["PSK"]
    m0 = A.off
    DQ = [A.alloc([64, 3, 2048], BF16) for _ in range(2)]
    DK = [A.alloc([64, 3, 2048], BF16) for _ in range(2)]
    DVt = [A.alloc([128, 3, 16, 65], BF16) for _ in range(2)]
    ACC = A.alloc([65, 2048], F32)
    EX = [A.alloc([128, 512], F32) for _ in range(2)]
    PTd = [A.alloc([128, 512], BF16) for _ in range(2)]
    GA = A.alloc([64, 2048], BF16)
    lrow = A.alloc([65, 512], F32)
    rbc = A.alloc([64, 512], F32)
    yf = A.alloc([64, 512], F32)
    yo = [A.alloc([64, 512], BF16) for _ in range(2)]
    for pb in range(2):
        P.op("pool", I("memset", DVt[pb][:, :, :, 64:65], 1.0), w=[("DVones", pb)])
    RS = (1, 4, 16)
    cnt = 0
    for c in range(NCH):
        pb = c % 2
        t4 = [4 * c + i for i in range(4)]
        for g in range(3):
            dma("sp", DQ[pb][:, g, :], dqk.ap()[g, 0:64, c * 2048:(c + 1) * 2048], [("dqk", g, t) for t in t4], [("DQ", pb, g)])
            dma("sp", DK[pb][:, g, :], dqk.ap()[g, 64:128, c * 2048:(c + 1) * 2048], [("dqk", g, t) for t in t4], [("DK", pb, g)])
        rdv = [("dv", t) for t in t4]
        src = bass.AP(tensor=dv, offset=c * 2048 * 192, ap=[[192, 128], [128 * 192, 16], [1, 64]])
        dma("sp", DVt[pb][:, 0, :, 0:64], src, rdv, [("DV", pb, 0)])
        for nl in range(4):
            src = bass.AP(tensor=dv, offset=(c * 2048 + 512 * nl) * 192 + 64, ap=[[4 * 192, 128], [192, 4], [1, 64]])
            dma("sp", DVt[pb][:, 1, 4 * nl:4 * nl + 4, 0:64], src, rdv, [("DV", pb, 1, nl)])
        src = bass.AP(tensor=dv, offset=c * 2048 * 192 + 128, ap=[[16 * 192, 128], [192, 16], [1, 64]])
        dma("sp", DVt[pb][:, 2, :, 0:64], src, rdv, [("DV", pb, 2)])
        dma("sp", GA[:], gaz.ap()[:, c * 2048:(c + 1) * 2048], [("gaz", t) for t in t4], ["GA"])
        dvk = lambda p_, g: ([("DV", p_, g)] if g != 1 else [("DV", p_, 1, nl) for nl in range(4)]) + [("DVones", p_)]
        for g, r_ in (enumerate(RS) if DSTOP != 'load' else []):
            for qd in range(4):
                po = 4 + (cnt % 2)
                kinds = []
                a0 = 0
                if c == 0:
                    npv = [a for a in range(4) if 4 * qd + a >= r_]
                    if npv:
                        a0 = npv[0]
                        kinds.append(1)
                else:
                    kinds.append(1)
                kinds.append(0)
                kinfo = []
                for kd in kinds:
                    aa = a0 if kd == 1 else 0
                    pS = cnt % 4
                    eb = cnt % 2
                    cnt += 1
                    rk = {}
                    for a in range(aa, 4):
                        i = 4 * qd + a
                        if kd == 0:
                            ks, kp = i, pb
                        else:
                            ks, kp = (i - r_, pb) if i >= r_ else (i - r_ + 16, 1 - pb)
                        rk[a] = (ks, kp)
                    for a, (ks, kp) in rk.items():
                        i = 4 * qd + a
                        P.op("pe", I("matmul", ps[pS][:, 128 * a:128 * a + 128], DK[kp][:, g, 128 * ks:128 * ks + 128],
                                     DQ[pb][:, g, 128 * i:128 * i + 128], start=True, stop=True),
                             r=[("DK", kp, g), ("DQ", pb, g)], w=[PSK(pS)])
                    P.op("act", I("activation", out=EX[eb][:, 128 * aa:512], in_=ps[pS][:, 128 * aa:512], func=AF.Exp, scale=0.125),
                         r=[PSK(pS)], w=[("EX", eb)])
                    for a in range(aa, 4):
                        P.op("dve", I("tensor_tensor", out=PTd[eb][:, 128 * a:128 * a + 128], in0=EX[eb][:, 128 * a:128 * a + 128],
                                      in1=T5[:, g, kd, :], op=ALU.mult), r=[("EX", eb), ("T5", g, kd)], w=[("PTd", eb)])
                    kinfo.append((kd, eb, rk))
                for a in range(4):
                    seq = [(kd, eb, rk[a]) for (kd, eb, rk) in kinfo if a in rk]
                    for n_, (kd, eb, (ks, kp)) in enumerate(seq):
                        P.op("pe", I("matmul", ps[po][0:65, 128 * a:128 * a + 128], DVt[kp][:, g, ks, :], PTd[eb][:, 128 * a:128 * a + 128],
                                     start=(n_ == 0), stop=(n_ == len(seq) - 1)),
                             r=dvk(kp, g) + [("PTd", eb)], w=[PSK(po)])
                inv = bass.AP(tensor=ps[po], offset=0, ap=[[512, 65], [128, 4], [1, 128]])
                if DSTOP in ('qk', 'pv'):
                    continue
                for (p0, np_) in ((0, 64), (64, 1)):
                    invp = bass.AP(tensor=ps[po], offset=p0 * 512, ap=[[512, np_], [128, 4], [1, 128]])
                    if g == 0:
                        P.op("act", I("activation", out=ACC[p0:p0 + np_, 512 * qd:512 * qd + 512], in_=ps[po][p0:p0 + np_, :], func=AF.Copy),
                             r=[PSK(po)], w=[("ACC", qd, p0)])
                    elif g == 1:
                        av = bass.AP(tensor=ACC, offset=p0 * 2048 + 512 * qd, ap=[[2048, np_], [1, 4], [4, 128]])
                        P.op("dve", I("tensor_tensor", out=av, in0=invp, in1=av, op=ALU.add), r=[PSK(po), ("ACC", qd, p0)], w=[("ACC", qd, p0)])
                    else:
                        av = bass.AP(tensor=ACC, offset=p0 * 2048 + 4 * qd, ap=[[2048, np_], [1, 4], [16, 128]])
                        allacc = [("ACC", q_, p0) for q_ in range(4)]
                        P.op("dve", I("tensor_tensor", out=av, in0=invp, in1=av, op=ALU.add), r=[PSK(po)] + allacc, w=allacc)
        for qd in (range(4) if DSTOP is None else []):
            ob = qd % 2
            cs_ = slice(512 * qd, 512 * qd + 512)
            P.op("act", I("activation", out=lrow[64:65, :], in_=ACC[64:65, cs_], func=AF.Copy), r=[("ACC", q_, 64) for q_ in range(4)], w=["lrow"])
            P.op("pe", I("matmul", ps[6][0:64, :], ones_f[64:65, 0:64], lrow[64:65, :], start=True, stop=True),
                 r=["lrow", "ones"], w=[PSK(6)])
            P.op("dve", I("reciprocal", out=rbc[:], in_=ps[6][0:64, :]), r=[PSK(6)], w=["rbc"])
            P.op("dve", I("tensor_tensor", out=yf[:], in0=ACC[0:64, cs_], in1=rbc[:], op=ALU.mult), r=[("ACC", q_, 0) for q_ in range(4)] + ["rbc"], w=["yf"])
            P.op("dve", I("tensor_tensor", out=yo[ob][:], in0=yf[:], in1=GA[:, cs_], op=ALU.mult), r=["yf", "GA"], w=[("yo", ob)])
            dma("sp", y_send.ap()[128:192, c * 2048 + 512 * qd:c * 2048 + 512 * qd + 512], yo[ob][:], [("yo", ob)], [("y_send", 4 * c + qd, 2)])
    P.barrier()
    A.off = m0


def t5_bucket_np(dist):
    dist = np.asarray(dist, dtype=np.int64)
    d = np.maximum(dist, 1).astype(np.float32)
    large = 16 + (np.log(d / np.float32(16)) / np.float32(math.log(2048 / 16)) * np.float32(16)).astype(np.int32)
    large = np.minimum(large, 31)
    return np.where(dist < 16, dist, large)


def host_inputs(x, c, positions, w_ada, b_ada, norm_g, w_in, q_norm_g, w_uq, kv_norm_g, w_ukv,
                w_out_a, w_out_b, w_o, rel_bias, final_norm_g):
    S = x.shape[1]
    L = w_in.shape[0]
    xT_full = np.ascontiguousarray(x[0].T)
    f32 = np.float32
    inv_freq = (1.0 / (10000.0 ** (np.arange(0, 32, 2, dtype=f32) / f32(32)))).astype(f32)
    invf = np.tile(np.concatenate([inv_freq, inv_freq]), 4).reshape(128, 1).astype(f32)
    oh5 = np.zeros((32, 3, 2, 256), f32)
    m5 = np.zeros((128, 3, 2, 256), f32)
    for g, r in enumerate((1, 4, 16)):
        for i in range(128):
            oh5[t5_bucket_np(i * r), g, 0, i] = 1.0
            m5[:, g, 0, i] = 1.0
        oh5[t5_bucket_np(128 * r), g, 1, 0] = 1.0
        m5[:, g, 1, 0] = 1.0
        for i in range(129, 256):
            oh5[t5_bucket_np((i - 128) * r), g, 1, i] = 1.0
            m5[:, g, 1, i] = 1.0
    kk = np.arange(128)[:, None]
    cc = np.arange(128)[None, :]
    tri = (kk <= cc).astype(f32).astype(ml_dtypes.bfloat16)
    cT = np.ascontiguousarray(c[0].reshape(8, 128).T)

    def kchunk(w):
        return np.ascontiguousarray(w.reshape(8, 128, -1).transpose(1, 0, 2))

    maps = []
    for j in range(NCORE):
        C = slice(128 * j, 128 * j + 128)
        cols = np.concatenate([np.arange(128 * j, 128 * j + 128), 1024 + np.arange(128 * j, 128 * j + 128), 2048 + np.arange(128 * j, 128 * j + 128)])
        wsel = []
        wsel.append(np.arange(5120, 5120 + 416))
        for g in range(3):
            wsel.append((0 * 3 + g) * 512 + j * 64 + np.arange(64))
            wsel.append((1 * 3 + g) * 512 + j * 64 + np.arange(64))
        for g in range(3):
            wsel.append((2 * 3 + g) * 512 + j * 64 + np.arange(64))
        wsel.append(4608 + j * 64 + np.arange(64))
        wsel.append(5536 + j * 128 + np.arange(128))
        wsel.append(6560 + 128 * j + np.arange(128))
        wsel.append(6560 + 1024 + 128 * j + np.arange(128))
        wsel = np.concatenate(wsel)
        assert wsel.size == NW
        uq_cols = np.concatenate([np.concatenate([hh * 96 + 64 + np.arange(32), hh * 96 + np.arange(64)]) for hh in (2 * j, 2 * j + 1)])
        ukv_cols = np.concatenate([(2 * j) * 128 + np.arange(64), (2 * j + 1) * 128 + np.arange(64),
                                   (2 * j) * 128 + 64 + np.arange(64), (2 * j + 1) * 128 + 64 + np.arange(64)])
        m = {
            "xT": np.ascontiguousarray(xT_full[C]),
            "cT": cT,
            "pos": np.ascontiguousarray(positions.astype(np.int32).reshape(1, S)),
            "invf": invf,
            "w_ada": np.stack([kchunk(w_ada[l][:, cols]) for l in range(L)]),
            "b_ada": np.stack([np.ascontiguousarray(b_ada[l][cols].reshape(3, 128).T) for l in range(L)]),
            "norm_g": np.stack([norm_g[l][C].reshape(128, 1) for l in range(L)]),
            "w_in": np.stack([kchunk(w_in[l][:, wsel]) for l in range(L)]),
            "qng": np.stack([np.ascontiguousarray(q_norm_g[l].reshape(2, 128).T) for l in range(L)]),
            "kvng": np.stack([kv_norm_g[l].reshape(128, 1) for l in range(L)]),
            "w_uq": np.stack([np.ascontiguousarray(w_uq[l][:, uq_cols].reshape(2, 128, 192).transpose(1, 0, 2)) for l in range(L)]),
            "w_ukv": np.stack([np.ascontiguousarray(w_ukv[l][:, ukv_cols]) for l in range(L)]),
            "w_out": np.stack([np.ascontiguousarray(np.concatenate([w_out_b[l][:, C].reshape(8, 128, 128), w_out_a[l][:, C].reshape(4, 128, 128)], 0).transpose(1, 0, 2))
                               for l in range(L)]),
            "w_o": np.stack([kchunk(w_o[l][:, C]) for l in range(L)]),
            "fin_g": final_norm_g[C].reshape(128, 1).astype(f32),
            "rel": np.ascontiguousarray(rel_bias[:, [g * 8 + j for g in range(3)]]),
            "oh5": oh5, "m5": m5, "tri": tri,
        }
        maps.append({k: (v if v.dtype != np.float64 else v.astype(f32)) for k, v in m.items()})
    return maps


_CACHE = {}


def _run(S, L, seg, maps, extra):
    key = (S, L, seg)
    if key not in _CACHE:
        _CACHE[key] = build(S, L, False, None, seg)
    nc = _CACHE[key]
    names = set(SEG_IO[seg[0]][0]) | set(SEG_W[seg[0]]) | set(COMMON)
    in_maps = []
    for j in range(NCORE):
        m = {}
        for k in names:
            if k in extra:
                v = extra[k]
                m[k] = v[j] if isinstance(v, list) else v
            else:
                m[k] = maps[j][k]
        in_maps.append(m)
    res = run_bass_kernel_spmd(nc, in_maps, core_ids=list(range(NCORE)))
    return res.results


def kernel(x, c, positions, w_ada, b_ada, norm_g, w_in, q_norm_g, w_uq, kv_norm_g, w_ukv,
           w_out_a, w_out_b, w_o, rel_bias, final_norm_g, _debug=False, _stop=None):
    args = [np.asarray(a) for a in (x, c, positions, w_ada, b_ada, norm_g, w_in, q_norm_g, w_uq, kv_norm_g, w_ukv,
                                    w_out_a, w_out_b, w_o, rel_bias, final_norm_g)]
    S = args[0].shape[1]
    L = args[6].shape[0]
    maps = host_inputs(*args)
    if FUSED or _debug:
        key = (S, L, _debug, _stop)
        if key not in _CACHE:
            _CACHE[key] = build(S, L, _debug, _stop)
        nc = _CACHE[key]
        res = run_bass_kernel_spmd(nc, maps, core_ids=list(range(NCORE)))
        outT = np.concatenate([res.results[j]["outT"] for j in range(NCORE)], axis=0)
        out = np.ascontiguousarray(outT.T).reshape(1, S, D).astype(np.float32)
        if _debug:
            return out, res
        return out
    gather = lambda r, nm: np.concatenate([r[j][nm] for j in range(NCORE)], axis=0)
    xs = [maps[j]["xT"] for j in range(NCORE)]
    r = _run(S, L, ("ss0", 0), maps, {"xT": xs})
    ss_all = gather(r, "ss_send")
    for l in range(L):
        r = _run(S, L, ("norm", l), maps, {"xT": xs, "ss_all": ss_all})
        hT_all = gather(r, "hT_send")
        r = _run(S, L, ("attn", l), maps, {"hT_all": hT_all})
        y_all = gather(r, "y_send")
        gm = [r[j]["gmg"] for j in range(NCORE)]
        r = _run(S, L, ("mrg", l), maps, {"y_all": y_all, "gmg": gm})
        mg_all = gather(r, "mg_send")
        r = _run(S, L, ("xnew", l), maps, {"mg_all": mg_all, "xT": xs})
        xs = [r[j]["xcur"] for j in range(NCORE)]
        ss_all = gather(r, "ss_send")
    r = _run(S, L, ("final", 0), maps, {"xT": xs, "ss_all": ss_all})
    outT = np.concatenate([r[j]["outT"] for j in range(NCORE)], axis=0)
    return np.ascontiguousarray(outT.T).reshape(1, S, D).astype(np.float32)
```

```python
import math
import numpy as np
import ml_dtypes
import concourse.bass as bass
import concourse.mybir as mybir
from concourse.bass_utils import run_bass_kernel_spmd

F32 = mybir.dt.float32
BF16 = mybir.dt.bfloat16
I32 = mybir.dt.int32
AF = mybir.ActivationFunctionType
ALU = mybir.AluOpType

D = 1024
NCORE = 8
NSLOT = 8
EPS = 1e-6
NW = 1440
TWO_PI = 2.0 * math.pi
import os as _os
PARTS = set(_os.environ.get('PARTS', 'dil,gates,lat,krope,q,k,vm,vd').split(','))
PSTOP = _os.environ.get('PSTOP')
DSTOP = _os.environ.get('DSTOP')


def I(method, *args, **kw):
    return lambda e: getattr(e, method)(*args, **kw)


class Prog:
    STREAMS = ["pe", "act", "dve", "pool", "sp"]

    def __init__(self):
        self.ops = []
        self.lastw = {}
        self.readers = {}
        self.dma_n = {}
        self.slot_last = {}
        self.bar = {}
        self.last_in_class = {}

    def _cls(self, o):
        if o["dma"]:
            return (o["stream"], o["slot"])
        if o["cc"]:
            return "cc"
        return o["stream"]

    def op(self, stream, fn, r=(), w=(), dma=False, cc=False):
        idx = len(self.ops)
        deps = set(self.bar.values())
        for k in r:
            if k in self.lastw:
                deps.add(self.lastw[k])
        for k in w:
            if k in self.lastw:
                deps.add(self.lastw[k])
            deps.update(self.readers.get(k, {}).values())
        o = dict(stream=stream, fn=fn, deps=deps, dma=dma, cc=cc, needed=False, idx=idx)
        if dma:
            n = self.dma_n.get(stream, 0)
            self.dma_n[stream] = n + 1
            o["slot"] = n % NSLOT
            o["slotn"] = n // NSLOT + 1
            prev = self.slot_last.get((stream, o["slot"]))
            if prev is not None:
                deps.add(prev)
            self.slot_last[(stream, o["slot"])] = idx
        c = self._cls(o)
        o["cls"] = c
        for k in w:
            self.lastw[k] = idx
            self.readers[k] = {}
        for k in r:
            self.readers.setdefault(k, {})[c] = idx
        self.last_in_class[c] = idx
        self.ops.append(o)
        return idx

    def barrier(self):
        self.bar = dict(self.last_in_class)
        self.lastw = {}
        self.readers = {}

    def emit(self, nc, block, sems):
        ops = self.ops
        for o in ops:
            for d in o["deps"]:
                ops[d]["needed"] = True
        cnt = {}
        for o in ops:
            if o["dma"]:
                o["val"] = 16 * o["slotn"]
            elif o["needed"]:
                cnt[o["cls"]] = cnt.get(o["cls"], 0) + 1
                o["val"] = cnt[o["cls"]]

        def run(stream, eng):
            waited = {}
            for o in ops:
                if o["stream"] != stream:
                    continue
                need = {}
                for d in o["deps"]:
                    od = ops[d]
                    if od["cls"] == "pe" and stream == "pe" and not o["dma"]:
                        continue
                    c = od["cls"]
                    need[c] = max(need.get(c, 0), od["val"])
                for c, v in need.items():
                    if waited.get(c, 0) < v:
                        eng.wait_ge(sems[c], v)
                        waited[c] = v
                ins = o["fn"](eng)
                if ins is not None and (o["dma"] or o["needed"]):
                    ins.then_inc(sems[o["cls"]], 16 if o["dma"] else 1)

        @block.tensor
        def _(e):
            run("pe", e)

        @block.scalar
        def _(e):
            run("act", e)

        @block.vector
        def _(e):
            run("dve", e)

        @block.gpsimd
        def _(e):
            run("pool", e)

        @block.sync
        def _(e):
            run("sp", e)


SEG_IO = {"ss0": (("xT",), ("ss_send",)), "norm": (("xT", "ss_all"), ("hT_send",)), "attn": (("hT_all",), ("y_send", "gmg")),
          "mrg": (("y_all", "gmg"), ("mg_send",)), "xnew": (("mg_all", "xT"), ("xcur", "ss_send")), "final": (("xT", "ss_all"), ("outT",))}
SEG_W = {"ss0": (), "norm": (), "attn": ("w_in", "w_uq", "w_ukv"), "mrg": ("w_out",), "xnew": ("w_o",), "final": ()}
COMMON = ("cT", "pos", "invf", "fin_g", "rel", "oh5", "m5", "tri", "w_ada", "b_ada", "norm_g", "qng", "kvng")
FUSED = False


class Arena:
    def __init__(self, nc, limit=229344):
        self.nc = nc
        self.off = 16512
        self.limit = limit
        self.n = 0

    def alloc(self, shape, dt):
        esz = 4 if dt in (F32, I32) else 2
        per = esz
        for s in shape[1:]:
            per *= s
        per = (per + 31) // 32 * 32
        assert self.off + per <= self.limit, ("SBUF overflow", self.off, per)
        self.n += 1
        t = self.nc.alloc_sbuf_tensor_at("sb%d" % self.n, list(shape), dt, offset=self.off)
        self.off += per
        return t


def build(S, L=2, debug=False, stop=None, seg=None):
    NT = S // 512
    NB = S // 128
    NCH = S // 2048
    SQ = S // 4
    nc = bass.Bass("TRN2", target_bir_lowering=False)
    P = Prog()

    ext_in = set() if seg is None else set(SEG_IO[seg[0]][0]) | set(SEG_W[seg[0]]) | set(COMMON)
    ext_out = set() if seg is None else set(SEG_IO[seg[0]][1])

    def want(kind, l_):
        return seg is None or (seg[0] == kind and (kind in ("ss0", "final") or seg[1] == l_))

    def din(name, shape, dt):
        if seg is not None and name not in ext_in:
            return nc.dram_tensor(name, shape, dt)
        return nc.dram_tensor(name, shape, dt, kind="ExternalInput")

    xT = din("xT", [128, S], F32)
    cT = din("cT", [128, 8], F32)
    pos = din("pos", [1, S], I32)
    invf = din("invf", [128, 1], F32)
    w_ada = din("w_ada", [L, 128, 8, 384], F32)
    b_ada = din("b_ada", [L, 128, 3], F32)
    norm_g = din("norm_g", [L, 128, 1], F32)
    w_in = din("w_in", [L, 128, 8, NW], F32)
    qng = din("qng", [L, 128, 2], F32)
    kvng = din("kvng", [L, 128, 1], F32)
    w_uq = din("w_uq", [L, 128, 2, 192], F32)
    w_ukv = din("w_ukv", [L, 128, 256], F32)
    w_out = din("w_out", [L, 128, 12, 128], F32)
    w_o = din("w_o", [L, 128, 8, 128], F32)
    fin_g = din("fin_g", [128, 1], F32)
    rel = din("rel", [32, 3], F32)
    oh5 = din("oh5", [32, 3, 2, 256], F32)
    m5 = din("m5", [128, 3, 2, 256], F32)
    tri_in = din("tri", [128, 128], BF16)
    def dscr(name, shape, dt):
        if name in ext_in:
            return nc.dram_tensor(name, shape, dt, kind="ExternalInput")
        if name in ext_out:
            return nc.dram_tensor(name, shape, dt, kind="ExternalOutput")
        return nc.dram_tensor(name, shape, dt)

    outT = nc.dram_tensor("outT", [128, S], F32, kind="ExternalOutput") if seg is None else dscr("outT", [128, S], F32)

    ss_send = dscr("ss_send", [1, S], F32)
    ss_all = dscr("ss_all", [8, S], F32)
    hT_send = dscr("hT_send", [128, S], BF16)
    hT_all = dscr("hT_all", [1024, S], BF16)
    y_send = dscr("y_send", [192, S], BF16)
    y_all = dscr("y_all", [1536, S], BF16)
    mg_send = dscr("mg_send", [128, S], BF16)
    mg_all = dscr("mg_all", [1024, S], BF16)
    xcur = dscr("xcur", [128, S], F32)
    qT = dscr("qT", [2, 96, S], BF16)
    kT = dscr("kT", [2, 96, S], BF16)
    vm = dscr("vm", [S, 128], BF16)
    dqk = dscr("dqk", [3, 128, S], BF16)
    dv = dscr("dv", [S, 192], BF16)
    gaz = dscr("gaz", [64, S], BF16)
    gmz = dscr("gmz", [128, S], BF16)
    gmg = dscr("gmg", [2, 128, S], BF16)
    cs = dscr("cs", [2, 32, S], F32)
    t5w = dscr("t5w", [3, 2, 128, 256], F32)
    dbg = {}
    if debug:
        for nm, shp, dt in (("dbg_hT", [128, S], BF16), ("dbg_y", [192, S], BF16), ("dbg_x", [128, S], F32),
                            ("dbg_q", [2, 96, S], BF16), ("dbg_k", [2, 96, S], BF16), ("dbg_dqk", [3, 128, S], BF16),
                            ("dbg_dv", [S, 192], BF16), ("dbg_T5", [128, 768], F32), ("dbg_gaz", [64, S], BF16),
                            ("dbg_hT2", [128, S], BF16), ("dbg_y2", [192, S], BF16), ("dbg_x2", [128, S], F32), ("dbg_mg2", [128, S], BF16)):
            dbg[nm] = nc.dram_tensor(nm, shp, dt, kind="ExternalOutput")

    def allk(name):
        return [(name, t) for t in range(NT)]

    A = Arena(nc)
    ps = [nc.alloc_psum_tensor("psb%d" % i, [128, 512], F32) for i in range(8)]

    def PSK(i):
        return ("ps", i)

    ones_f = A.alloc([128, 128], F32)
    tri = A.alloc([128, 128], BF16)
    T5 = A.alloc([128, 3, 2, 128], F32)
    modc = A.alloc([128, 8], F32)
    sc = A.alloc([128, 8], F32)
    small = A.alloc([128, 16], F32)
    base_off = A.off

    def dma(stream, out, in_, r, w):
        P.op(stream, I("dma_start", out=out, in_=in_), r=r, w=w, dma=True)

    def ag(send, recv, name_s, name_r):
        if seg is not None:
            return
        P.op("pool", I("collective_compute", "AllGather", ALU.bypass, replica_groups=[list(range(NCORE))],
                                                     ins=[send.ap()], outs=[recv.ap()]),
             r=allk(name_s), w=allk(name_r), cc=True)

    P.op("dve", I("memset", ones_f[:], 1.0), w=["ones"])
    dma("sp", tri[:], tri_in.ap(), [], ["tri"])
    dma("sp", small[:, 4:5], fin_g.ap(), [], ["small4"])
    dma("sp", small[:, 5:6], invf.ap(), [], ["small5"])
    ct = A.alloc([128, 8], F32)
    dma("sp", ct[:], cT.ap(), [], ["ct"])
    sg = A.alloc([128, 8], F32)
    P.op("act", I("activation", out=sg[:], in_=ct[:], func=AF.Sigmoid), r=["ct"], w=["sg"])
    P.op("dve", I("tensor_tensor", out=sc[:], in0=ct[:], in1=sg[:], op=ALU.mult), r=["ct", "sg"], w=["sc"])

    if seg is None or seg[0] == "attn":
        relt = A.alloc([32, 3], F32)
        oh = A.alloc([32, 3, 2, 256], F32)
        m5t = A.alloc([128, 3, 2, 256], F32)
        relbc = A.alloc([32, 3, 128], F32)
        f5 = A.alloc([128, 3, 2, 256], F32)
        dma("sp", relt[:], rel.ap(), [], ["relt"])
        dma("sp", oh[:], oh5.ap(), [], ["oh"])
        dma("sp", m5t[:], m5.ap(), [], ["m5t"])
        for g in range(3):
            P.op("dve", I("tensor_scalar", out=relbc[:, g, :], in0=ones_f[0:32, :], scalar1=relt[:, g:g + 1],
                                                       scalar2=None, op0=ALU.mult), r=["ones", "relt"], w=[("relbc", g)])
            for kd in range(2):
                b = (g * 2 + kd) % 2
                P.op("pe", I("matmul", ps[b][:, 0:256], relbc[:, g, :], oh[:, g, kd, :], start=True, stop=True),
                     r=[("relbc", g), "oh"], w=[PSK(b)])
                P.op("act", I("activation", out=f5[:, g, kd, :], in_=ps[b][:, 0:256], func=AF.Exp),
                     r=[PSK(b)], w=[("f5", g, kd)])
                P.op("dve", I("tensor_tensor", out=f5[:, g, kd, :], in0=f5[:, g, kd, :], in1=m5t[:, g, kd, :], op=ALU.mult),
                     r=[("f5", g, kd), "m5t"], w=[("f5", g, kd)])
                dma("sp", t5w.ap()[g, kd], f5[:, g, kd, :], [("f5", g, kd)], [("t5w", g, kd)])
                src = bass.AP(tensor=t5w, offset=(g * 2 + kd) * 128 * 256, ap=[[255, 128], [1, 128]])
                dma("sp", T5[:, g, kd, :], src, [("t5w", g, kd)], [("T5", g, kd)])

        posi = A.alloc([128, SQ], I32)
        ang = A.alloc([128, SQ], F32)
        kf = A.alloc([128, SQ], F32)
        rr = A.alloc([128, SQ], F32)
        tmp = A.alloc([128, SQ], F32)
        ki = posi
        for q in range(4):
            src = bass.AP(tensor=pos, offset=q * SQ, ap=[[0, 32], [1, SQ]])
            dma("sp", posi[32 * q:32 * q + 32, :], src, [], [("posi", q)])
        pk = [("posi", q) for q in range(4)]
        P.op("dve", I("tensor_copy", out=ang[:], in_=posi[:]), r=pk, w=["ang"])
        P.op("dve", I("tensor_scalar", out=ang[:], in0=ang[:], scalar1=small[:, 5:6], scalar2=None, op0=ALU.mult),
             r=["ang", "small5"], w=["ang"])
        P.op("dve", I("tensor_scalar", out=kf[:], in0=ang[:], scalar1=1.0 / TWO_PI, scalar2=None, op0=ALU.mult), r=["ang"], w=["kf"])
        P.op("dve", I("tensor_copy", out=ki[:], in_=kf[:]), r=["kf"] + pk, w=["ki"])
        P.op("dve", I("tensor_copy", out=kf[:], in_=ki[:]), r=["ki"], w=["kf"])
        C1 = 6.28125
        C2 = TWO_PI - C1
        P.op("dve", I("scalar_tensor_tensor", out=rr[:], in0=kf[:], scalar=-C1, in1=ang[:], op0=ALU.mult, op1=ALU.add),
             r=["kf", "ang"], w=["rr"])
        P.op("dve", I("scalar_tensor_tensor", out=rr[:], in0=kf[:], scalar=-C2, in1=rr[:], op0=ALU.mult, op1=ALU.add),
             r=["kf", "rr"], w=["rr"])

        def fold(buf, key):
            P.op("dve", I("tensor_scalar", out=tmp[:], in0=buf[:], scalar1=math.pi, scalar2=-TWO_PI, op0=ALU.is_gt, op1=ALU.mult),
                 r=[key], w=["tmp"])
            P.op("dve", I("tensor_tensor", out=buf[:], in0=buf[:], in1=tmp[:], op=ALU.add), r=[key, "tmp"], w=[key])
            P.op("dve", I("tensor_scalar", out=tmp[:], in0=buf[:], scalar1=-math.pi, scalar2=TWO_PI, op0=ALU.is_lt, op1=ALU.mult),
                 r=[key], w=["tmp"])
            P.op("dve", I("tensor_tensor", out=buf[:], in0=buf[:], in1=tmp[:], op=ALU.add), r=[key, "tmp"], w=[key])
            P.op("dve", I("tensor_scalar", out=buf[:], in0=buf[:], scalar1=math.pi, scalar2=-math.pi, op0=ALU.min, op1=ALU.max),
                 r=[key], w=[key])

        fold(rr, "rr")
        P.op("act", I("activation", out=kf[:], in_=rr[:], func=AF.Sin), r=["rr"], w=["kf"])
        for q in range(4):
            dma("sp", cs.ap()[1, :, q * SQ:(q + 1) * SQ], kf[32 * q:32 * q + 32, :], ["kf"], [("cs", 1, q)])
        P.op("dve", I("tensor_scalar", out=rr[:], in0=rr[:], scalar1=math.pi / 2, scalar2=None, op0=ALU.add), r=["rr"], w=["rr"])
        fold(rr, "rr")
        P.op("act", I("activation", out=ang[:], in_=rr[:], func=AF.Sin), r=["rr"], w=["ang"])
        for q in range(4):
            dma("sp", cs.ap()[0, :, q * SQ:(q + 1) * SQ], ang[32 * q:32 * q + 32, :], ["ang"], [("cs", 0, q)])
    P.barrier()
    A.off = base_off
    STOPPED = [False]

    def chk(name):
        if stop == name:
            STOPPED[0] = True
        return STOPPED[0]

    def sumsq_seg(src_dram, src_name):
        m0 = A.off
        xt = [A.alloc([128, 512], F32) for _ in range(2)]
        sq = [A.alloc([128, 512], F32) for _ in range(2)]
        row = [A.alloc([1, 512], F32) for _ in range(2)]
        for t in range(NT):
            b = t % 2
            sl = slice(t * 512, (t + 1) * 512)
            dma("sp", xt[b][:], src_dram.ap()[:, sl], [(src_name, t)], [("xt", b)])
            P.op("dve", I("tensor_tensor", out=sq[b][:], in0=xt[b][:], in1=xt[b][:], op=ALU.mult), r=[("xt", b)], w=[("sq", b)])
            P.op("pe", I("matmul", ps[b][0:1, :], ones_f[:, 0:1], sq[b][:], start=True, stop=True), r=[("sq", b), "ones"], w=[PSK(b)])
            P.op("act", I("activation", out=row[b][:], in_=ps[b][0:1, :], func=AF.Copy), r=[PSK(b)], w=[("row", b)])
            dma("sp", ss_send.ap()[:, sl], row[b][:], [("row", b)], [("ss_send", t)])
        ag(ss_send, ss_all, "ss_send", "ss_all")
        P.barrier()
        A.off = m0

    def norm_seg(src_dram, src_name, dst_dram, dst_name, dst_dt, mulcol, addcol):
        m0 = A.off
        xt = [A.alloc([128, 512], F32) for _ in range(2)]
        s8 = [A.alloc([8, 512], F32) for _ in range(2)]
        rs = [A.alloc([128, 512], F32) for _ in range(2)]
        ot = [A.alloc([128, 512], dst_dt) for _ in range(2)]
        for t in range(NT):
            b = t % 2
            sl = slice(t * 512, (t + 1) * 512)
            dma("sp", xt[b][:], src_dram.ap()[:, sl], [(src_name, t)], [("xt", b)])
            dma("sp", s8[b][:], ss_all.ap()[:, sl], [("ss_all", t)], [("s8", b)])
            P.op("pe", I("matmul", ps[b][:, :], ones_f[0:8, :], s8[b][:], start=True, stop=True), r=[("s8", b), "ones"], w=[PSK(b)])
            P.op("dve", I("tensor_scalar", out=rs[b][:], in0=ps[b][:, :], scalar1=1.0 / D, scalar2=EPS, op0=ALU.mult, op1=ALU.add),
                 r=[PSK(b)], w=[("rs", b)])
            P.op("act", I("activation", out=rs[b][:], in_=rs[b][:], func=AF.Sqrt), r=[("rs", b)], w=[("rs", b)])
            P.op("dve", I("reciprocal", out=rs[b][:], in_=rs[b][:]), r=[("rs", b)], w=[("rs", b)])
            P.op("dve", I("tensor_tensor", out=xt[b][:], in0=xt[b][:], in1=rs[b][:], op=ALU.mult), r=[("xt", b), ("rs", b)], w=[("xt", b)])
            if addcol is None:
                P.op("dve", I("tensor_scalar", out=ot[b][:], in0=xt[b][:], scalar1=mulcol, scalar2=None, op0=ALU.mult),
                     r=[("xt", b), "modc", "small4"], w=[("ot", b)])
            else:
                P.op("dve", I("tensor_scalar", out=ot[b][:], in0=xt[b][:], scalar1=mulcol, scalar2=addcol, op0=ALU.mult, op1=ALU.add),
                     r=[("xt", b), "modc"], w=[("ot", b)])
            dma("sp", dst_dram.ap()[:, sl], ot[b][:], [("ot", b)], [(dst_name, t)])
        P.barrier()
        A.off = m0

    def load_cast(dst, src_ap, shape, key, parts=128):
        m0 = A.off
        st = A.alloc(shape, F32)
        dma("sp", st[0:parts], src_ap, [], [("stg", key)])
        P.op("pool", I("tensor_copy", out=dst, in_=st[0:parts]), r=[("stg", key)], w=[key])
        return m0

    x_dram, x_name = xT, "xT"
    if want("ss0", 0) and not chk("init"):
        sumsq_seg(x_dram, x_name)
    for l in range(L):
        if seg is not None and (seg[0] in ("ss0", "final") or seg[1] != l):
            continue
        if chk("sumsq"):
            break
        if seg is None or seg[0] != "mrg":
            m0 = A.off
            wa = A.alloc([128, 8, 384], F32)
            ba = A.alloc([128, 3], F32)
            dma("sp", wa[:], w_ada.ap()[l], [], ["wa"])
            dma("sp", ba[:], b_ada.ap()[l], [], ["ba"])
            dma("sp", small[:, 0:1], norm_g.ap()[l], [], ["small0"])
            dma("sp", small[:, 1:3], qng.ap()[l], [], ["small1"])
            dma("sp", small[:, 3:4], kvng.ap()[l], [], ["small3"])
            for grp in range(3):
                for kc in range(8):
                    P.op("pe", I("matmul", ps[0][:, grp:grp + 1], wa[:, kc, grp * 128:(grp + 1) * 128], sc[:, kc:kc + 1],
                                                                  start=(kc == 0), stop=(kc == 7)), r=["wa", "sc"], w=[PSK(0)])
            P.op("dve", I("tensor_tensor", out=modc[:, 0:3], in0=ps[0][:, 0:3], in1=ba[:], op=ALU.add), r=[PSK(0), "ba"], w=["modc"])
            P.op("dve", I("scalar_tensor_tensor", out=modc[:, 3:4], in0=modc[:, 1:2], scalar=1.0, in1=small[:, 0:1], op0=ALU.add, op1=ALU.mult),
                 r=["modc", "small0"], w=["modc"])
            P.barrier()
            A.off = m0
        if want("norm", l):
            norm_seg(x_dram, x_name, hT_send, "hT_send", BF16, modc[:, 3:4], modc[:, 0:1])
            ag(hT_send, hT_all, "hT_send", "hT_all")
            if debug and l == 0:
                dma("sp", dbg["dbg_hT"].ap(), hT_send.ap(), allk("hT_send"), [])
            if debug and l == 1:
                dma("sp", dbg["dbg_hT2"].ap(), hT_send.ap(), allk("hT_send"), [])
            P.barrier()
            if chk("norm"):
                break
        if want("attn", l):
            proj_seg(nc, P, A, ps, locals())
            if chk("proj"):
                break
            mla_seg(nc, P, A, ps, locals())
            if debug and l == 0 and stop == "mla":
                dma("sp", dbg["dbg_y"].ap(), y_send.ap(), [], [])
                dma("sp", dbg["dbg_q"].ap(), qT.ap(), [], [])
                dma("sp", dbg["dbg_k"].ap(), kT.ap(), [], [])
                P.barrier()
            if chk("mla"):
                break
            dil_seg(nc, P, A, ps, locals())
            if chk("dil"):
                break
            ag(y_send, y_all, "y_send", "y_all")
            if debug and l == 0:
                dma("sp", dbg["dbg_y"].ap(), y_send.ap(), allk("y_send"), [])
                dma("sp", dbg["dbg_q"].ap(), qT.ap(), allk("qT0") + allk("qT1"), [])
                dma("sp", dbg["dbg_k"].ap(), kT.ap(), allk("kT0") + allk("kT1"), [])
                dma("sp", dbg["dbg_dqk"].ap(), dqk.ap(), [], [])
                dma("sp", dbg["dbg_dv"].ap(), dv.ap(), [], [])
                dma("sp", dbg["dbg_gaz"].ap(), gaz.ap(), [], [])
                dma("sp", dbg["dbg_T5"].ap(), T5[:].rearrange("p a b c -> p (a b c)"), [], [])
            if debug and l == 1:
                dma("sp", dbg["dbg_y2"].ap(), y_send.ap(), allk("y_send"), [])
            P.barrier()
        if want("mrg", l):
            m0 = A.off
            wo = A.alloc([128, 12, 128], BF16)
            load_cast(wo[:], w_out.ap()[l], [128, 12, 128], "wo")
            ym = [A.alloc([128, 8, 512], BF16) for _ in range(2)]
            yd = [A.alloc([128, 4, 512], BF16) for _ in range(2)]
            gg = [A.alloc([128, 2, 512], BF16) for _ in range(2)]
            t1 = [A.alloc([128, 512], F32) for _ in range(2)]
            t2 = [A.alloc([128, 512], F32) for _ in range(2)]
            mo = [A.alloc([128, 512], BF16) for _ in range(2)]
            for t in range(NT):
                b = t % 2
                sl = slice(t * 512, (t + 1) * 512)
                src = bass.AP(tensor=y_all, offset=t * 512, ap=[[S, 128], [192 * S, 8], [1, 512]])
                dma("sp", ym[b][:], src, [("y_all", t)], [("ym", b)])
                for hf in range(2):
                    src = bass.AP(tensor=y_all, offset=(hf * 192 + 128) * S + t * 512, ap=[[S, 64], [384 * S, 4], [1, 512]])
                    dma("sp", yd[b][64 * hf:64 * hf + 64], src, [("y_all", t)], [("yd", b, hf)])
                src = bass.AP(tensor=gmg, offset=t * 512, ap=[[S, 128], [128 * S, 2], [1, 512]])
                dma("sp", gg[b][:], src, [("gmg", t)], [("gg", b)])
                pa, pb = 2 * b, 2 * b + 1
                for c in range(4):
                    P.op("pe", I("matmul", ps[pa][:, :], wo[:, 8 + c, :], yd[b][:, c, :], start=(c == 0), stop=(c == 3)),
                         r=["wo", ("yd", b, 0), ("yd", b, 1)], w=[PSK(pa)])
                for r_ in range(8):
                    P.op("pe", I("matmul", ps[pb][:, :], wo[:, r_, :], ym[b][:, r_, :], start=(r_ == 0), stop=(r_ == 7)),
                         r=["wo", ("ym", b)], w=[PSK(pb)])
                P.op("dve", I("tensor_tensor", out=t1[b][:], in0=ps[pa][:, :], in1=gg[b][:, 0, :], op=ALU.mult),
                     r=[PSK(pa), ("gg", b)], w=[("t1", b)])
                P.op("dve", I("tensor_tensor", out=t2[b][:], in0=ps[pb][:, :], in1=gg[b][:, 1, :], op=ALU.mult),
                     r=[PSK(pb), ("gg", b)], w=[("t2", b)])
                P.op("pool", I("tensor_tensor", out=mo[b][:], in0=t1[b][:], in1=t2[b][:], op=ALU.add),
                     r=[("t1", b), ("t2", b)], w=[("mo", b)])
                dma("sp", mg_send.ap()[:, sl], mo[b][:], [("mo", b)], [("mg_send", t)])
            ag(mg_send, mg_all, "mg_send", "mg_all")
            P.barrier()
            A.off = m0
        if want("xnew", l):
            m0 = A.off
            wot = A.alloc([128, 8, 128], BF16)
            load_cast(wot[:], w_o.ap()[l], [128, 8, 128], "wot")
            mt = [A.alloc([128, 8, 512], BF16) for _ in range(2)]
            xt = [A.alloc([128, 512], F32) for _ in range(2)]
            xn = [A.alloc([128, 512], F32) for _ in range(2)]
            sq = [A.alloc([128, 512], F32) for _ in range(2)]
            row = [A.alloc([1, 512], F32) for _ in range(2)]
            for t in range(NT):
                b = t % 2
                sl = slice(t * 512, (t + 1) * 512)
                src = bass.AP(tensor=mg_all, offset=t * 512, ap=[[S, 128], [128 * S, 8], [1, 512]])
                dma("sp", mt[b][:], src, [("mg_all", t)], [("mt", b)])
                dma("sp", xt[b][:], x_dram.ap()[:, sl], [(x_name, t)], [("xt", b)])
                pa, pb = 2 * b, 2 * b + 1
                for kc in range(8):
                    P.op("pe", I("matmul", ps[pa][:, :], wot[:, kc, :], mt[b][:, kc, :], start=(kc == 0), stop=(kc == 7)),
                         r=["wot", ("mt", b)], w=[PSK(pa)])
                P.op("dve", I("scalar_tensor_tensor", out=xn[b][:], in0=ps[pa][:, :], scalar=modc[:, 2:3], in1=xt[b][:],
                                                                         op0=ALU.mult, op1=ALU.add), r=[PSK(pa), ("xt", b), "modc"], w=[("xn", b)])
                dma("sp", xcur.ap()[:, sl], xn[b][:], [("xn", b)], [("xcur", t)])
                P.op("pool", I("tensor_tensor", out=sq[b][:], in0=xn[b][:], in1=xn[b][:], op=ALU.mult), r=[("xn", b)], w=[("sq", b)])
                P.op("pe", I("matmul", ps[pb][0:1, :], ones_f[:, 0:1], sq[b][:], start=True, stop=True), r=[("sq", b), "ones"], w=[PSK(pb)])
                P.op("act", I("activation", out=row[b][:], in_=ps[pb][0:1, :], func=AF.Copy), r=[PSK(pb)], w=[("row", b)])
                dma("sp", ss_send.ap()[:, sl], row[b][:], [("row", b)], [("ss_send", t)])
            ag(ss_send, ss_all, "ss_send", "ss_all")
            P.barrier()
            A.off = m0
        if seg is None:
            x_dram, x_name = xcur, "xcur"
        if debug and l == 0:
            dma("sp", dbg["dbg_x"].ap(), xcur.ap(), allk("xcur"), [])
            P.barrier()
        if debug and l == 1:
            dma("sp", dbg["dbg_x2"].ap(), xcur.ap(), allk("xcur"), [])
            dma("sp", dbg["dbg_mg2"].ap(), mg_send.ap(), [], [])
            P.barrier()
    if want("final", 0) and not STOPPED[0]:
        norm_seg(x_dram, x_name, outT, "outT", F32, small[:, 4:5], None)
    P.barrier()
    P.op("sp", lambda e: None, r=[])

    sem_names = ["pe", "act", "dve", "pool", "sp", "cc"] + [("sp", i) for i in range(NSLOT)] + [("pool", i) for i in range(NSLOT)]
    sems = {}
    import contextlib
    with contextlib.ExitStack() as stk:
        for i, nm in enumerate(sem_names):
            sems[nm] = stk.enter_context(nc.semaphore("s%d" % i))
        block = stk.enter_context(nc.Block())
        P.emit(nc, block, sems)
    return nc


def proj_seg(nc, P, A, ps, env):
    S, NT, l = env["S"], env["NT"], env["l"]
    dma, load_cast = env["dma"], env["load_cast"]
    small, ones_f = env["small"], env["ones_f"]
    hT_all, w_in, w_uq, w_ukv, cs = env["hT_all"], env["w_in"], env["w_uq"], env["w_ukv"], env["cs"]
    qT, kT, vm, dqk, dv, gaz, gmz, gmg = (env[k] for k in ("qT", "kT", "vm", "dqk", "dv", "gaz", "gmz", "gmg"))
    PSK = env["PSK"]
    m0 = A.off
    win = A.alloc([128, 8, NW + 32], BF16)
    for pc in range(4):
        mm = A.off
        st = A.alloc([128, 8, 360], F32)
        dma("sp", st[:], w_in.ap()[l][:, :, pc * 360:(pc + 1) * 360], [], [("wst", pc)])
        P.op("pool" if pc % 2 else "dve", I("tensor_copy", out=win[:, :, pc * 360:(pc + 1) * 360], in_=st[:]),
             r=[("wst", pc)], w=[("win", pc)])
    winK = [("win", pc) for pc in range(4)]
    P.op("dve", I("tensor_scalar", out=win[:, :, NW:NW + 16], in0=win[:, :, 400:416], scalar1=-1.0, scalar2=None, op0=ALU.mult),
         r=winK, w=["winrot"])
    P.op("dve", I("tensor_copy", out=win[:, :, NW + 16:NW + 32], in_=win[:, :, 384:400]), r=winK, w=["winrot2"])
    winK = winK + ["winrot", "winrot2"]
    wuq = A.alloc([128, 2, 192 + 64], BF16)
    st = A.alloc([128, 2, 192], F32)
    dma("sp", st[:], w_uq.ap()[l], [], ["wuqs"])
    P.op("dve", I("tensor_copy", out=wuq[:, :, 0:192], in_=st[:]), r=["wuqs"], w=["wuq"])
    for h in range(2):
        P.op("dve", I("tensor_scalar", out=wuq[:, :, 192 + 32 * h:192 + 32 * h + 16], in0=st[:, :, 96 * h + 16:96 * h + 32],
                                                   scalar1=-1.0, scalar2=None, op0=ALU.mult), r=["wuqs"], w=[("wuqr", h)])
        P.op("dve", I("tensor_copy", out=wuq[:, :, 192 + 32 * h + 16:192 + 32 * h + 32], in_=st[:, :, 96 * h:96 * h + 16]),
             r=["wuqs"], w=[("wuqr2", h)])
    wuqK = ["wuq"] + [("wuqr", h) for h in range(2)] + [("wuqr2", h) for h in range(2)]
    wkv = A.alloc([128, 256], BF16)
    st2 = A.alloc([128, 256], F32)
    dma("sp", st2[:], w_ukv.ap()[l], [], ["wkvs"])
    P.op("dve", I("tensor_copy", out=wkv[:], in_=st2[:]), r=["wkvs"], w=["wkv"])

    NBUF = 2
    hT = [A.alloc([128, 8, 512], BF16) for _ in range(NBUF)]
    csq = [A.alloc([32, 2, 512], F32) for _ in range(NBUF)]
    oq = [A.alloc([128, 3, 512], BF16) for _ in range(NBUF)]
    ogz = [A.alloc([128, 4, 512], BF16) for _ in range(NBUF)]
    sgt = [A.alloc([128, 512], F32) for _ in range(NBUF)]
    cq2 = [A.alloc([128, 2, 512], F32) for _ in range(NBUF)]
    rsq = [A.alloc([128, 512], F32) for _ in range(NBUF)]
    rsk = [A.alloc([128, 512], F32) for _ in range(NBUF)]
    cqn = [A.alloc([128, 2, 512], BF16) for _ in range(NBUF)]
    ckn = [A.alloc([128, 512], BF16) for _ in range(NBUF)]
    krp = [A.alloc([32, 512], BF16) for _ in range(NBUF)]
    rt1 = [A.alloc([32, 512], F32) for _ in range(NBUF)]
    rt2 = [A.alloc([32, 512], F32) for _ in range(NBUF)]
    qo = [A.alloc([96, 2, 512], BF16) for _ in range(NBUF)]
    ko = [A.alloc([64, 2, 512], BF16) for _ in range(NBUF)]
    vdo = [A.alloc([128, 4, 192], BF16) for _ in range(NBUF)]
    vmo = [A.alloc([128, 4, 128], BF16) for _ in range(NBUF)]

    CQ, CKV, KR, DQK, DV, AZ, MZ, MG = 0, 256, 384, 416, 800, 992, 1056, 1184
    pcnt = [0]

    def nps():
        pcnt[0] = (pcnt[0] + 1) % 8
        return pcnt[0]

    def mm_feat(b, pi, rows, c0, ncols=None):
        for kc in range(8):
            P.op("pe", I("matmul", ps[pi][0:rows, :], win[:, kc, c0:c0 + rows], hT[b][:, kc, :], start=(kc == 0), stop=(kc == 7)),
                 r=winK + [("hT", b)], w=[PSK(pi)])

    for t in range(NT):
        b = t % NBUF
        sl = slice(t * 512, (t + 1) * 512)
        src = bass.AP(tensor=hT_all, offset=t * 512, ap=[[S, 128], [128 * S, 8], [1, 512]])
        dma("sp", hT[b][:], src, [("hT_all", t)], [("hT", b)])
        src = bass.AP(tensor=cs, offset=t * 512, ap=[[S, 32], [32 * S, 2], [1, 512]])
        dma("sp", csq[b][:], src, [("cs", 0, q) for q in range(4)] + [("cs", 1, q) for q in range(4)], [("csq", b)])
        for g, r_ in (enumerate((1, 4, 16)) if 'dil' in PARTS else []):
            pi = nps()
            mm_feat(b, pi, 128, DQK + 128 * g)
            if r_ == 1:
                P.op("act", I("activation", out=oq[b][:, g, :], in_=ps[pi][:, :], func=AF.Copy), r=[PSK(pi)], w=[("oq", b, g)])
            else:
                nm = 512 // r_
                inv = bass.AP(tensor=ps[pi], offset=0, ap=[[512, 128], [1, r_], [r_, nm]])
                if r_ == 4:
                    outv = bass.AP(tensor=oq[b], offset=g * 512, ap=[[1536, 128], [128, 4], [1, 128]])
                else:
                    outv = None
                if r_ == 4:
                    P.op("act", I("activation", out=outv, in_=inv, func=AF.Copy), r=[PSK(pi)], w=[("oq", b, g)])
                else:
                    outv = bass.AP(tensor=oq[b], offset=g * 512, ap=[[1536, 128], [32, 16], [1, 32]])
                    P.op("act", I("activation", out=outv, in_=inv, func=AF.Copy), r=[PSK(pi)], w=[("oq", b, g)])
            if r_ == 16:
                ch, sub = t // 4, t % 4
                dst = bass.AP(tensor=dqk, offset=g * 128 * S + ch * 2048 + 32 * sub, ap=[[S, 128], [128, 16], [1, 32]])
                srcv = bass.AP(tensor=oq[b], offset=g * 512, ap=[[1536, 128], [32, 16], [1, 32]])
                dma("sp", dst, srcv, [("oq", b, g)], [("dqk", g, t)])
            else:
                dma("sp", dqk.ap()[g, :, sl], oq[b][:, g, :], [("oq", b, g)], [("dqk", g, t)])
        for gi, (c0, rows, dst, nm) in enumerate(() if 'gates' not in PARTS else ((AZ, 64, gaz.ap()[:, sl], "gaz"), (MZ, 128, gmz.ap()[:, sl], "gmz"),
                                                 (MG, 128, gmg.ap()[0, :, sl], "gmg0"), (MG + 128, 128, gmg.ap()[1, :, sl], "gmg1"))):
            pi = nps()
            mm_feat(b, pi, rows, c0)
            if gi < 2:
                P.op("act", I("activation", out=sgt[b][0:rows, :], in_=ps[pi][0:rows, :], func=AF.Sigmoid),
                     r=[PSK(pi)], w=[("sgt", b)])
                P.op("dve", I("tensor_tensor", out=ogz[b][0:rows, gi, :], in0=ps[pi][0:rows, :], in1=sgt[b][0:rows, :], op=ALU.mult),
                     r=[PSK(pi), ("sgt", b)], w=[("ogz", b, gi)])
            else:
                P.op("act", I("activation", out=ogz[b][:, gi, :], in_=ps[pi][:, :], func=AF.Sigmoid), r=[PSK(pi)], w=[("ogz", b, gi)])
            dma("sp", dst, ogz[b][0:rows, gi, :], [("ogz", b, gi)], [(nm, t)])
        if PSTOP == 'gates':
            continue
        pq = [nps(), nps()]
        for i in range(2):
            mm_feat(b, pq[i], 128, CQ + 128 * i)
            P.op("act", I("activation", out=cq2[b][:, i, :], in_=ps[pq[i]][:, :], func=AF.Square),
                 r=[PSK(pq[i])], w=[("cq2", b, i)])
        pr = nps()
        for i in range(2):
            P.op("pe", I("matmul", ps[pr][:, :], ones_f[:, :], cq2[b][:, i, :], start=(i == 0), stop=(i == 1)),
                 r=["ones", ("cq2", b, 0), ("cq2", b, 1)], w=[PSK(pr)])
        P.op("dve", I("tensor_scalar", out=rsq[b][:], in0=ps[pr][:, :], scalar1=1.0 / 256, scalar2=EPS, op0=ALU.mult, op1=ALU.add),
             r=[PSK(pr)], w=[("rsq", b)])
        P.op("act", I("activation", out=rsq[b][:], in_=rsq[b][:], func=AF.Sqrt), r=[("rsq", b)], w=[("rsq", b)])
        P.op("dve", I("reciprocal", out=rsq[b][:], in_=rsq[b][:]), r=[("rsq", b)], w=[("rsq", b)])
        for i in range(2):
            P.op("dve", I("scalar_tensor_tensor", out=cqn[b][:, i, :], in0=ps[pq[i]][:, :], scalar=small[:, 1 + i:2 + i], in1=rsq[b][:],
                                                              op0=ALU.mult, op1=ALU.mult), r=[PSK(pq[i]), ("rsq", b), "small1"], w=[("cqn", b, i)])
        pk_ = nps()
        mm_feat(b, pk_, 128, CKV)
        P.op("act", I("activation", out=cq2[b][:, 0, :], in_=ps[pk_][:, :], func=AF.Square),
             r=[PSK(pk_)], w=[("cq2", b, 0)])
        pr2 = nps()
        P.op("pe", I("matmul", ps[pr2][:, :], ones_f[:, :], cq2[b][:, 0, :], start=True, stop=True),
             r=["ones", ("cq2", b, 0)], w=[PSK(pr2)])
        P.op("dve", I("tensor_scalar", out=rsk[b][:], in0=ps[pr2][:, :], scalar1=1.0 / 128, scalar2=EPS, op0=ALU.mult, op1=ALU.add),
             r=[PSK(pr2)], w=[("rsk", b)])
        P.op("act", I("activation", out=rsk[b][:], in_=rsk[b][:], func=AF.Sqrt), r=[("rsk", b)], w=[("rsk", b)])
        P.op("dve", I("reciprocal", out=rsk[b][:], in_=rsk[b][:]), r=[("rsk", b)], w=[("rsk", b)])
        P.op("dve", I("scalar_tensor_tensor", out=ckn[b][:], in0=ps[pk_][:, :], scalar=small[:, 3:4], in1=rsk[b][:],
                                                              op0=ALU.mult, op1=ALU.mult), r=[PSK(pk_), ("rsk", b), "small3"], w=[("ckn", b)])
        if PSTOP == 'lat':
            continue
        p1, p2 = nps(), nps()
        for kc in range(8):
            P.op("pe", I("matmul", ps[p1][0:32, :], win[:, kc, KR:KR + 32], hT[b][:, kc, :], start=(kc == 0), stop=(kc == 7)),
                 r=winK + [("hT", b)], w=[PSK(p1)])
        for kc in range(8):
            P.op("pe", I("matmul", ps[p2][0:32, :], win[:, kc, NW:NW + 32], hT[b][:, kc, :], start=(kc == 0), stop=(kc == 7)),
                 r=winK + [("hT", b)], w=[PSK(p2)])
        P.op("dve", I("tensor_tensor", out=rt1[b][:], in0=ps[p1][0:32, :], in1=csq[b][:, 0, :], op=ALU.mult), r=[PSK(p1), ("csq", b)], w=[("rt1", b)])
        P.op("dve", I("tensor_tensor", out=rt2[b][:], in0=ps[p2][0:32, :], in1=csq[b][:, 1, :], op=ALU.mult), r=[PSK(p2), ("csq", b)], w=[("rt2", b)])
        P.op("dve", I("tensor_tensor", out=krp[b][:], in0=rt1[b][:], in1=rt2[b][:], op=ALU.add), r=[("rt1", b), ("rt2", b)], w=[("krp", b)])
        for h in range(2):
            dma("sp", kT.ap()[h, 0:32, sl], krp[b][:], [("krp", b)], [("kT%d" % h, t, "r")])
        if PSTOP == 'krope':
            continue
        for h in range(2):
            pa_, pb_ = nps(), nps()
            for i in range(2):
                P.op("pe", I("matmul", ps[pa_][0:96, :], wuq[:, i, 96 * h:96 * h + 96], cqn[b][:, i, :], start=(i == 0), stop=(i == 1)),
                     r=wuqK + [("cqn", b, 0), ("cqn", b, 1)], w=[PSK(pa_)])
            for i in range(2):
                P.op("pe", I("matmul", ps[pb_][0:32, :], wuq[:, i, 192 + 32 * h:192 + 32 * h + 32], cqn[b][:, i, :], start=(i == 0), stop=(i == 1)),
                     r=wuqK + [("cqn", b, 0), ("cqn", b, 1)], w=[PSK(pb_)])
            P.op("act", I("activation", out=qo[b][32:64, h, :], in_=ps[pa_][32:64, :], func=AF.Copy), r=[PSK(pa_)], w=[("qo", b, h, "n")])
            P.op("act", I("activation", out=qo[b][64:96, h, :], in_=ps[pa_][64:96, :], func=AF.Copy), r=[PSK(pa_)], w=[("qo", b, h, "n2")])
            P.op("dve", I("tensor_tensor", out=rt1[b][:], in0=ps[pa_][0:32, :], in1=csq[b][:, 0, :], op=ALU.mult),
                 r=[PSK(pa_), ("csq", b)], w=[("rt1", b)])
            P.op("dve", I("tensor_tensor", out=rt2[b][:], in0=ps[pb_][0:32, :], in1=csq[b][:, 1, :], op=ALU.mult),
                 r=[PSK(pb_), ("csq", b)], w=[("rt2", b)])
            P.op("pool", I("tensor_tensor", out=qo[b][0:32, h, :], in0=rt1[b][:], in1=rt2[b][:], op=ALU.add),
                 r=[("rt1", b), ("rt2", b)], w=[("qo", b, h, "r")])
            dma("sp", qT.ap()[h, :, sl], qo[b][:, h, :], [("qo", b, h, "n"), ("qo", b, h, "n2"), ("qo", b, h, "r")], [("qT%d" % h, t)])
        if PSTOP == 'q':
            continue
        for h in range(2):
            pi = nps()
            P.op("pe", I("matmul", ps[pi][0:64, :], wkv[:, 64 * h:64 * h + 64], ckn[b][:], start=True, stop=True),
                 r=["wkv", ("ckn", b)], w=[PSK(pi)])
            P.op("act", I("activation", out=ko[b][:, h, :], in_=ps[pi][0:64, :], func=AF.Copy), r=[PSK(pi)], w=[("ko", b, h)])
            dma("sp", kT.ap()[h, 32:96, sl], ko[b][:, h, :], [("ko", b, h)], [("kT%d" % h, t, "n")])
        if PSTOP == 'k':
            continue
        pi = nps()
        for s4 in range(4):
            P.op("pe", I("matmul", ps[pi][:, 128 * s4:128 * s4 + 128], ckn[b][:, 128 * s4:128 * s4 + 128], wkv[:, 128:256], start=True, stop=True),
                 r=["wkv", ("ckn", b)], w=[PSK(pi)])
        P.op("act", I("activation", out=vmo[b][:, :, :], in_=ps[pi][:, :].rearrange("p (a c) -> p a c", a=4), func=AF.Copy),
             r=[PSK(pi)], w=[("vmo", b)])
        dst = bass.AP(tensor=vm, offset=t * 512 * 128, ap=[[128, 128], [128 * 128, 4], [1, 128]])
        dma("sp", dst, vmo[b][:, :, :], [("vmo", b)], [("vm", t)])
        pi = nps()
        pj = nps()
        for s4 in range(4):
            pp = pi if s4 < 2 else pj
            o0 = 192 * (s4 % 2)
            for kc in range(8):
                P.op("pe", I("matmul", ps[pp][:, o0:o0 + 192], hT[b][:, kc, 128 * s4:128 * s4 + 128], win[:, kc, DV:DV + 192],
                                                                         start=(kc == 0), stop=(kc == 7)), r=winK + [("hT", b)], w=[PSK(pp)])
        P.op("act", I("activation", out=vdo[b][:, 0:2, :], in_=ps[pi][:, 0:384].rearrange("p (a c) -> p a c", a=2), func=AF.Copy),
             r=[PSK(pi)], w=[("vdo", b, 0)])
        P.op("act", I("activation", out=vdo[b][:, 2:4, :], in_=ps[pj][:, 0:384].rearrange("p (a c) -> p a c", a=2), func=AF.Copy),
             r=[PSK(pj)], w=[("vdo", b, 1)])
        dst = bass.AP(tensor=dv, offset=t * 512 * 192, ap=[[192, 128], [128 * 192, 4], [1, 192]])
        dma("sp", dst, vdo[b][:, :, :], [("vdo", b, 0), ("vdo", b, 1)], [("dv", t)])
    P.barrier()
    A.off = m0


def mla_seg(nc, P, A, ps, env):
    S, NT, NB = env["S"], env["NT"], env["NB"]
    dma = env["dma"]
    ones_f, tri = env["ones_f"], env["tri"]
    qT, kT, vm, gmz, y_send = env["qT"], env["kT"], env["vm"], env["gmz"], env["y_send"]
    PSK = env["PSK"]
    m0 = A.off
    scale = 96.0 ** -0.5
    KT = A.alloc([96, S], BF16)
    VA = A.alloc([128, NB, 65], BF16)
    QT = [A.alloc([96, 512], BF16) for _ in range(2)]
    GZ = [A.alloc([64, 512], BF16) for _ in range(3)]
    PT = [A.alloc([128, 512], BF16) for _ in range(4)]
    lrow = A.alloc([65, 512], F32)
    rbc = A.alloc([64, 512], F32)
    yf = A.alloc([64, 512], F32)
    yo = [A.alloc([64, 512], BF16) for _ in range(2)]
    P.op("pool", I("memset", VA[:, :, 64:65], 1.0), w=["VAones"])
    blk = 0
    for h in range(2):
        allkT = [("kT%d" % h, t, "r") for t in range(NT)] + [("kT%d" % h, t, "n") for t in range(NT)]
        dma("sp", KT[:], kT.ap()[h], allkT, ["KT"])
        for v0 in range(0, NB, 16):
            src = bass.AP(tensor=vm, offset=64 * h + v0 * 128 * 128, ap=[[128, 128], [128 * 128, 16], [1, 64]])
            dma("sp", VA[:, v0:v0 + 16, 0:64], src, [("vm", t) for t in range(NT)], [("VA", v0)])
        PD = 2
        blocks = [(qt, kb) for qt in range(NT) for kb in range(4 * qt + 4)]
        nblk = len(blocks)

        def load_q(qt):
            qb = qt % 2
            sl = slice(qt * 512, (qt + 1) * 512)
            dma("sp", QT[qb][:], qT.ap()[h, :, sl], [("qT%d" % h, qt)], [("QT", qb)])
            dma("sp", GZ[qt % 3][:], gmz.ap()[64 * h:64 * h + 64, sl], [("gmz", qt)], [("GZ", qt % 3)])

        load_q(0)
        for i in range(nblk + PD):
            if i < nblk:
                qt, kb = blocks[i]
                qb = qt % 2
                if kb == 0 and qt + 1 < NT:
                    load_q(qt + 1)
                d = kb - 4 * qt
                c0 = 128 * d if d > 0 else 0
                sb_ = i % 4
                P.op("pe", I("matmul", ps[sb_][:, c0:512], KT[:, kb * 128:(kb + 1) * 128], QT[qb][:, c0:512], start=True, stop=True),
                     r=["KT", ("QT", qb)], w=[PSK(sb_)])
                P.op("act", I("activation", out=PT[sb_][:, c0:512], in_=ps[sb_][:, c0:512], func=AF.Exp, scale=scale),
                     r=[PSK(sb_)], w=[("PT", sb_)])
                if d >= 0:
                    P.op("dve", I("tensor_tensor", out=PT[sb_][:, c0:c0 + 128], in0=PT[sb_][:, c0:c0 + 128], in1=tri[:], op=ALU.mult),
                         r=[("PT", sb_), "tri"], w=[("PT", sb_)])
            j = i - PD
            if j >= 0:
                qt, kb = blocks[j]
                qb = qt % 2
                po = 4 + qb
                nkb = 4 * qt + 4
                d = kb - 4 * qt
                c0 = 128 * d if d > 0 else 0
                sb_ = j % 4
                sl = slice(qt * 512, (qt + 1) * 512)
                P.op("pe", I("matmul", ps[po][0:65, c0:512], VA[:, kb, :], PT[sb_][:, c0:512], start=(kb == 0), stop=(kb == nkb - 1)),
                     r=[("VA", kb // 16 * 16), "VAones", ("PT", sb_)], w=[PSK(po)])
                if kb == nkb - 1:
                    P.op("act", I("activation", out=lrow[64:65, :], in_=ps[po][64:65, :], func=AF.Copy), r=[PSK(po)], w=["lrow"])
                    P.op("pe", I("matmul", ps[6][0:64, :], ones_f[64:65, 0:64], lrow[64:65, :], start=True, stop=True), r=["lrow", "ones"], w=[PSK(6)])
                    P.op("dve", I("reciprocal", out=rbc[:], in_=ps[6][0:64, :]), r=[PSK(6)], w=["rbc"])
                    P.op("dve", I("tensor_tensor", out=yf[:], in0=ps[po][0:64, :], in1=rbc[:], op=ALU.mult), r=[PSK(po), "rbc"], w=["yf"])
                    P.op("pool", I("tensor_tensor", out=yo[qb][:], in0=yf[:], in1=GZ[qt % 3][:], op=ALU.mult), r=["yf", ("GZ", qt % 3)], w=[("yo", qb)])
                    dma("sp", y_send.ap()[64 * h:64 * h + 64, sl], yo[qb][:], [("yo", qb)], [("y_send", qt, h)])
    P.barrier()
    A.off = m0


def dil_seg(nc, P, A, ps, env):
    S, NT, NCH = env["S"], env["NT"], env["NCH"]
    dma = env["dma"]
    ones_f, T5 = env["ones_f"], env["T5"]
    dqk, dv, gaz, y_send = env["dqk"], env["dv"], env["gaz"], env["y_send"]
    PSK = env["PSK"]
    m0 = A.off
    DQ = [A.alloc([64, 3, 2048], BF16) for _ in range(2)]
    DK = [A.alloc([64, 3, 2048], BF16) for _ in range(2)]
    DVt = [A.alloc([128, 3, 16, 65], BF16) for _ in range(2)]
    ACC = A.alloc([65, 2048], F32)
    EX = [A.alloc([128, 512], F32) for _ in range(2)]
    PTd = [A.alloc([128, 512], BF16) for _ in range(2)]
    GA = A.alloc([64, 2048], BF16)
    lrow = A.alloc([65, 512], F32)
    rbc = A.alloc([64, 512], F32)
    yf = A.alloc([64, 512], F32)
    yo = [A.alloc([64, 512], BF16) for _ in range(2)]
    for pb in range(2):
        P.op("pool", I("memset", DVt[pb][:, :, :, 64:65], 1.0), w=[("DVones", pb)])
    RS = (1, 4, 16)
    cnt = 0
    for c in range(NCH):
        pb = c % 2
        t4 = [4 * c + i for i in range(4)]
        for g in range(3):
            dma("sp", DQ[pb][:, g, :], dqk.ap()[g, 0:64, c * 2048:(c + 1) * 2048], [("dqk", g, t) for t in t4], [("DQ", pb, g)])
            dma("sp", DK[pb][:, g, :], dqk.ap()[g, 64:128, c * 2048:(c + 1) * 2048], [("dqk", g, t) for t in t4], [("DK", pb, g)])
        rdv = [("dv", t) for t in t4]
        src = bass.AP(tensor=dv, offset=c * 2048 * 192, ap=[[192, 128], [128 * 192, 16], [1, 64]])
        dma("sp", DVt[pb][:, 0, :, 0:64], src, rdv, [("DV", pb, 0)])
        for nl in range(4):
            src = bass.AP(tensor=dv, offset=(c * 2048 + 512 * nl) * 192 + 64, ap=[[4 * 192, 128], [192, 4], [1, 64]])
            dma("sp", DVt[pb][:, 1, 4 * nl:4 * nl + 4, 0:64], src, rdv, [("DV", pb, 1, nl)])
        src = bass.AP(tensor=dv, offset=c * 2048 * 192 + 128, ap=[[16 * 192, 128], [192, 16], [1, 64]])
        dma("sp", DVt[pb][:, 2, :, 0:64], src, rdv, [("DV", pb, 2)])
        dma("sp", GA[:], gaz.ap()[:, c * 2048:(c + 1) * 2048], [("gaz", t) for t in t4], ["GA"])
        dvk = lambda p_, g: ([("DV", p_, g)] if g != 1 else [("DV", p_, 1, nl) for nl in range(4)]) + [("DVones", p_)]
        for g, r_ in (enumerate(RS) if DSTOP != 'load' else []):
            for qd in range(4):
                po = 4 + (cnt % 2)
                kinds = []
                a0 = 0
                if c == 0:
                    npv = [a for a in range(4) if 4 * qd + a >= r_]
                    if npv:
                        a0 = npv[0]
                        kinds.append(1)
                else:
                    kinds.append(1)
                kinds.append(0)
                kinfo = []
                for kd in kinds:
                    aa = a0 if kd == 1 else 0
                    pS = cnt % 4
                    eb = cnt % 2
                    cnt += 1
                    rk = {}
                    for a in range(aa, 4):
                        i = 4 * qd + a
                        if kd == 0:
                            ks, kp = i, pb
                        else:
                            ks, kp = (i - r_, pb) if i >= r_ else (i - r_ + 16, 1 - pb)
                        rk[a] = (ks, kp)
                    for a, (ks, kp) in rk.items():
                        i = 4 * qd + a
                        P.op("pe", I("matmul", ps[pS][:, 128 * a:128 * a + 128], DK[kp][:, g, 128 * ks:128 * ks + 128],
                                     DQ[pb][:, g, 128 * i:128 * i + 128], start=True, stop=True),
                             r=[("DK", kp, g), ("DQ", pb, g)], w=[PSK(pS)])
                    P.op("act", I("activation", out=EX[eb][:, 128 * aa:512], in_=ps[pS][:, 128 * aa:512], func=AF.Exp, scale=0.125),
                         r=[PSK(pS)], w=[("EX", eb)])
                    for a in range(aa, 4):
                        P.op("dve", I("tensor_tensor", out=PTd[eb][:, 128 * a:128 * a + 128], in0=EX[eb][:, 128 * a:128 * a + 128],
                                      in1=T5[:, g, kd, :], op=ALU.mult), r=[("EX", eb), ("T5", g, kd)], w=[("PTd", eb)])
                    kinfo.append((kd, eb, rk))
                for a in range(4):
                    seq = [(kd, eb, rk[a]) for (kd, eb, rk) in kinfo if a in rk]
                    for n_, (kd, eb, (ks, kp)) in enumerate(seq):
                        P.op("pe", I("matmul", ps[po][0:65, 128 * a:128 * a + 128], DVt[kp][:, g, ks, :], PTd[eb][:, 128 * a:128 * a + 128],
                                     start=(n_ == 0), stop=(n_ == len(seq) - 1)),
                             r=dvk(kp, g) + [("PTd", eb)], w=[PSK(po)])
                inv = bass.AP(tensor=ps[po], offset=0, ap=[[512, 65], [128, 4], [1, 128]])
                if DSTOP in ('qk', 'pv'):
                    continue
                for (p0, np_) in ((0, 64), (64, 1)):
                    invp = bass.AP(tensor=ps[po], offset=p0 * 512, ap=[[512, np_], [128, 4], [1, 128]])
                    if g == 0:
                        P.op("act", I("activation", out=ACC[p0:p0 + np_, 512 * qd:512 * qd + 512], in_=ps[po][p0:p0 + np_, :], func=AF.Copy),
                             r=[PSK(po)], w=[("ACC", qd, p0)])
                    elif g == 1:
                        av = bass.AP(tensor=ACC, offset=p0 * 2048 + 512 * qd, ap=[[2048, np_], [1, 4], [4, 128]])
                        P.op("dve", I("tensor_tensor", out=av, in0=invp, in1=av, op=ALU.add), r=[PSK(po), ("ACC", qd, p0)], w=[("ACC", qd, p0)])
                    else:
                        av = bass.AP(tensor=ACC, offset=p0 * 2048 + 4 * qd, ap=[[2048, np_], [1, 4], [16, 128]])
                        allacc = [("ACC", q_, p0) for q_ in range(4)]
                        P.op("dve", I("tensor_tensor", out=av, in0=invp, in1=av, op=ALU.add), r=[PSK(po)] + allacc, w=allacc)
        for qd in (range(4) if DSTOP is None else []):
            ob = qd % 2
            cs_ = slice(512 * qd, 512 * qd + 512)
            P.op("act", I("activation", out=lrow[64:65, :], in_=ACC[64:65, cs_], func=AF.Copy), r=[("ACC", q_, 64) for q_ in range(4)], w=["lrow"])
            P.op("pe", I("matmul", ps[6][0:64, :], ones_f[64:65, 0:64], lrow[64:65, :], start=True, stop=True),
                 r=["lrow", "ones"], w=[PSK(6)])
            P.op("dve", I("reciprocal", out=rbc[:], in_=ps[6][0:64, :]), r=[PSK(6)], w=["rbc"])
            P.op("dve", I("tensor_tensor", out=yf[:], in0=ACC[0:64, cs_], in1=rbc[:], op=ALU.mult), r=[("ACC", q_, 0) for q_ in range(4)] + ["rbc"], w=["yf"])
            P.op("dve", I("tensor_tensor", out=yo[ob][:], in0=yf[:], in1=GA[:, cs_], op=ALU.mult), r=["yf", "GA"], w=[("yo", ob)])
            dma("sp", y_send.ap()[128:192, c * 2048 + 512 * qd:c * 2048 + 512 * qd + 512], yo[ob][:], [("yo", ob)], [("y_send", 4 * c + qd, 2)])
    P.barrier()
    A.off = m0


def t5_bucket_np(dist):
    dist = np.asarray(dist, dtype=np.int64)
    d = np.maximum(dist, 1).astype(np.float32)
    large = 16 + (np.log(d / np.float32(16)) / np.float32(math.log(2048 / 16)) * np.float32(16)).astype(np.int32)
    large = np.minimum(large, 31)
    return np.where(dist < 16, dist, large)


def host_inputs(x, c, positions, w_ada, b_ada, norm_g, w_in, q_norm_g, w_uq, kv_norm_g, w_ukv,
                w_out_a, w_out_b, w_o, rel_bias, final_norm_g):
    S = x.shape[1]
    L = w_in.shape[0]
    xT_full = np.ascontiguousarray(x[0].T)
    f32 = np.float32
    inv_freq = (1.0 / (10000.0 ** (np.arange(0, 32, 2, dtype=f32) / f32(32)))).astype(f32)
    invf = np.tile(np.concatenate([inv_freq, inv_freq]), 4).reshape(128, 1).astype(f32)
    oh5 = np.zeros((32, 3, 2, 256), f32)
    m5 = np.zeros((128, 3, 2, 256), f32)
    for g, r in enumerate((1, 4, 16)):
        for i in range(128):
            oh5[t5_bucket_np(i * r), g, 0, i] = 1.0
            m5[:, g, 0, i] = 1.0
        oh5[t5_bucket_np(128 * r), g, 1, 0] = 1.0
        m5[:, g, 1, 0] = 1.0
        for i in range(129, 256):
            oh5[t5_bucket_np((i - 128) * r), g, 1, i] = 1.0
            m5[:, g, 1, i] = 1.0
    kk = np.arange(128)[:, None]
    cc = np.arange(128)[None, :]
    tri = (kk <= cc).astype(f32).astype(ml_dtypes.bfloat16)
    cT = np.ascontiguousarray(c[0].reshape(8, 128).T)

    def kchunk(w):
        return np.ascontiguousarray(w.reshape(8, 128, -1).transpose(1, 0, 2))

    maps = []
    for j in range(NCORE):
        C = slice(128 * j, 128 * j + 128)
        cols = np.concatenate([np.arange(128 * j, 128 * j + 128), 1024 + np.arange(128 * j, 128 * j + 128), 2048 + np.arange(128 * j, 128 * j + 128)])
        wsel = []
        wsel.append(np.arange(5120, 5120 + 416))
        for g in range(3):
            wsel.append((0 * 3 + g) * 512 + j * 64 + np.arange(64))
            wsel.append((1 * 3 + g) * 512 + j * 64 + np.arange(64))
        for g in range(3):
            wsel.append((2 * 3 + g) * 512 + j * 64 + np.arange(64))
        wsel.append(4608 + j * 64 + np.arange(64))
        wsel.append(5536 + j * 128 + np.arange(128))
        wsel.append(6560 + 128 * j + np.arange(128))
        wsel.append(6560 + 1024 + 128 * j + np.arange(128))
        wsel = np.concatenate(wsel)
        assert wsel.size == NW
        uq_cols = np.concatenate([np.concatenate([hh * 96 + 64 + np.arange(32), hh * 96 + np.arange(64)]) for hh in (2 * j, 2 * j + 1)])
        ukv_cols = np.concatenate([(2 * j) * 128 + np.arange(64), (2 * j + 1) * 128 + np.arange(64),
                                   (2 * j) * 128 + 64 + np.arange(64), (2 * j + 1) * 128 + 64 + np.arange(64)])
        m = {
            "xT": np.ascontiguousarray(xT_full[C]),
            "cT": cT,
            "pos": np.ascontiguousarray(positions.astype(np.int32).reshape(1, S)),
            "invf": invf,
            "w_ada": np.stack([kchunk(w_ada[l][:, cols]) for l in range(L)]),
            "b_ada": np.stack([np.ascontiguousarray(b_ada[l][cols].reshape(3, 128).T) for l in range(L)]),
            "norm_g": np.stack([norm_g[l][C].reshape(128, 1) for l in range(L)]),
            "w_in": np.stack([kchunk(w_in[l][:, wsel]) for l in range(L)]),
            "qng": np.stack([np.ascontiguousarray(q_norm_g[l].reshape(2, 128).T) for l in range(L)]),
            "kvng": np.stack([kv_norm_g[l].reshape(128, 1) for l in range(L)]),
            "w_uq": np.stack([np.ascontiguousarray(w_uq[l][:, uq_cols].reshape(2, 128, 192).transpose(1, 0, 2)) for l in range(L)]),
            "w_ukv": np.stack([np.ascontiguousarray(w_ukv[l][:, ukv_cols]) for l in range(L)]),
            "w_out": np.stack([np.ascontiguousarray(np.concatenate([w_out_b[l][:, C].reshape(8, 128, 128), w_out_a[l][:, C].reshape(4, 128, 128)], 0).transpose(1, 0, 2))
                               for l in range(L)]),
            "w_o": np.stack([kchunk(w_o[l][:, C]) for l in range(L)]),
            "fin_g": final_norm_g[C].reshape(128, 1).astype(f32),
            "rel": np.ascontiguousarray(rel_bias[:, [g * 8 + j for g in range(3)]]),
            "oh5": oh5, "m5": m5, "tri": tri,
        }
        maps.append({k: (v if v.dtype != np.float64 else v.astype(f32)) for k, v in m.items()})
    return maps


_CACHE = {}


def _run(S, L, seg, maps, extra):
    key = (S, L, seg)
    if key not in _CACHE:
        _CACHE[key] = build(S, L, False, None, seg)
    nc = _CACHE[key]
    names = set(SEG_IO[seg[0]][0]) | set(SEG_W[seg[0]]) | set(COMMON)
    in_maps = []
    for j in range(NCORE):
        m = {}
        for k in names:
            if k in extra:
                v = extra[k]
                m[k] = v[j] if isinstance(v, list) else v
            else:
                m[k] = maps[j][k]
        in_maps.append(m)
    res = run_bass_kernel_spmd(nc, in_maps, core_ids=list(range(NCORE)))
    return res.results


def kernel(x, c, positions, w_ada, b_ada, norm_g, w_in, q_norm_g, w_uq, kv_norm_g, w_ukv,
           w_out_a, w_out_b, w_o, rel_bias, final_norm_g, _debug=False, _stop=None):
    args = [np.asarray(a) for a in (x, c, positions, w_ada, b_ada, norm_g, w_in, q_norm_g, w_uq, kv_norm_g, w_ukv,
                                    w_out_a, w_out_b, w_o, rel_bias, final_norm_g)]
    S = args[0].shape[1]
    L = args[6].shape[0]
    maps = host_inputs(*args)
    if FUSED or _debug:
        key = (S, L, _debug, _stop)
        if key not in _CACHE:
            _CACHE[key] = build(S, L, _debug, _stop)
        nc = _CACHE[key]
        res = run_bass_kernel_spmd(nc, maps, core_ids=list(range(NCORE)))
        outT = np.concatenate([res.results[j]["outT"] for j in range(NCORE)], axis=0)
        out = np.ascontiguousarray(outT.T).reshape(1, S, D).astype(np.float32)
        if _debug:
            return out, res
        return out
    gather = lambda r, nm: np.concatenate([r[j][nm] for j in range(NCORE)], axis=0)
    xs = [maps[j]["xT"] for j in range(NCORE)]
    r = _run(S, L, ("ss0", 0), maps, {"xT": xs})
    ss_all = gather(r, "ss_send")
    for l in range(L):
        r = _run(S, L, ("norm", l), maps, {"xT": xs, "ss_all": ss_all})
        hT_all = gather(r, "hT_send")
        r = _run(S, L, ("attn", l), maps, {"hT_all": hT_all})
        y_all = gather(r, "y_send")
        gm = [r[j]["gmg"] for j in range(NCORE)]
        r = _run(S, L, ("mrg", l), maps, {"y_all": y_all, "gmg": gm})
        mg_all = gather(r, "mg_send")
        r = _run(S, L, ("xnew", l), maps, {"mg_all": mg_all, "xT": xs})
        xs = [r[j]["xcur"] for j in range(NCORE)]
        ss_all = gather(r, "ss_send")
    r = _run(S, L, ("final", 0), maps, {"xT": xs, "ss_all": ss_all})
    outT = np.concatenate([r[j]["outT"] for j in range(NCORE)], axis=0)
    return np.ascontiguousarray(outT.T).reshape(1, S, D).astype(np.float32)
```
